# Optimizing a Trainium2 kernel written in Bass

```python
import math
import jax
import jax.numpy as jnp
from jax import lax
import numpy as np

D_MODEL = 1024
BATCH = 16
SEQ = 2048
DEPTH = 4

HEAD_DIM = 64
MOBA_HEADS = 4
MOBA_BLOCK = 256
MOBA_TOPK = 3
MOBA_Q_CHUNK = 32
DIFF_HEADS = 4
DIFF_QK_DIM = HEAD_DIM // 2
DIFF_V_DIM = HEAD_DIM
NSA_HEADS = 8
NSA_KV_HEADS = 2
NSA_GROUP = NSA_HEADS // NSA_KV_HEADS
NSA_CMP_BLOCK = 32
NSA_CMP_STRIDE = 16
NSA_CMP_HIDDEN = 256
NSA_SLC_BLOCK = 64
NSA_SLC_TOPK = 16
NSA_LOCAL_BLOCKS = 2
NSA_WINDOW = 512
NSA_Q_CHUNK = 32
ATTN_Q_BLOCK = 128
REL_BUCKETS = 32
REL_MAX_DIST = 128
N_BIAS_HEADS = MOBA_HEADS + DIFF_HEADS + NSA_HEADS
D_FF = 2816
RMS_EPS = 1e-6
NEG_INF = -1e30
FORCE_SCORE = 1e4

MOBA_W = MOBA_HEADS * HEAD_DIM
DIFF_QK_W = DIFF_HEADS * 2 * DIFF_QK_DIM
DIFF_V_W = DIFF_HEADS * DIFF_V_DIM
NSA_Q_W = NSA_HEADS * HEAD_DIM
NSA_KV_W = NSA_KV_HEADS * HEAD_DIM
NSA_GATE_W = 3 * NSA_HEADS
IN_SPLITS = (MOBA_W, MOBA_W, MOBA_W, DIFF_QK_W, DIFF_QK_W, DIFF_V_W, NSA_Q_W, NSA_KV_W, NSA_KV_W, NSA_KV_W, NSA_KV_W, NSA_KV_W, NSA_KV_W, NSA_GATE_W)
IN_WIDTH = 3 * MOBA_W + 2 * DIFF_QK_W + DIFF_V_W + NSA_Q_W + 6 * NSA_KV_W + NSA_GATE_W
MIX_WIDTH = MOBA_W + DIFF_V_W + NSA_Q_W

kernel_name = 'hybrid_moba_diff_nsa_macaron'


def rms_norm(x, g):
    xf = x.astype(jnp.float32)
    y = xf * lax.rsqrt(jnp.mean(xf * xf, axis=-1, keepdims=True) + RMS_EPS)
    return (y * g.astype(jnp.float32)).astype(x.dtype)


def swiglu(x, w_gate, w_up, w_down):
    return (jax.nn.silu(x @ w_gate) * (x @ w_up)) @ w_down


def rel_bucket(dist):
    n = jnp.maximum(dist, 0)
    max_exact = REL_BUCKETS // 2
    n_f = jnp.maximum(n, max_exact).astype(jnp.float32)
    large = max_exact + (jnp.log(n_f / max_exact) / math.log(REL_MAX_DIST / max_exact) * (REL_BUCKETS - max_exact)).astype(jnp.int32)
    return jnp.where(n < max_exact, n, jnp.minimum(large, REL_BUCKETS - 1))


def moba_attention(q, k, v, bias_table):
    B, H, S, dh = q.shape
    n_blk = -(-S // MOBA_BLOCK)
    pad = n_blk * MOBA_BLOCK - S
    kp = jnp.pad(k, ((0, 0), (0, 0), (0, pad), (0, 0)))
    vp = jnp.pad(v, ((0, 0), (0, 0), (0, pad), (0, 0)))
    k_blk = kp.reshape(B, H, n_blk, MOBA_BLOCK, dh)
    v_blk = vp.reshape(B, H, n_blk, MOBA_BLOCK, dh)
    k_mean = jnp.mean(k_blk.astype(jnp.float32), axis=3)
    top = min(MOBA_TOPK, max(n_blk - 1, 1))
    scale = dh ** -0.5
    bi = jnp.arange(B)[:, None, None, None]
    hi = jnp.arange(H)[None, :, None, None]
    blk_ids = jnp.arange(n_blk)
    offs = jnp.arange(MOBA_BLOCK)
    n_sel = top * MOBA_BLOCK

    def chunk(c):
        q0 = c * MOBA_Q_CHUNK
        own = q0 // MOBA_BLOCK
        q_pos = q0 + jnp.arange(MOBA_Q_CHUNK)
        qc = lax.dynamic_slice_in_dim(q, q0, MOBA_Q_CHUNK, axis=2).astype(jnp.float32)
        gate = jnp.einsum('bhqd,bhnd->bhqn', qc, k_mean)
        gate = jnp.where(blk_ids < own, gate, NEG_INF)
        _, idx = lax.top_k(gate, top)
        k_sel = k_blk[bi, hi, idx]
        v_sel = v_blk[bi, hi, idx]
        s_sel = jnp.einsum('bhqd,bhqnkd->bhqnk', qc, k_sel) * scale
        dist_sel = q_pos[None, None, :, None, None] - (idx[..., None] * MOBA_BLOCK + offs)
        s_sel = s_sel + bias_table[rel_bucket(dist_sel), hi[..., None]]
        s_sel = jnp.where((idx < own)[..., None], s_sel, NEG_INF)
        k_own = lax.dynamic_slice_in_dim(kp, own * MOBA_BLOCK, MOBA_BLOCK, axis=2)
        v_own = lax.dynamic_slice_in_dim(vp, own * MOBA_BLOCK, MOBA_BLOCK, axis=2)
        dist_own = q_pos[:, None] - (own * MOBA_BLOCK + offs)[None, :]
        s_own = jnp.einsum('bhqd,bhkd->bhqk', qc, k_own) * scale + jnp.moveaxis(bias_table[rel_bucket(dist_own)], -1, 0)
        s_own = jnp.where(dist_own >= 0, s_own, NEG_INF)
        p = jax.nn.softmax(jnp.concatenate([s_sel.reshape(B, H, MOBA_Q_CHUNK, n_sel), s_own], axis=-1), axis=-1)
        p_sel = p[..., :n_sel].reshape(B, H, MOBA_Q_CHUNK, top, MOBA_BLOCK)
        return jnp.einsum('bhqnk,bhqnkd->bhqd', p_sel, v_sel) + jnp.einsum('bhqk,bhkd->bhqd', p[..., n_sel:], v_own)

    o = lax.map(chunk, jnp.arange(S // MOBA_Q_CHUNK))
    return jnp.moveaxis(o, 0, 2).reshape(B, H, S, dh)


def diff_attention(q, k, v, bias_table, lam_params, subln_g, lambda_init):
    B, S, _ = q.shape
    H = DIFF_HEADS
    qh = q.reshape(B, S, H, 2, DIFF_QK_DIM).transpose(0, 2, 3, 1, 4).astype(jnp.float32)
    kh = k.reshape(B, S, H, 2, DIFF_QK_DIM).transpose(0, 2, 3, 1, 4)
    vh = v.reshape(B, S, H, DIFF_V_DIM).transpose(0, 2, 1, 3)
    lp = lam_params.astype(jnp.float32)
    lam = jnp.exp(jnp.sum(lp[0] * lp[1])) - jnp.exp(jnp.sum(lp[2] * lp[3])) + lambda_init
    scale = DIFF_QK_DIM ** -0.5
    k_pos = jnp.arange(S)

    def chunk(c):
        q0 = c * ATTN_Q_BLOCK
        qc = lax.dynamic_slice_in_dim(qh, q0, ATTN_Q_BLOCK, axis=3)
        dist = (q0 + jnp.arange(ATTN_Q_BLOCK))[:, None] - k_pos[None, :]
        bias = jnp.moveaxis(bias_table[rel_bucket(dist)], -1, 0)
        s = jnp.einsum('bhmqd,bhmkd->bhmqk', qc, kh) * scale + bias[None, :, None]
        p = jax.nn.softmax(jnp.where(dist >= 0, s, NEG_INF), axis=-1)
        a = p[:, :, 0] - lam * p[:, :, 1]
        return jnp.einsum('bhqk,bhkd->bhqd', a, vh)

    o = lax.map(chunk, jnp.arange(S // ATTN_Q_BLOCK))
    o = jnp.moveaxis(o, 0, 2).reshape(B, H, S, DIFF_V_DIM)
    o = rms_norm(o, subln_g) * (1.0 - lambda_init)
    return o.transpose(0, 2, 1, 3).reshape(B, S, H * DIFF_V_DIM)


def nsa_attention(q, kc, vc, ks, vs, kw, vw, gate_logits, bias_table, cmp_pe, cmp_w1, cmp_w2):
    B, S, _ = q.shape
    G, R, dh = NSA_KV_HEADS, NSA_GROUP, HEAD_DIM
    scale = dh ** -0.5
    qg = q.reshape(B, S, G, R, dh).transpose(0, 2, 3, 1, 4).astype(jnp.float32)

    def kv_heads(t):
        return t.reshape(B, S, G, dh).transpose(0, 2, 1, 3)

    kc, vc, ks, vs, kw, vw = (kv_heads(t) for t in (kc, vc, ks, vs, kw, vw))
    tbl = bias_table.reshape(REL_BUCKETS, G, R)
    pos = jnp.arange(S)

    n_cmp = (S - NSA_CMP_BLOCK) // NSA_CMP_STRIDE + 1
    cmp_start = np.arange(n_cmp) * NSA_CMP_STRIDE
    win_idx = cmp_start[:, None] + np.arange(NSA_CMP_BLOCK)[None, :]

    def compress(t, i):
        blocks = t[:, :, win_idx] + cmp_pe[i]
        flat = blocks.reshape(B, G, n_cmp, NSA_CMP_BLOCK * dh)
        return jax.nn.gelu(flat @ cmp_w1[i]) @ cmp_w2[i]

    k_cmp = compress(kc, 0)
    v_cmp = compress(vc, 1)
    cmp_valid = jnp.asarray(cmp_start + NSA_CMP_BLOCK - 1)[None, :] <= pos[:, None]
    s_cmp = jnp.einsum('bgrsd,bgnd->bgrsn', qg, k_cmp) * scale
    p_cmp = jnp.where(cmp_valid, jax.nn.softmax(jnp.where(cmp_valid, s_cmp, NEG_INF), axis=-1), 0.0)
    o_cmp = jnp.einsum('bgrsn,bgnd->bgrsd', p_cmp, v_cmp)

    n_slc = S // NSA_SLC_BLOCK
    slc_start = np.arange(n_slc) * NSA_SLC_BLOCK
    overlap = np.clip(np.minimum(cmp_start[:, None] + NSA_CMP_BLOCK, slc_start[None, :] + NSA_SLC_BLOCK) - np.maximum(cmp_start[:, None], slc_start[None, :]), 0, None) / NSA_CMP_BLOCK
    imp = jnp.einsum('bgrsn,nj->bgsj', p_cmp, jnp.asarray(overlap, jnp.float32))
    blk = jnp.arange(n_slc)[None, :]
    cur = (pos // NSA_SLC_BLOCK)[:, None]
    forced = (blk == 0) | (blk > cur - NSA_LOCAL_BLOCKS)
    score = jnp.where(blk <= cur, jnp.where(forced, FORCE_SCORE, imp), NEG_INF)
    top = min(NSA_SLC_TOPK, n_slc)
    _, sel_idx = lax.top_k(score, top)
    ks_blk = ks.reshape(B, G, n_slc, NSA_SLC_BLOCK, dh)
    vs_blk = vs.reshape(B, G, n_slc, NSA_SLC_BLOCK, dh)
    bi = jnp.arange(B)[:, None, None, None]
    gi = jnp.arange(G)[None, :, None, None]
    gi6 = jnp.arange(G)[None, :, None, None, None, None]
    ri6 = jnp.arange(R)[None, None, :, None, None, None]
    offs = jnp.arange(NSA_SLC_BLOCK)
    n_key = top * NSA_SLC_BLOCK

    def slc_chunk(c):
        q0 = c * NSA_Q_CHUNK
        qc = lax.dynamic_slice_in_dim(qg, q0, NSA_Q_CHUNK, axis=3)
        idx = lax.dynamic_slice_in_dim(sel_idx, q0, NSA_Q_CHUNK, axis=2)
        k_sel = ks_blk[bi, gi, idx]
        v_sel = vs_blk[bi, gi, idx]
        dist = (q0 + jnp.arange(NSA_Q_CHUNK))[None, None, :, None, None] - (idx[..., None] * NSA_SLC_BLOCK + offs)
        s = jnp.einsum('bgrqd,bgqnkd->bgrqnk', qc, k_sel) * scale + tbl[rel_bucket(dist)[:, :, None], gi6, ri6]
        s = jnp.where((dist >= 0)[:, :, None], s, NEG_INF)
        p = jax.nn.softmax(s.reshape(B, G, R, NSA_Q_CHUNK, n_key), axis=-1).reshape(s.shape)
        return jnp.einsum('bgrqnk,bgqnkd->bgrqd', p, v_sel)

    o_slc = lax.map(slc_chunk, jnp.arange(S // NSA_Q_CHUNK))
    o_slc = jnp.moveaxis(o_slc, 0, 3).reshape(B, G, R, S, dh)

    n_k = NSA_WINDOW + ATTN_Q_BLOCK
    kw_p = jnp.pad(kw, ((0, 0), (0, 0), (NSA_WINDOW, 0), (0, 0)))
    vw_p = jnp.pad(vw, ((0, 0), (0, 0), (NSA_WINDOW, 0), (0, 0)))

    def win_chunk(c):
        q0 = c * ATTN_Q_BLOCK
        qc = lax.dynamic_slice_in_dim(qg, q0, ATTN_Q_BLOCK, axis=3)
        kb = lax.dynamic_slice_in_dim(kw_p, q0, n_k, axis=2)
        vb = lax.dynamic_slice_in_dim(vw_p, q0, n_k, axis=2)
        k_pos = q0 - NSA_WINDOW + jnp.arange(n_k)
        dist = (q0 + jnp.arange(ATTN_Q_BLOCK))[:, None] - k_pos[None, :]
        bias = jnp.transpose(tbl[rel_bucket(dist)], (2, 3, 0, 1))
        s = jnp.einsum('bgrqd,bgkd->bgrqk', qc, kb) * scale + bias
        valid = (dist >= 0) & (dist < NSA_WINDOW) & (k_pos >= 0)[None, :]
        p = jax.nn.softmax(jnp.where(valid, s, NEG_INF), axis=-1)
        return jnp.einsum('bgrqk,bgkd->bgrqd', p, vb)

    o_win = lax.map(win_chunk, jnp.arange(S // ATTN_Q_BLOCK))
    o_win = jnp.moveaxis(o_win, 0, 3).reshape(B, G, R, S, dh)

    g = jax.nn.sigmoid(gate_logits.astype(jnp.float32)).reshape(B, S, G, R, 3).transpose(0, 2, 3, 1, 4)
    o = g[..., 0:1] * o_cmp + g[..., 1:2] * o_slc + g[..., 2:3] * o_win
    return o.transpose(0, 3, 1, 2, 4).reshape(B, S, NSA_HEADS * dh)


def hybrid_mixer(h, w_in, w_out, rel_bias, diff_lambda, diff_subln, lambda_init, cmp_pe, cmp_w1, cmp_w2):
    B, S, _ = h.shape
    proj = h @ w_in
    split_points = np.cumsum(IN_SPLITS)[:-1].tolist()
    (mq, mk, mv, dq, dk, dv, nq, nkc, nvc, nks, nvs, nkw, nvw, ng) = jnp.split(proj, split_points, axis=-1)

    def heads(t):
        return t.reshape(B, S, MOBA_HEADS, HEAD_DIM).transpose(0, 2, 1, 3)

    h0, h1 = MOBA_HEADS, MOBA_HEADS + DIFF_HEADS
    o_moba = moba_attention(heads(mq), heads(mk), heads(mv), rel_bias[:, :h0])
    o_moba = o_moba.transpose(0, 2, 1, 3).reshape(B, S, MOBA_W)
    o_diff = diff_attention(dq, dk, dv, rel_bias[:, h0:h1], diff_lambda, diff_subln, lambda_init)
    o_nsa = nsa_attention(nq, nkc, nvc, nks, nvs, nkw, nvw, ng, rel_bias[:, h1:], cmp_pe, cmp_w1, cmp_w2)
    o = jnp.concatenate([o_moba, o_diff, o_nsa], axis=-1).astype(h.dtype)
    return o @ w_out


def setup_inputs(seed: int = 0) -> dict:
    key = jax.random.key(seed)
    ks = jax.random.split(key, 20)

    def nrm(k, shape, scale):
        return jax.random.normal(k, shape, jnp.float32) * scale

    L, HD = NSA_CMP_BLOCK, HEAD_DIM
    return {
        'x': nrm(ks[0], (BATCH, SEQ, D_MODEL), 1.0),
        'rel_bias': nrm(ks[1], (REL_BUCKETS, N_BIAS_HEADS), 0.5),
        'norm_ffn1': 1.0 + nrm(ks[2], (DEPTH, D_MODEL), 0.02),
        'ffn1_gate': nrm(ks[3], (DEPTH, D_MODEL, D_FF), D_MODEL ** -0.5),
        'ffn1_up': nrm(ks[4], (DEPTH, D_MODEL, D_FF), D_MODEL ** -0.5),
        'ffn1_down': nrm(ks[5], (DEPTH, D_FF, D_MODEL), D_FF ** -0.5),
        'norm_mix': 1.0 + nrm(ks[6], (DEPTH, D_MODEL), 0.02),
        'w_in': nrm(ks[7], (DEPTH, D_MODEL, IN_WIDTH), D_MODEL ** -0.5),
        'diff_lambda': nrm(ks[8], (DEPTH, 4, DIFF_QK_DIM), 0.1),
        'diff_subln': 1.0 + nrm(ks[9], (DEPTH, DIFF_V_DIM), 0.02),
        'nsa_cmp_pe': nrm(ks[10], (DEPTH, 2, L, HD), 0.1),
        'nsa_cmp_w1': nrm(ks[11], (DEPTH, 2, L * HD, NSA_CMP_HIDDEN), (L * HD) ** -0.5),
        'nsa_cmp_w2': nrm(ks[12], (DEPTH, 2, NSA_CMP_HIDDEN, HD), NSA_CMP_HIDDEN ** -0.5),
        'w_out': nrm(ks[13], (DEPTH, MIX_WIDTH, D_MODEL), MIX_WIDTH ** -0.5),
        'norm_ffn2': 1.0 + nrm(ks[14], (DEPTH, D_MODEL), 0.02),
        'ffn2_gate': nrm(ks[15], (DEPTH, D_MODEL, D_FF), D_MODEL ** -0.5),
        'ffn2_up': nrm(ks[16], (DEPTH, D_MODEL, D_FF), D_MODEL ** -0.5),
        'ffn2_down': nrm(ks[17], (DEPTH, D_FF, D_MODEL), D_FF ** -0.5),
        'final_norm': 1.0 + nrm(ks[18], (D_MODEL,), 0.02),
    }


def reference(x, rel_bias, norm_ffn1, ffn1_gate, ffn1_up, ffn1_down, norm_mix, w_in, diff_lambda, diff_subln, nsa_cmp_pe, nsa_cmp_w1, nsa_cmp_w2, w_out, norm_ffn2, ffn2_gate, ffn2_up, ffn2_down, final_norm):
    for l in range(DEPTH):
        lambda_init = 0.8 - 0.6 * math.exp(-0.3 * l)
        x = x + 0.5 * swiglu(rms_norm(x, norm_ffn1[l]), ffn1_gate[l], ffn1_up[l], ffn1_down[l])
        x = x + hybrid_mixer(rms_norm(x, norm_mix[l]), w_in[l], w_out[l], rel_bias, diff_lambda[l], diff_subln[l], lambda_init, nsa_cmp_pe[l], nsa_cmp_w1[l], nsa_cmp_w2[l])
        x = x + 0.5 * swiglu(rms_norm(x, norm_ffn2[l]), ffn2_gate[l], ffn2_up[l], ffn2_down[l])
    return rms_norm(x, final_norm)
```

```python
import math
import os
import numpy as np
import concourse.bass as bass
import concourse.mybir as mybir
from concourse.bass_utils import run_bass_kernel_spmd
from contextlib import ExitStack

F32 = mybir.dt.float32
BF16 = mybir.dt.bfloat16
AF = mybir.ActivationFunctionType
ALU = mybir.AluOpType
AX = mybir.AxisListType

S = 2048
D = 1024
KC = 8
FF = 2816
NFC = 22
DEPTH = 4
NCORES = 8
RMS_EPS = 1e-6
NEG = -30000.0


class _Op:
    __slots__ = ("eng", "fn", "reads", "writes", "isdma", "key", "n", "deps",
                 "signal", "val", "waits")


class Sched:
    ENGS = ("pe", "act", "dve", "pool", "sp")

    def __init__(self):
        self.ops = []
        self.per_eng = {e: [] for e in self.ENGS}
        self.last_w = {}
        self.readers = {}
        self.dma_count = {}
        self.pending_barrier = {e: None for e in self.ENGS}
        self.last_op = {e: None for e in self.ENGS}
        self.live_dma = []

    def _mk(self, eng, fn, reads, writes, isdma, key):
        op = _Op()
        op.eng = eng
        op.fn = fn
        op.reads = tuple(reads)
        op.writes = tuple(writes)
        op.isdma = isdma
        op.key = key
        op.deps = []
        op.signal = False
        op.val = 0
        op.waits = []
        deps = set()
        for t in op.reads:
            w = self.last_w.get(t)
            if w is not None:
                deps.add((w, "raw"))
        for t in op.writes:
            w = self.last_w.get(t)
            if w is not None:
                deps.add((w, "waw"))
            for r in self.readers.get(t, ()):
                deps.add((r, "war"))
        pb = self.pending_barrier[eng]
        if pb is not None:
            for o in pb:
                deps.add((o, "raw"))
            self.pending_barrier[eng] = None
        for (p, kind) in deps:
            if p is op:
                continue
            if (not p.isdma) and (not isdma) and p.eng == eng:
                if kind != "raw" or eng == "pe":
                    continue
            op.deps.append(p)
        for t in op.writes:
            self.last_w[t] = op
            self.readers[t] = []
        for t in op.reads:
            self.readers.setdefault(t, []).append(op)
        if isdma:
            c = self.dma_count.get(key, 0) + 1
            self.dma_count[key] = c
            op.val = 16 * c
            op.signal = True
            self.live_dma.append(op)
        self.ops.append(op)
        self.per_eng[eng].append(op)
        if not isdma:
            self.last_op[eng] = op
        return op

    def add(self, eng, fn, reads=(), writes=()):
        return self._mk(eng, fn, reads, writes, False, None)

    def dma(self, queue, fn, reads=(), writes=(), key=None):
        assert key is not None
        return self._mk(queue, fn, reads, writes, True, key)

    def barrier(self):
        lst = [o for o in self.last_op.values() if o is not None]
        lst += self.live_dma
        self.live_dma = []
        for e in self.ENGS:
            prev = self.pending_barrier[e]
            self.pending_barrier[e] = list(lst) + (prev or [])

    def final_wait(self, eng, tokens):
        return self._mk(eng, None, tokens, (), False, None)

    def analyse(self):
        seqno = {}
        cnt = {e: 0 for e in self.ENGS}
        for op in self.ops:
            if not op.isdma:
                cnt[op.eng] += 1
                seqno[id(op)] = cnt[op.eng]
        seen = {e: {} for e in self.ENGS}
        need = []
        for op in self.ops:
            sd = seen[op.eng]
            best = {}
            for p in op.deps:
                if p.isdma:
                    k = ("d", p.key)
                    v = p.val
                else:
                    k = ("e", p.eng)
                    v = seqno[id(p)]
                if sd.get(k, 0) >= v:
                    continue
                if k not in best or best[k][0] < v:
                    best[k] = (v, p)
            lst = []
            for k, (v, p) in best.items():
                sd[k] = v
                p.signal = True
                lst.append(p)
            need.append(lst)
        cnt = {e: 0 for e in self.ENGS}
        for op in self.ops:
            if (not op.isdma) and op.signal:
                cnt[op.eng] += 1
                op.val = cnt[op.eng]
        for op, lst in zip(self.ops, need):
            op.waits = [(("d", p.key) if p.isdma else ("e", p.eng), p.val) for p in lst]
        return cnt

    def emit(self, nc, stack):
        cnt = self.analyse()
        sems = {}
        for e in self.ENGS:
            sems[("e", e)] = stack.enter_context(nc.semaphore("s_" + e))
        for k in self.dma_count:
            sems[("d", k)] = stack.enter_context(nc.semaphore("d_" + str(k)))
        block = stack.enter_context(nc.Block())

        def run(engname):
            def body(eng):
                for op in self.per_eng[engname]:
                    for (k, v) in op.waits:
                        eng.wait_ge(sems[k], v)
                    if op.fn is None:
                        continue
                    inst = op.fn(eng)
                    if op.isdma:
                        inst.then_inc(sems[("d", op.key)], 16)
                    elif op.signal:
                        inst.then_inc(sems[("e", engname)], 1)
            return body

        block.tensor(run("pe"))
        block.scalar(run("act"))
        block.vector(run("dve"))
        block.gpsimd(run("pool"))
        block.sync(run("sp"))
        return cnt


SB_BASE = 16512
SB_END = 229376

SC8 = 0.125
SCD = 32.0 ** -0.5
GELU_C = 1.5957691216057308


def _nbytes(dt):
    return 4 if dt == F32 else 2


def V(t, ftot, p0, npart, f0, dims):
    return bass.AP(t, p0 * ftot + f0, [[ftot, npart]] + [[s, c] for (s, c) in dims])


class Builder:
    def __init__(self, n_layers, n_seq, do_final=True, parts=("ffn1", "mix", "ffn2"),
                 branches=("moba", "diff", "nsa"), layer0=0):
        self.L = n_layers
        self.NS = n_seq
        self.do_final = do_final
        self.parts = parts
        self.branches = branches
        self.layer0 = layer0
        self.nc = bass.Bass("TRN2", target_bir_lowering=False)
        self.sc = Sched()
        self.stack = ExitStack()
        self.off = SB_BASE
        self.nalloc = 0
        self.rr = {}

    def din(self, name, shape, dt=F32):
        return self.nc.dram_tensor(name, list(shape), dt, kind="ExternalInput").ap()

    def dout(self, name, shape, dt=F32):
        return self.nc.dram_tensor(name, list(shape), dt, kind="ExternalOutput").ap()

    def sb(self, name, shape, dt):
        n = 1
        for s_ in shape[1:]:
            n *= s_
        nb = (n * _nbytes(dt) + 63) // 64 * 64
        off = self.off
        self.off += nb
        assert self.off <= SB_END, ("SBUF overflow", name, self.off)
        self.nalloc += 1
        return self.nc.alloc_sbuf_tensor_at("%s_%d" % (name, self.nalloc), list(shape), dt, offset=off)

    def ps(self, name, shape, dt=F32):
        return self.stack.enter_context(self.nc.psum_tensor(name, list(shape), dt))

    def rot(self, name, n):
        i = self.rr.get(name, 0)
        self.rr[name] = i + 1
        return i % n

    def mm(self, out, lhsT, rhs, start, stop, reads, writes):
        self.sc.add("pe", lambda e: e.matmul(out, lhsT, rhs, start=start, stop=stop,
                                             skip_group_check=True), reads, writes)

    def actf(self, out, in_, func, reads, writes, scale=1.0, bias=0.0):
        self.sc.add("act", lambda e: e.activation(out=out, in_=in_, func=func, bias=bias, scale=scale),
                    reads, writes)

    def tt(self, out, in0, in1, op, reads, writes, eng="dve"):
        self.sc.add(eng, lambda e: e.tensor_tensor(out=out, in0=in0, in1=in1, op=op), reads, writes)

    def ts(self, out, in0, s1, s2, op0, op1, reads, writes, eng="dve"):
        if op1 is None:
            self.sc.add(eng, lambda e: e.tensor_scalar(out=out, in0=in0, scalar1=s1, scalar2=None, op0=op0),
                        reads, writes)
        else:
            self.sc.add(eng, lambda e: e.tensor_scalar(out=out, in0=in0, scalar1=s1, scalar2=s2,
                                                        op0=op0, op1=op1), reads, writes)

    def stt(self, out, in0, scalar, in1, op0, op1, reads, writes, eng="dve"):
        self.sc.add(eng, lambda e: e.scalar_tensor_tensor(out=out, in0=in0, scalar=scalar, in1=in1,
                                                           op0=op0, op1=op1), reads, writes)

    def cp(self, out, in_, reads, writes, eng="dve"):
        if eng == "act":
            self.actf(out, in_, AF.Copy, reads, writes)
        else:
            self.sc.add(eng, lambda e: e.tensor_copy(out=out, in_=in_), reads, writes)

    def recip(self, out, in_, reads, writes):
        self.sc.add("dve", lambda e: e.reciprocal(out=out, in_=in_), reads, writes)

    def red(self, out, in_, op, reads, writes):
        self.sc.add("dve", lambda e: e.tensor_reduce(out=out, in_=in_, axis=AX.X, op=op), reads, writes)

    def memset(self, ap, val, writes, eng="dve"):
        self.sc.add(eng, lambda e: e.memset(ap, val), (), writes)

    def dmaq(self, queue, out, in_, reads, writes, key):
        self.sc.dma(queue, lambda e: e.dma_start(out=out, in_=in_), reads, writes, key)

    def declare(self):
        L, NS = self.L, self.NS
        d = self.din
        self.xT_d = d("xT", [NS, D, S])
        self.out_d = self.dout("outT", [NS, D, S])
        self.gains_d = d("gains", [128, (3 * L + 1) * KC])
        self.gu_d = [d("gu%d" % i, [L, NFC, 128, 2 * KC * 128]) for i in (1, 2)]
        self.wd_d = [d("wd%d" % i, [L, KC, 128, NFC * 128]) for i in (1, 2)]
        self.winF_d = d("winF", [L, 20, 128, KC * 128])
        self.winTm_d = d("winTm", [L, 128, KC * 256])
        self.winTd_d = d("winTd", [L, 128, KC * 256])
        self.winTn_d = d("winTn", [L, 128, KC * 320])
        self.wout_d = d("woutp", [L, 4, 128, 2 * 1024])
        self.w1_d = d("w1p", [L, 2, 8, 64, 4 * 256])
        self.w2k_d = d("w2k", [L, 128, 2 * 128])
        self.w2v_d = d("w2v", [L, 128, 2 * 64])
        self.peT_d = d("peT", [L, 64, 2 * 32])
        self.dl_d = d("dlrep", [L, 128, 128])
        self.sub_d = d("subrep", [L, 128, 64])
        self.tbl_d = d("tblrep", [128, 512])
        self.cE_d = d("c_E", [128, 2 * 32 * 128])
        self.cid_d = d("c_ident", [128, 128])
        self.ccaus_d = d("c_caus", [128, 128])
        self.cwedge_d = d("c_wedge", [128, 128])
        self.cmcmp_d = d("c_mcmp", [128, 2048])
        self.cbim_d = d("c_bim", [8, 2048])
        self.cbin_d = d("c_bin", [32, 2048])
        self.cmA_d = d("c_mA", [128, 128])
        self.cmB_d = d("c_mB", [128, 128])
        self.cmN_d = d("c_mN", [128, 128])
        self.cnA_d = d("c_nA", [128, 512])
        self.cnB_d = d("c_nB", [128, 512])
        self.cov_d = d("c_ov", [128, 32])

        sb = self.sb
        self.xT = sb("xT_s", [128, KC, S], F32)
        self.hT = sb("hT_s", [128, KC, S], BF16)
        self.Tb = sb("Tb", [128, 16 * 2 * 128], BF16)
        self.ident = sb("ident", [128, 128], BF16)
        self.ones_bf = sb("ones_bf", [128, 128], BF16)
        self.gains = sb("gains_s", [128, (3 * L + 1) * KC], F32)
        self.lam = sb("lam", [128, 8], F32)
        self.arena0 = self.off

        self.off = self.arena0
        self.E_s = sb("E_s", [128, 2 * 32 * 128], BF16)
        self.tbl_s = sb("tbl_s", [128, 512], F32)
        self.Tacc = sb("Tacc", [128, 2 * 16 * 128], F32)
        self.Ttmp = sb("Ttmp", [128, 16 * 128], F32)
        self.caus_s = sb("caus_s", [128, 128], F32)

        self.off = self.arena0
        self.act_s = sb("act_s", [128, NFC, 1024], BF16)
        self.wgu = [sb("wgu_s%d" % i, [128, 2 * KC * 128], BF16) for i in range(3)]
        self.wd = [sb("wd_s%d" % i, [128, NFC * 128], BF16) for i in range(2)]
        self.sg = [sb("sg%d" % i, [128, 512], F32) for i in range(2)]
        self.sq = [sb("sq%d" % i, [128, 512], BF16) for i in range(2)]
        self.rstd = [sb("rstd%d" % i, [128, 512], F32) for i in range(2)]
        self.ffn_end = self.off

        self.off = self.arena0
        self.PT = [sb("PT%d" % i, [128, 512], BF16) for i in range(3)]
        self.small = sb("small", [128, 256], F32)
        self.maskT = [sb("maskT%d" % i, [32, 512], BF16) for i in range(2)]
        self.mb = sb("mb", [128, 128], BF16)
        self.cmpbuf = sb("cmpbuf", [128, 1024], BF16)
        u0 = self.off
        self.winbuf = [sb("winbuf%d" % i, [128, KC * 128], BF16) for i in range(2)]
        self.wT = sb("wT", [128, KC * 512], BF16)
        e1 = self.off
        self.off = u0
        self.w1buf = [sb("w1buf%d" % i, [64, 4 * 256], BF16) for i in range(2)]
        self.HT = sb("HT", [128, 2 * 2 * 128], BF16)
        self.w2k_s = sb("w2k_s", [128, 2 * 128], BF16)
        self.w2v_s = sb("w2v_s", [128, 2 * 64], BF16)
        self.peT_s = sb("peT_s", [64, 64], BF16)
        self.b1_s = sb("b1_s", [128, 2], F32)
        self.gx = [sb("gx%d" % i, [128, 128], F32) for i in range(3)]
        assert self.off <= e1
        self.off = e1
        k0 = self.off
        self.kcT = [sb("kcT%d" % i, [64, S], BF16) for i in range(4)]
        e2 = self.off
        self.off = k0
        self.sq_m = [sb("sqm%d" % i, [128, 512], BF16) for i in range(2)]
        self.rstd_m = [sb("rstdm%d" % i, [128, 512], F32) for i in range(2)]
        assert self.off <= e2
        self.off = u0
        self.otok = sb("otok", [128, 16 * 256], BF16)
        self.oT = sb("oT", [128, 2 * S], BF16)
        self.wout_s = sb("wout_s", [128, 2 * 1024], BF16)
        self.tmp = [sb("tmp%d" % i, [128, 256], F32) for i in range(4)]
        self.ocomb = sb("ocomb", [128, 4 * 256], F32)
        e3 = self.off
        self.off = max(e2, e3)
        self.mix0 = self.off

        self.off = self.mix0
        self.pjM = [sb("pjM%d" % i, [128, S], BF16) for i in range(4)]
        self.VM = sb("VM", [128, 16 * 4 * 65], BF16)
        self.bim = sb("bim", [8, 2048], BF16)
        self.mA = sb("mA", [128, 128], F32)
        self.mB = sb("mB", [128, 128], F32)
        self.mN = sb("mN", [128, 128], F32)
        self.kmT = sb("kmT", [128, 4 * 8], BF16)
        self.kmF = sb("kmF", [128, 4 * 8], F32)
        self.moba_end = self.off

        self.off = self.mix0
        self.pjD = [sb("pjD%d" % i, [128, S], BF16) for i in range(6)]
        self.VD = sb("VD", [128, 16 * 4 * 65], BF16)
        self.dl_s = sb("dl_s", [128, 128], F32)
        self.sub_s = sb("sub_s", [128, 64], F32)
        self.diff_end = self.off

        self.off = self.mix0
        self.pjN = [sb("pjN%d" % i, [128, S], BF16) for i in range(8)]
        self.VS = sb("VS", [128, 16 * 2 * 65], BF16)
        self.VW = sb("VW", [128, 16 * 2 * 65], BF16)
        self.gate_s = sb("gate_s", [128, 16 * 24], F32)
        self.mcmp = sb("mcmp", [128, 2048], BF16)
        self.bin_ = sb("bin", [32, 2048], BF16)
        self.nA = sb("nA", [128, 512], F32)
        self.nB = sb("nB", [128, 512], F32)
        self.kcmpT = sb("kcmpT", [128, 2 * 128], BF16)
        self.vcaug = sb("vcaug", [128, 2 * 97], BF16)
        self.wedge = sb("wedge", [128, 128], BF16)
        self.impacc = sb("impacc", [128, 128], F32)
        self.nsa_end = self.off
        self.off = max(self.ffn_end, self.moba_end, self.diff_end, self.nsa_end)
        print("SBUF end", self.off, "of", SB_END, "ffn", self.ffn_end, "moba", self.moba_end,
              "diff", self.diff_end, "nsa", self.nsa_end)

        self.bank = [self.ps("bank%d" % i, [128, 512], F32) for i in range(8)]

    def bankv(self, b, p0, npart, f0, dims):
        return V(self.bank[b], 512, p0, npart, f0, dims)

    def setup(self, consts):
        sc = self.sc
        self.dmaq("sp", self.gains[:], self.gains_d[:, :], [], ["gains"], "gains")
        self.memset(self.ones_bf[:], 1.0, ["ones"])
        self.dmaq("pool", self.ident[:], self.cid_d[:, :], [], ["ident"], "ident")
        self.dmaq("pool", self.E_s[:], self.cE_d[:, :], [], ["E_s"], "E_s")
        self.dmaq("sp", self.tbl_s[:], self.tbl_d[:, :], [], ["tbl_s"], "tbl_s")
        self.dmaq("sp", self.caus_s[:], self.ccaus_d[:, :], [], ["caus_s"], "caus_s")
        tb3 = V(self.tbl_s, 512, 0, 128, 0, [(16, 32), (1, 16)])
        t31 = V(self.tbl_s, 512, 0, 128, 31 * 16, [(0, 32), (1, 16)])
        self.tt(tb3, tb3, t31, ALU.subtract, ["tbl_s"], ["tbl_s"])
        first = {0: True, 1: True}
        for di in range(2):
            for b in range(31):
                if not consts["E_nonzero"][di][b]:
                    continue
                Eb = V(self.E_s, 8192, 0, 128, (di * 32 + b) * 128, [(0, 16), (1, 128)])
                tv = V(self.tbl_s, 512, 0, 128, b * 16, [(1, 16), (0, 128)])
                acc = V(self.Tacc, 4096, 0, 128, di * 2048, [(128, 16), (1, 128)])
                if first[di]:
                    self.tt(acc, Eb, tv, ALU.mult, ["E_s", "tbl_s"], [("Tacc", di)])
                    first[di] = False
                else:
                    tmp = V(self.Ttmp, 2048, 0, 128, 0, [(128, 16), (1, 128)])
                    self.tt(tmp, Eb, tv, ALU.mult, ["E_s", "tbl_s"], ["Ttmp"])
                    self.tt(acc, acc, tmp, ALU.add, [("Tacc", di), "Ttmp"], [("Tacc", di)])
        for h in range(16):
            inv = (1.0 / SCD) if 4 <= h < 8 else (1.0 / SC8)
            for di in range(2):
                src = V(self.Tacc, 4096, 0, 128, di * 2048 + h * 128, [(1, 128)])
                dst = V(self.Tb, 4096, 0, 128, (h * 2 + di) * 128, [(1, 128)])
                if di == 0:
                    self.stt(dst, src, inv, self.caus_s[:], ALU.mult, ALU.add,
                             [("Tacc", di), "caus_s"], ["Tb"])
                else:
                    self.ts(dst, src, inv, None, ALU.mult, None, [("Tacc", di)], ["Tb"])
        sc.barrier()

    def Tbv(self, h, di):
        return V(self.Tb, 4096, 0, 128, (h * 2 + di) * 128, [(1, 128)])

    def load_x(self, s):
        for c in range(KC):
            self.dmaq("sp", self.xT[:, c, :], self.xT_d[s, c * 128:(c + 1) * 128, :],
                      [], [("xT", c, tt) for tt in range(4)], "xload%d" % c)

    def store_out(self, s):
        toks = []
        for c in range(KC):
            self.dmaq("sp", self.out_d[s, c * 128:(c + 1) * 128, :], self.xT[:, c, :],
                      [("xT", c, tt) for tt in range(4)], [("outd", s, c)], "xstore%d" % c)
            toks.append(("outd", s, c))
        return toks

    def norm(self, gidx, final=False, mixer=False):
        sqs = self.sq_m if mixer else self.sq
        rstds = self.rstd_m if mixer else self.rstd
        for tt in range(4):
            tsl = slice(tt * 512, (tt + 1) * 512)
            bi = 6 + self.rot("nbank", 2)
            bank = self.bank[bi]
            btok = ("bank", bi)
            for c in range(KC):
                qi = self.rot("sq", 2)
                sq = sqs[qi]
                self.actf(sq[:], self.xT[:, c, tsl], AF.Square, [("xT", c, tt)], [("sq", qi)])
                self.mm(bank[:], self.ones_bf[:], sq[:], c == 0, c == KC - 1,
                        [("sq", qi), "ones"], [btok])
            ri = self.rot("rstd", 2)
            rstd = rstds[ri]
            self.actf(rstd[:], bank[:], AF.Sqrt, [btok], [("rstd", ri)], scale=1.0 / D, bias=RMS_EPS)
            self.recip(rstd[:], rstd[:], [("rstd", ri)], [("rstd", ri)])
            for c in range(KC):
                gcol = self.gains[:, gidx * KC + c: gidx * KC + c + 1]
                if not final:
                    self.stt(self.hT[:, c, tsl], self.xT[:, c, tsl], gcol, rstd[:], ALU.mult, ALU.mult,
                             [("xT", c, tt), ("rstd", ri), "gains"], [("hT", c, tt)])
                else:
                    self.stt(self.xT[:, c, tsl], self.xT[:, c, tsl], gcol, rstd[:], ALU.mult, ALU.mult,
                             [("xT", c, tt), ("rstd", ri), "gains"], [("xT", c, tt)])

    def ffn(self, l, which):
        gu_d = self.gu_d[which]
        wd_d = self.wd_d[which]
        for half in range(2):
            for fc in range(NFC):
                wi = self.rot("wgu", 3)
                w = self.wgu[wi]
                self.dmaq("pool", w[:], gu_d[l, fc, :, :], [], [("wgu", wi)], "wgu%d" % wi)
                for sub in range(2):
                    tt = half * 2 + sub
                    tsl = slice(tt * 512, (tt + 1) * 512)
                    gb = self.rot("abank", 3) * 2
                    gbank, ubank = self.bank[gb], self.bank[gb + 1]
                    for c in range(KC):
                        self.mm(gbank[:], w[:, c * 128:(c + 1) * 128], self.hT[:, c, tsl],
                                c == 0, c == KC - 1, [("wgu", wi), ("hT", c, tt)], [("bank", gb)])
                    for c in range(KC):
                        self.mm(ubank[:], w[:, (KC + c) * 128:(KC + c + 1) * 128], self.hT[:, c, tsl],
                                c == 0, c == KC - 1, [("wgu", wi), ("hT", c, tt)], [("bank", gb + 1)])
                    si = self.rot("sg", 2)
                    sg = self.sg[si]
                    self.actf(sg[:], gbank[:], AF.Silu, [("bank", gb)], [("sg", si)])
                    self.tt(self.act_s[:, fc, sub * 512:(sub + 1) * 512], ubank[:], sg[:], ALU.mult,
                            [("bank", gb + 1), ("sg", si)], [("act", fc, sub)])
            for dc in range(KC):
                wi = self.rot("wd", 2)
                w = self.wd[wi]
                self.dmaq("pool", w[:], wd_d[l, dc, :, :], [], [("wd", wi)], "wd%d" % wi)
                for sub in range(2):
                    tt = half * 2 + sub
                    tsl = slice(tt * 512, (tt + 1) * 512)
                    yb = 6 + self.rot("nbank", 2)
                    ybank = self.bank[yb]
                    for fc in range(NFC):
                        self.mm(ybank[:], w[:, fc * 128:(fc + 1) * 128],
                                self.act_s[:, fc, sub * 512:(sub + 1) * 512],
                                fc == 0, fc == NFC - 1, [("wd", wi), ("act", fc, sub)], [("bank", yb)])
                    self.stt(self.xT[:, dc, tsl], ybank[:], 0.5, self.xT[:, dc, tsl], ALU.mult, ALU.add,
                             [("bank", yb), ("xT", dc, tt)], [("xT", dc, tt)])

    def proj_F(self, l, tile, subs):
        wi = self.rot("winbuf", 2)
        w = self.winbuf[wi]
        self.dmaq("pool", w[:], self.winF_d[l, tile, :, :], [], [("winbuf", wi)], "winbuf%d" % wi)
        for tt in range(4):
            tsl = slice(tt * 512, (tt + 1) * 512)
            for (c0, c1, dst, dtok) in subs:
                M = c1 - c0
                bi = self.rot("pbank", 6)
                bank = self.bank[bi]
                for c in range(KC):
                    self.mm(bank[0:M, :], w[:, c * 128 + c0: c * 128 + c1], self.hT[:, c, tsl],
                            c == 0, c == KC - 1, [("winbuf", wi), ("hT", c, tt)], [("bank", bi)])
                eng = "act" if self.rot("pev", 2) == 0 else "dve"
                self.cp(dst[0:M, tsl], bank[0:M, :], [("bank", bi)], [dtok + (tt,)], eng=eng)

    def proj_T(self, l, wd_ap, ncols, evac):
        wT = self.wT
        self.dmaq("pool", wT[:, 0:KC * ncols], wd_ap, [], ["wT"], "wT")
        for kt in range(16):
            bi = self.rot("pbank", 6)
            bank = self.bank[bi]
            for c in range(KC):
                self.mm(bank[:, 0:ncols], self.hT[:, c, kt * 128:(kt + 1) * 128],
                        wT[:, c * ncols:(c + 1) * ncols], c == 0, c == KC - 1,
                        ["wT", ("hT", c, kt // 4)], [("bank", bi)])
            evac(kt, bi)

    def attn(self, Q, kt_list, kap, qap, vap, nv, scale, extras, ob, otoks,
             kparts=128, extra_reads=()):
        first = {}
        last = {}
        for (kt, j0, j1) in kt_list:
            for j in range(j0, j1):
                if j not in first:
                    first[j] = kt
                last[j] = kt
        obank = self.bank[ob]
        ofirst = True
        for (kt, j0, j1) in kt_list:
            si = self.rot("sbank", 3)
            sbank = self.bank[si]
            c0, c1 = j0 * 128, j1 * 128
            ex = extras(kt, j0, j1)
            self.mm(sbank[0:kparts, c0:c1], kap(kt), qap(Q * 512 + c0, Q * 512 + c1),
                    True, len(ex) == 0, list(extra_reads), [("bank", si)])
            for i, (e0, e1, l_ap, r_ap, rd) in enumerate(ex):
                self.mm(sbank[0:kparts, e0:e1], l_ap, r_ap, False, i == len(ex) - 1,
                        list(rd), [("bank", si)])
            pi = self.rot("PT", 3)
            pt = self.PT[pi]
            self.actf(pt[0:kparts, c0:c1], sbank[0:kparts, c0:c1], AF.Exp, [("bank", si)], [("pt", pi)],
                      scale=scale)
            for j in range(j0, j1):
                self.mm(obank[:, j * 128: j * 128 + nv], pt[0:kparts, j * 128:(j + 1) * 128], vap(kt),
                        ofirst, last[j] == kt, [("pt", pi)] + list(otoks), [("bank", ob)])
                ofirst = False

    def outproj(self, l, grp):
        self.dmaq("pool", self.wout_s[:], self.wout_d[l, grp, :, :], [], ["wout"], "wout")
        for f in range(2):
            for Qi in range(4):
                bi = 6 + self.rot("nbank", 2)
                bank = self.bank[bi]
                for j in range(4):
                    J = Qi * 4 + j
                    src = V(self.otok, 4096, 0, 128, J * 256 + f * 128, [(1, 128)])
                    self.mm(bank[:, j * 128:(j + 1) * 128], src, self.ident[:], True, True,
                            [("otok", J), "ident"], [("bank", bi)])
                eng = "act" if self.rot("pev", 2) == 0 else "dve"
                dst = V(self.oT, 4096, 0, 128, f * 2048 + Qi * 512, [(1, 512)])
                self.cp(dst, bank[:], [("bank", bi)], [("oT", f, Qi)], eng=eng)
        for dc in range(KC):
            for tt in range(4):
                bi = 6 + self.rot("nbank", 2)
                bank = self.bank[bi]
                for f in range(2):
                    self.mm(bank[:], self.wout_s[:, f * 1024 + dc * 128: f * 1024 + (dc + 1) * 128],
                            V(self.oT, 4096, 0, 128, f * 2048 + tt * 512, [(1, 512)]),
                            f == 0, f == 1, ["wout", ("oT", f, tt)], [("bank", bi)])
                tsl = slice(tt * 512, (tt + 1) * 512)
                self.tt(self.xT[:, dc, tsl], bank[:], self.xT[:, dc, tsl], ALU.add,
                        [("bank", bi), ("xT", dc, tt)], [("xT", dc, tt)])

    def rank_mask(self, score_ap_fn, n, nj, K, mult_tab, mb_out_fn, stok, mtok):
        for j in range(nj):
            a = score_ap_fn(j, [(0, n), (1, n)])
            b = score_ap_fn(j, [(1, n), (0, n)])
            cmpv = V(self.cmpbuf, 1024, 0, 128, 0, [(n, n), (1, n)])
            self.tt(cmpv, a, b, ALU.is_gt, [stok], ["cmpbuf"])
            rk = V(self.small, 256, 0, 128, 192, [(1, n)])
            self.red(rk, cmpv, ALU.add, ["cmpbuf"], ["rank"])
            if callable(mult_tab):
                self.stt(mb_out_fn(j), rk, K - 0.5, mult_tab(j), ALU.is_ge, ALU.mult,
                         ["rank", "tabs"], [mtok])
            else:
                self.ts(mb_out_fn(j), rk, K - 0.5, mult_tab, ALU.is_ge, ALU.mult, ["rank"], [mtok])

    def pass_moba(self, l):
        sc = self.sc
        pj = self.pjM
        for t in range(4):
            self.proj_F(l, t, [(0, 128, pj[t], ("pjM", t))])
        self.memset(self.VM[:], 1.0, ["VM"])

        def evac(kt, bi):
            dst = V(self.VM, 4160, 0, 128, kt * 260, [(65, 4), (1, 64)])
            src = V(self.bank[bi], 512, 0, 128, 0, [(64, 4), (1, 64)])
            self.cp(dst, src, [("bank", bi)], ["VM"], eng="act" if kt % 2 else "dve")
        self.proj_T(l, self.winTm_d[l, :, :], 256, evac)
        self.dmaq("pool", self.bim[:], self.cbim_d[:, :], [], ["bim"], "bim")
        self.dmaq("sp", self.mA[:], self.cmA_d[:, :], [], ["tabs"], "mA")
        self.dmaq("sp", self.mB[:], self.cmB_d[:, :], [], ["tabs"], "mB")
        self.dmaq("sp", self.mN[:], self.cmN_d[:, :], [], ["tabs"], "mN")
        sc.barrier()
        for h in range(4):
            r0 = 64 * (h % 2)
            kt_t = pj[2 + h // 2]
            src = V(kt_t, 2048, r0, 64, 0, [(256, 8), (1, 256)])
            dstf = V(self.kmF, 32, r0, 64, h * 8, [(1, 8)])
            self.red(dstf, src, ALU.add, [("pjM", 2 + h // 2, tt) for tt in range(4)], [("kmF", h)])
            dst = V(self.kmT, 32, r0, 64, h * 8, [(1, 8)])
            self.ts(dst, dstf, 1.0 / 256, None, ALU.mult, None, [("kmF", h)], [("kmT", h)])
        for h in range(4):
            r0 = 64 * (h % 2)
            qtile = pj[h // 2]
            ktile = pj[2 + h // 2]
            qtoks = [("pjM", h // 2, tt) for tt in range(4)]
            ktoks = [("pjM", 2 + h // 2, tt) for tt in range(4)]
            for Q in range(4):
                use_mask = Q >= 2
                mi = None
                if use_mask:
                    gb = 7
                    for j in range(4):
                        J = Q * 4 + j
                        self.mm(self.bank[gb][:, j * 8:(j + 1) * 8],
                                V(qtile, 2048, r0, 64, J * 128, [(1, 128)]),
                                V(self.kmT, 32, r0, 64, h * 8, [(1, 8)]), True, True,
                                qtoks + [("kmT", h)], [("bank", gb)])
                    scv = V(self.small, 256, 0, 128, 0, [(1, 32)])
                    Av = V(self.mA, 128, 0, 128, Q * 32, [(1, 32)])
                    Bv = V(self.mB, 128, 0, 128, Q * 32, [(1, 32)])
                    self.tt(scv, self.bank[gb][:, 0:32], Av, ALU.mult, [("bank", gb), "tabs"], ["score"])
                    self.tt(scv, scv, Bv, ALU.add, ["score", "tabs"], ["score"])
                    self.rank_mask(lambda j, dims: V(self.small, 256, 0, 128, j * 8, dims), 8, 4, 3,
                                   lambda j: V(self.mN, 128, 0, 128, Q * 32 + j * 8, [(1, 8)]),
                                   lambda j: V(self.mb, 128, 0, 128, j * 8, [(1, 8)]), "score", "mb")
                    mi = self.rot("maskT", 2)
                    tb_ = 7
                    for j in range(4):
                        self.mm(self.bank[tb_][0:8, j * 128:(j + 1) * 128],
                                V(self.mb, 128, 0, 128, j * 8, [(1, 8)]), self.ident[:], True, True,
                                ["mb", "ident"], [("bank", tb_)])
                    self.cp(self.maskT[mi][0:8, :], self.bank[tb_][0:8, :], [("bank", tb_)], [("maskT", mi)])
                ob = 3 + self.rot("obank", 3)
                kt_list = []
                for kt in range(4 * Q + 4):
                    j0 = max(0, kt - 4 * Q)
                    kt_list.append((kt, j0, 4))

                def extras(kt, j0, j1, Q=Q, h=h, mi=mi, use_mask=use_mask):
                    ex = []
                    for j in range(j0, j1):
                        dlt = 4 * Q + j - kt
                        if dlt in (0, 1):
                            ex.append((j * 128, (j + 1) * 128, self.ident[:], self.Tbv(h, dlt), ["ident", "Tb"]))
                    if use_mask and (kt // 2) < (4 * Q + 3) // 2:
                        ex.append((j0 * 128, j1 * 128, self.bim[0:8, kt * 128:(kt + 1) * 128],
                                   self.maskT[mi][0:8, j0 * 128:j1 * 128], ["bim", ("maskT", mi)]))
                    return ex
                self.attn(Q, kt_list,
                          lambda kt: V(ktile, 2048, r0, 64, kt * 128, [(1, 128)]),
                          lambda c0, c1: V(qtile, 2048, r0, 64, c0, [(1, c1 - c0)]),
                          lambda kt: V(self.VM, 4160, 0, 128, kt * 260 + h * 65, [(1, 65)]),
                          65, SC8, extras, ob, ["VM"], extra_reads=qtoks + ktoks)
                rbase = 64 + self.rot("rs", 4) * 4
                rs = V(self.small, 256, 0, 128, rbase, [(1, 4)])
                self.recip(rs, V(self.bank[ob], 512, 0, 128, 64, [(128, 4)]), [("bank", ob)], ["rs"])
                rsb = V(self.small, 256, 0, 128, rbase, [(1, 4), (0, 64)])
                dst = V(self.otok, 4096, 0, 128, Q * 4 * 256 + h * 64, [(256, 4), (1, 64)])
                Ov = V(self.bank[ob], 512, 0, 128, 0, [(128, 4), (1, 64)])
                self.tt(dst, Ov, rsb, ALU.mult, [("bank", ob), "rs"],
                        [("otok", Q * 4 + j) for j in range(4)])
        self.outproj(l, 0)

    def mixer(self, l):
        sc = self.sc
        sc.barrier()
        self.norm(3 * l + 1, mixer=True)
        if "moba" in self.branches:
            sc.barrier()
            self.pass_moba(l)
        if "diff" in self.branches:
            sc.barrier()
            self.pass_diff(l)
        if "nsa" in self.branches:
            sc.barrier()
            self.pass_nsa(l)
        sc.barrier()

    def build(self, consts):
        self.declare()
        self.setup(consts)
        outtoks = []
        for s in range(self.NS):
            self.load_x(s)
            for l in range(self.L):
                if "ffn1" in self.parts:
                    self.norm(3 * l + 0)
                    self.ffn(l, 0)
                if "mix" in self.parts:
                    self.mixer(l)
                if "ffn2" in self.parts:
                    self.norm(3 * l + 2)
                    self.ffn(l, 1)
            if self.do_final:
                self.norm(3 * self.L, final=True)
            outtoks += self.store_out(s)
        self.sc.final_wait("sp", outtoks)
        cnt = self.sc.emit(self.nc, self.stack)
        self.stack.close()
        return self.nc, cnt

    def causal_kts(self, Q):
        return [(kt, max(0, kt - 4 * Q), 4) for kt in range(4 * Q + 4)]

    def toep_extras(self, hb, Q, kt, j0, j1):
        ex = []
        for j in range(j0, j1):
            dlt = 4 * Q + j - kt
            if dlt in (0, 1):
                ex.append((j * 128, (j + 1) * 128, self.ident[:], self.Tbv(hb, dlt), ["ident", "Tb"]))
        return ex

    def obv(self, ob):
        return V(self.bank[ob], 512, 0, 128, 0, [(128, 4), (1, 64)])

    def osum(self, ob):
        return V(self.bank[ob], 512, 0, 128, 64, [(128, 4)])

    def pass_diff(self, l):
        sc = self.sc
        lam_init = 0.8 - 0.6 * math.exp(-0.3 * (l + self.layer0))
        pj = self.pjD
        for t in range(6):
            self.proj_F(l, 4 + t, [(0, 128, pj[t], ("pjD", t))])
        self.memset(self.VD[:], 1.0, ["VD"])

        def evac(kt, bi):
            dst = V(self.VD, 4160, 0, 128, kt * 260, [(65, 4), (1, 64)])
            src = V(self.bank[bi], 512, 0, 128, 0, [(64, 4), (1, 64)])
            self.cp(dst, src, [("bank", bi)], ["VD"], eng="act" if kt % 2 else "dve")
        self.proj_T(l, self.winTd_d[l, :, :], 256, evac)
        self.dmaq("sp", self.dl_s[:], self.dl_d[l, :, :], [], ["dl_s"], "dl_s")
        self.dmaq("sp", self.sub_s[:], self.sub_d[l, :, :], [], ["sub_s"], "sub_s")
        sc.barrier()
        t0 = V(self.tmp[0], 256, 0, 128, 0, [(1, 32)])
        self.tt(t0, self.dl_s[:, 0:32], self.dl_s[:, 32:64], ALU.mult, ["dl_s"], ["tmp0"])
        self.red(self.lam[:, 0:1], t0, ALU.add, ["tmp0"], ["lam"])
        self.tt(t0, self.dl_s[:, 64:96], self.dl_s[:, 96:128], ALU.mult, ["dl_s", "lam"], ["tmp0"])
        self.red(self.lam[:, 1:2], t0, ALU.add, ["tmp0"], ["lam"])
        self.actf(self.lam[:, 2:4], self.lam[:, 0:2], AF.Exp, ["lam"], ["lam"])
        self.tt(self.lam[:, 4:5], self.lam[:, 2:3], self.lam[:, 3:4], ALU.subtract, ["lam"], ["lam"])
        self.ts(self.lam[:, 5:6], self.lam[:, 4:5], lam_init, -1.0, ALU.add, ALU.mult, ["lam"], ["lam"])
        sub_b = V(self.sub_s, 64, 0, 128, 0, [(0, 4), (1, 64)])
        for h in range(4):
            for Q in range(4):
                obs = []
                for m in range(2):
                    gi = 2 * h + m
                    r0 = 32 * (gi % 3)
                    qtile = pj[gi // 3]
                    ktile = pj[3 + gi // 3]
                    ob = 3 + self.rot("obank", 3)
                    obs.append(ob)
                    self.attn(Q, self.causal_kts(Q),
                              lambda kt, ktile=ktile, r0=r0: V(ktile, 2048, r0, 32, kt * 128, [(1, 128)]),
                              lambda c0, c1, qtile=qtile, r0=r0: V(qtile, 2048, r0, 32, c0, [(1, c1 - c0)]),
                              lambda kt: V(self.VD, 4160, 0, 128, kt * 260 + h * 65, [(1, 65)]),
                              65, SCD, lambda kt, j0, j1: self.toep_extras(4 + h, Q, kt, j0, j1),
                              ob, ["VD"])
                rb = 64 + self.rot("rs", 4) * 16
                rs0 = V(self.small, 256, 0, 128, rb, [(1, 4)])
                rs1 = V(self.small, 256, 0, 128, rb + 4, [(1, 4)])
                nl = V(self.small, 256, 0, 128, rb + 8, [(1, 4)])
                ssv = V(self.small, 256, 0, 128, rb + 12, [(1, 4)])
                self.recip(rs0, self.osum(obs[0]), [("bank", obs[0])], ["rs"])
                self.recip(rs1, self.osum(obs[1]), [("bank", obs[1])], ["rs"])
                self.ts(nl, rs1, self.lam[:, 5:6], None, ALU.mult, None, ["rs", "lam"], ["rs"])
                bc = lambda base: V(self.small, 256, 0, 128, base, [(1, 4), (0, 64)])
                tv = [V(self.tmp[i], 256, 0, 128, 0, [(64, 4), (1, 64)]) for i in range(4)]
                self.tt(tv[0], self.obv(obs[0]), bc(rb), ALU.mult, [("bank", obs[0]), "rs"], ["tmp0"])
                self.tt(tv[1], self.obv(obs[1]), bc(rb + 8), ALU.mult, [("bank", obs[1]), "rs"], ["tmp1"])
                self.tt(tv[0], tv[0], tv[1], ALU.add, ["tmp0", "tmp1"], ["tmp0"])
                self.tt(tv[2], tv[0], tv[0], ALU.mult, ["tmp0"], ["tmp2"])
                self.red(ssv, tv[2], ALU.add, ["tmp2"], ["rs"])
                self.actf(ssv, ssv, AF.Sqrt, ["rs"], ["rs"], scale=1.0 / 64, bias=RMS_EPS)
                self.recip(ssv, ssv, ["rs"], ["rs"])
                self.stt(tv[1], tv[0], 1.0 - lam_init, bc(rb + 12), ALU.mult, ALU.mult,
                         ["tmp0", "rs"], ["tmp1"])
                dst = V(self.otok, 4096, 0, 128, Q * 4 * 256 + h * 64, [(256, 4), (1, 64)])
                self.tt(dst, tv[1], sub_b, ALU.mult, ["tmp1", "sub_s"],
                        [("otok", Q * 4 + j) for j in range(4)])
        self.outproj(l, 1)

    def pass_nsa(self, l):
        sc = self.sc
        pj = self.pjN
        for t in range(8):
            self.proj_F(l, 10 + t, [(0, 128, pj[t], ("pjN", t))])
        self.proj_F(l, 18, [(0, 64, self.kcT[0], ("kcT", 0)), (64, 128, self.kcT[1], ("kcT", 1))])
        self.proj_F(l, 19, [(0, 64, self.kcT[2], ("kcT", 2)), (64, 128, self.kcT[3], ("kcT", 3))])
        self.memset(self.VS[:], 1.0, ["VS"])
        self.memset(self.VW[:], 1.0, ["VW"])

        def evac(kt, bi):
            bank = self.bank[bi]
            dbg = os.environ.get("NSA_DBG", "")
            if "a" in dbg:
                self.cp(self.gate_s[:, kt * 24:(kt + 1) * 24], bank[:, 256:280], [("bank", bi)], ["gate"])
                return
            if "c" in dbg:
                self.cp(V(self.VS, 2080, 0, 128, kt * 130, [(65, 2), (1, 64)]),
                        V(bank, 512, 0, 128, 0, [(64, 2), (1, 64)]), [("bank", bi)], ["VS"], eng="dve")
                self.cp(V(self.VW, 2080, 0, 128, kt * 130, [(65, 2), (1, 64)]),
                        V(bank, 512, 0, 128, 128, [(64, 2), (1, 64)]), [("bank", bi)], ["VW"], eng="dve")
                return
            if "d" in dbg:
                self.actf(self.gate_s[:, kt * 24:(kt + 1) * 24], bank[:, 256:280], AF.Tanh, [("bank", bi)], ["gate"], scale=0.5)
                return
            if "b" in dbg:
                self.cp(V(self.VS, 2080, 0, 128, kt * 130, [(65, 2), (1, 64)]),
                        V(bank, 512, 0, 128, 0, [(64, 2), (1, 64)]), [("bank", bi)], ["VS"], eng="dve")
                return
            self.cp(V(self.VS, 2080, 0, 128, kt * 130, [(65, 2), (1, 64)]),
                    V(bank, 512, 0, 128, 0, [(64, 2), (1, 64)]), [("bank", bi)], ["VS"], eng="dve")
            self.cp(V(self.VW, 2080, 0, 128, kt * 130, [(65, 2), (1, 64)]),
                    V(bank, 512, 0, 128, 128, [(64, 2), (1, 64)]), [("bank", bi)], ["VW"], eng="dve")
            self.cp(self.gate_s[:, kt * 24:(kt + 1) * 24], bank[:, 256:280], [("bank", bi)], ["gate"])
        stage = int(os.environ.get("NSA_STAGE", "99"))
        if stage < 1:
            return
        self.proj_T(l, self.winTn_d[l, :, :], 320, evac)
        self.actf(self.gate_s[:], self.gate_s[:], AF.Tanh, ["gate"], ["gate"], scale=0.5)
        self.ts(self.gate_s[:], self.gate_s[:], 0.5, 0.5, ALU.mult, ALU.add, ["gate"], ["gate"])
        if stage < 2:
            return
        self.dmaq("pool", self.mcmp[:], self.cmcmp_d[:, :], [], ["mcmp"], "mcmp")
        self.dmaq("pool", self.bin_[:], self.cbin_d[:, :], [], ["bin"], "bin")
        self.dmaq("pool", self.wedge[:], self.cwedge_d[:, :], [], ["wedge"], "wedge")
        self.dmaq("sp", self.nA[:], self.cnA_d[:, :], [], ["tabs"], "nA")
        self.dmaq("sp", self.nB[:], self.cnB_d[:, :], [], ["tabs"], "nB")
        self.memset(self.vcaug[:], 1.0, ["vcaug"])
        for g in range(2):
            self.dmaq("pool", self.vcaug[:, g * 97 + 65: g * 97 + 97], self.cov_d[:, :], ["vcaug"], ["vcaug"],
                      "vcaug")
        sc.barrier()
        if stage < 3:
            return
        self.dmaq("pool", self.w2k_s[:], self.w2k_d[l, :, :], [], ["w2k"], "w2k")
        self.dmaq("pool", self.w2v_s[:], self.w2v_d[l, :, :], [], ["w2v"], "w2v")
        self.dmaq("pool", self.peT_s[:], self.peT_d[l, :, :], [], ["peT"], "peT")
        b1first = True
        for i in range(2):
            ab = i
            afirst = True
            for lg in range(8):
                wi = self.rot("w1buf", 2)
                wb = self.w1buf[wi]
                self.dmaq("pool", wb[:], self.w1_d[l, i, lg, :, :], [], [("w1buf", wi)], "w1buf%d" % wi)
                for ll in range(4):
                    li = lg * 4 + ll
                    for hc in range(2):
                        lhsT = wb[0:64, ll * 256 + hc * 128: ll * 256 + (hc + 1) * 128]
                        for g in range(2):
                            self.mm(self.bank[ab][:, (g * 2 + hc) * 128:(g * 2 + hc) * 128 + 127], lhsT,
                                    V(self.kcT[i * 2 + g], 2048, 0, 64, li, [(16, 127)]),
                                    afirst, li == 31, [("w1buf", wi)], [("bank", ab)])
                            afirst = False
                        self.mm(self.bank[2][:, i * 2 + hc: i * 2 + hc + 1], lhsT,
                                self.peT_s[0:64, i * 32 + li: i * 32 + li + 1],
                                b1first, li == 31, [("w1buf", wi), "peT"], [("bank", 2)])
                        b1first = False
            self.cp(self.b1_s[:, 0:2], self.bank[2][:, i * 2: i * 2 + 2], [("bank", 2)], ["b1"])
            for g in range(2):
                for hc in range(2):
                    acc = self.bank[ab][:, (g * 2 + hc) * 128:(g * 2 + hc) * 128 + 127]
                    xs, x2, sgm = self.gx[0][:, 0:127], self.gx[1][:, 0:127], self.gx[2][:, 0:127]
                    self.actf(xs, acc, AF.Identity, [("bank", ab), "b1"], ["gx0"], bias=self.b1_s[:, hc:hc + 1])
                    self.tt(x2, xs, xs, ALU.mult, ["gx0"], ["gx1"])
                    self.ts(x2, x2, 0.044715, 1.0, ALU.mult, ALU.add, ["gx1"], ["gx1"])
                    self.tt(x2, x2, xs, ALU.mult, ["gx1", "gx0"], ["gx1"])
                    self.actf(sgm, x2, AF.Tanh, ["gx1"], ["gx2"], scale=GELU_C * 0.5)
                    self.ts(sgm, sgm, 0.5, 0.5, ALU.mult, ALU.add, ["gx2"], ["gx2"])
                    hdst = V(self.HT, 512, 0, 128, (g * 2 + hc) * 128, [(1, 127)])
                    self.tt(hdst, xs, sgm, ALU.mult, ["gx0", "gx2"], [("HT", g, hc)])
            for g in range(2):
                if i == 0:
                    for hc in range(2):
                        self.mm(self.bank[3][:, g * 128: g * 128 + 127], self.w2k_s[:, hc * 128:(hc + 1) * 128],
                                V(self.HT, 512, 0, 128, (g * 2 + hc) * 128, [(1, 127)]),
                                g == 0 and hc == 0, hc == 1, ["w2k", ("HT", g, hc)], [("bank", 3)])
                else:
                    for hc in range(2):
                        self.mm(self.bank[4][0:127, g * 64:(g + 1) * 64],
                                V(self.HT, 512, 0, 128, (g * 2 + hc) * 128, [(1, 127)]),
                                self.w2v_s[:, hc * 64:(hc + 1) * 64],
                                g == 0 and hc == 0, hc == 1, ["w2v", ("HT", g, hc)], [("bank", 4)])
            if i == 0:
                for g in range(2):
                    self.cp(self.kcmpT[:, g * 128: g * 128 + 127], self.bank[3][:, g * 128: g * 128 + 127],
                            [("bank", 3)], ["kcmpT"])
            else:
                for g in range(2):
                    self.cp(self.vcaug[0:127, g * 97: g * 97 + 64], self.bank[4][0:127, g * 64:(g + 1) * 64],
                            [("bank", 4)], ["vcaug"])
        sc.barrier()
        if stage < 4:
            return
        bc64 = lambda base: V(self.small, 256, 0, 128, base, [(1, 4), (0, 64)])
        bc32 = lambda base: V(self.small, 256, 0, 128, base, [(1, 4), (0, 32)])
        tv = [V(self.tmp[i], 256, 0, 128, 0, [(64, 4), (1, 64)]) for i in range(4)]
        for g in range(2):
            for Q in range(4):
                for r in range(4):
                    h = 4 * g + r
                    r0 = 64 * (h % 2)
                    qtile = pj[h // 2]
                    ob = 3 + self.rot("obank", 3)
                    self.attn(Q, [(0, 0, 4)],
                              lambda kt, r0=r0: V(self.kcmpT, 256, r0, 64, g * 128, [(1, 127)]),
                              lambda c0, c1, qtile=qtile, r0=r0: V(qtile, 2048, r0, 64, c0, [(1, c1 - c0)]),
                              lambda kt: V(self.vcaug, 194, 0, 127, g * 97, [(1, 97)]),
                              97, SC8,
                              lambda kt, j0, j1: [(0, 512, self.ident[0:127, 0:127],
                                                   self.mcmp[0:127, Q * 512:(Q + 1) * 512], ["ident", "mcmp"])],
                              ob, ["vcaug"], kparts=127, extra_reads=["kcmpT"])
                    rb = 64 + self.rot("rs", 4) * 16
                    rs = V(self.small, 256, 0, 128, rb, [(1, 4)])
                    fv = V(self.small, 256, 0, 128, rb + 4, [(1, 4)])
                    self.ts(rs, self.osum(ob), 1e-30, None, ALU.add, None, [("bank", ob)], ["rs"])
                    self.recip(rs, rs, ["rs"], ["rs"])
                    impv = V(self.bank[ob], 512, 0, 128, 65, [(128, 4), (1, 32)])
                    iacc = V(self.impacc, 128, 0, 128, 0, [(32, 4), (1, 32)])
                    if r == 0:
                        self.tt(iacc, impv, bc32(rb), ALU.mult, [("bank", ob), "rs"], ["impacc"])
                    else:
                        itmp = V(self.tmp[3], 256, 0, 128, 0, [(32, 4), (1, 32)])
                        self.tt(itmp, impv, bc32(rb), ALU.mult, [("bank", ob), "rs"], ["tmp3"])
                        self.tt(iacc, iacc, itmp, ALU.add, ["impacc", "tmp3"], ["impacc"])
                    gv = V(self.gate_s, 384, 0, 128, Q * 96 + h * 3 + 0, [(24, 4)])
                    self.tt(fv, rs, gv, ALU.mult, ["rs", "gate"], ["rs"])
                    oc = V(self.ocomb, 1024, 0, 128, r * 256, [(64, 4), (1, 64)])
                    self.tt(oc, self.obv(ob), bc64(rb + 4), ALU.mult, [("bank", ob), "rs"], [("ocomb", r)])
                if stage < 5:
                    continue
                mi = None
                if Q >= 2:
                    scv = V(self.impacc, 128, 0, 128, 0, [(1, 128)])
                    self.tt(scv, scv, self.nA[:, Q * 128:(Q + 1) * 128], ALU.mult, ["impacc", "tabs"], ["impacc"])
                    self.tt(scv, scv, self.nB[:, Q * 128:(Q + 1) * 128], ALU.add, ["impacc", "tabs"], ["impacc"])
                    self.rank_mask(lambda j, dims: V(self.impacc, 128, 0, 128, j * 32, dims), 32, 4, 16,
                                   NEG, lambda j: V(self.mb, 128, 0, 128, j * 32, [(1, 32)]), "impacc", "mb")
                    mi = self.rot("maskT", 2)
                    for j in range(4):
                        self.mm(self.bank[7][0:32, j * 128:(j + 1) * 128],
                                V(self.mb, 128, 0, 128, j * 32, [(1, 32)]), self.ident[:], True, True,
                                ["mb", "ident"], [("bank", 7)])
                    self.cp(self.maskT[mi][0:32, :], self.bank[7][0:32, :], [("bank", 7)], [("maskT", mi)])
                if stage < 6:
                    continue
                for r in range(4):
                    h = 4 * g + r
                    r0 = 64 * (h % 2)
                    qtile = pj[h // 2]
                    ktile = pj[4 + g]
                    ob = 3 + self.rot("obank", 3)

                    def extras(kt, j0, j1, h=h, mi=mi):
                        ex = self.toep_extras(8 + h, Q, kt, j0, j1)
                        if mi is not None:
                            ex.append((j0 * 128, j1 * 128, self.bin_[0:32, kt * 128:(kt + 1) * 128],
                                       self.maskT[mi][0:32, j0 * 128:j1 * 128], ["bin", ("maskT", mi)]))
                        return ex
                    self.attn(Q, self.causal_kts(Q),
                              lambda kt, ktile=ktile, r0=r0: V(ktile, 2048, r0, 64, kt * 128, [(1, 128)]),
                              lambda c0, c1, qtile=qtile, r0=r0: V(qtile, 2048, r0, 64, c0, [(1, c1 - c0)]),
                              lambda kt: V(self.VS, 2080, 0, 128, kt * 130 + g * 65, [(1, 65)]),
                              65, SC8, extras, ob, ["VS"])
                    rb = 64 + self.rot("rs", 4) * 16
                    rs = V(self.small, 256, 0, 128, rb, [(1, 4)])
                    fv = V(self.small, 256, 0, 128, rb + 4, [(1, 4)])
                    self.recip(rs, self.osum(ob), [("bank", ob)], ["rs"])
                    gv = V(self.gate_s, 384, 0, 128, Q * 96 + h * 3 + 1, [(24, 4)])
                    self.tt(fv, rs, gv, ALU.mult, ["rs", "gate"], ["rs"])
                    oc = V(self.ocomb, 1024, 0, 128, r * 256, [(64, 4), (1, 64)])
                    self.tt(tv[0], self.obv(ob), bc64(rb + 4), ALU.mult, [("bank", ob), "rs"], ["tmp0"])
                    self.tt(oc, oc, tv[0], ALU.add, [("ocomb", r), "tmp0"], [("ocomb", r)])
                if stage < 7:
                    continue
                for r in range(4):
                    h = 4 * g + r
                    r0 = 64 * (h % 2)
                    qtile = pj[h // 2]
                    ktile = pj[6 + g]
                    ob = 3 + self.rot("obank", 3)
                    kts = []
                    for kt in range(max(0, 4 * Q - 4), 4 * Q + 4):
                        j0 = max(0, kt - 4 * Q)
                        j1 = min(4, kt - 4 * Q + 5)
                        kts.append((kt, j0, j1))

                    def extras(kt, j0, j1, h=h):
                        ex = self.toep_extras(8 + h, Q, kt, j0, j1)
                        for j in range(j0, j1):
                            if 4 * Q + j - kt == 4:
                                ex.append((j * 128, (j + 1) * 128, self.ident[:], self.wedge[:],
                                           ["ident", "wedge"]))
                        return ex
                    self.attn(Q, kts,
                              lambda kt, ktile=ktile, r0=r0: V(ktile, 2048, r0, 64, kt * 128, [(1, 128)]),
                              lambda c0, c1, qtile=qtile, r0=r0: V(qtile, 2048, r0, 64, c0, [(1, c1 - c0)]),
                              lambda kt: V(self.VW, 2080, 0, 128, kt * 130 + g * 65, [(1, 65)]),
                              65, SC8, extras, ob, ["VW"])
                    rb = 64 + self.rot("rs", 4) * 16
                    rs = V(self.small, 256, 0, 128, rb, [(1, 4)])
                    fv = V(self.small, 256, 0, 128, rb + 4, [(1, 4)])
                    self.recip(rs, self.osum(ob), [("bank", ob)], ["rs"])
                    gv = V(self.gate_s, 384, 0, 128, Q * 96 + h * 3 + 2, [(24, 4)])
                    self.tt(fv, rs, gv, ALU.mult, ["rs", "gate"], ["rs"])
                    oc = V(self.ocomb, 1024, 0, 128, r * 256, [(64, 4), (1, 64)])
                    self.tt(tv[1], self.obv(ob), bc64(rb + 4), ALU.mult, [("bank", ob), "rs"], ["tmp1"])
                    dst = V(self.otok, 4096, 0, 128, Q * 4 * 256 + r * 64, [(256, 4), (1, 64)])
                    self.tt(dst, oc, tv[1], ALU.add, [("ocomb", r), "tmp1"],
                            [("otok", Q * 4 + j) for j in range(4)])
            if stage >= 8:
                self.outproj(l, 2 + g)


def _rel_bucket(dist):
    n = np.maximum(dist, 0)
    max_exact = 16
    n_f = np.maximum(n, max_exact).astype(np.float32)
    large = max_exact + (np.log(n_f / np.float32(max_exact)) / np.float32(np.log(128 / 16))
                         * np.float32(16)).astype(np.int32)
    return np.where(n < max_exact, n, np.minimum(large, 31))


def make_consts():
    c = {}
    k = np.arange(128)[:, None]
    q = np.arange(128)[None, :]
    E = np.zeros((128, 2, 32, 128), np.float32)
    for di in range(2):
        dist = di * 128 + q - k
        bk = _rel_bucket(dist)
        for b in range(32):
            E[:, di, b, :] = ((bk == b) & (dist >= 0)).astype(np.float32)
    c["E_nonzero"] = [[bool(E[:, di, b, :].any()) for b in range(32)] for di in range(2)]
    c["c_E"] = E.reshape(128, -1)
    c["c_ident"] = np.eye(128, dtype=np.float32)
    c["c_caus"] = np.where(q >= k, 0.0, NEG).astype(np.float32)
    c["c_wedge"] = np.where(q < k, 0.0, NEG).astype(np.float32)
    n = np.arange(128)[:, None]
    qq = np.arange(S)[None, :]
    mc = np.where(16 * n + 31 <= qq, 0.0, NEG).astype(np.float32)
    mc[127, :] = 0.0
    c["c_mcmp"] = mc
    kk = np.arange(S)[None, :]
    c["c_bim"] = (kk // 256 == np.arange(8)[:, None]).astype(np.float32)
    c["c_bin"] = (kk // 64 == np.arange(32)[:, None]).astype(np.float32)
    p = np.arange(128)[:, None, None]
    J = np.arange(16)[None, :, None]
    nn = np.arange(8)[None, None, :]
    own = (J * 128 + p) // 256
    A = (nn < own).astype(np.float32)
    B = np.where(nn < own, 0.0, -1e9 - 1e6 * nn).astype(np.float32)
    c["c_mA"] = A.reshape(128, 128)
    c["c_mB"] = np.broadcast_to(B, (128, 16, 8)).reshape(128, 128).copy()
    c["c_mN"] = (NEG * A).reshape(128, 128).astype(np.float32)
    jj = np.arange(32)[None, None, :]
    cur = (J * 128 + p) // 64
    valid = jj <= cur
    forced = (jj == 0) | (jj > cur - 2)
    A = (valid & ~forced).astype(np.float32)
    B = np.where(valid & forced, 1e4 + jj, np.where(~valid, -1e9 - 1e6 * jj, 0.0)).astype(np.float32)
    c["c_nA"] = A.reshape(128, 512)
    c["c_nB"] = np.broadcast_to(B, (128, 16, 32)).reshape(128, 512).copy()
    n_cmp = 127
    cs = np.arange(n_cmp) * 16
    ss = np.arange(32) * 64
    ov = np.clip(np.minimum(cs[:, None] + 32, ss[None, :] + 64) - np.maximum(cs[:, None], ss[None, :]),
                 0, None) / 32
    ovp = np.zeros((128, 32), np.float32)
    ovp[:127] = ov
    c["c_ov"] = ovp
    return c


_CONSTS = None


def _consts():
    global _CONSTS
    if _CONSTS is None:
        _CONSTS = make_consts()
    return _CONSTS


def _lay_gu(g, u):
    L = g.shape[0]
    a = np.stack([g, u], axis=1)
    a = a.reshape(L, 2, KC, 128, NFC, 128)
    a = a.transpose(0, 4, 3, 1, 2, 5)
    return np.ascontiguousarray(a).reshape(L, NFC, 128, 2 * KC * 128)


def _lay_wd(w):
    L = w.shape[0]
    a = w.reshape(L, NFC, 128, KC, 128)
    a = a.transpose(0, 3, 2, 1, 4)
    return np.ascontiguousarray(a).reshape(L, KC, 128, NFC * 128)


def _lay_gains(vecs):
    cols = [v.reshape(KC, 128).T for v in vecs]
    return np.ascontiguousarray(np.concatenate(cols, axis=1)).astype(np.float32)


def _win_tiles():
    tiles = []
    for t in range(2):
        tiles.append(list(range(0 + 128 * t, 128 * (t + 1))))
    for t in range(2):
        tiles.append(list(range(256 + 128 * t, 256 + 128 * (t + 1))))
    for base in (768, 1024):
        for t in range(3):
            cols = []
            for slot in range(4):
                gi = t * 3 + slot
                if slot < 3 and gi < 8:
                    cols += list(range(base + gi * 32, base + gi * 32 + 32))
                else:
                    cols += [-1] * 32
            tiles.append(cols)
    for t in range(4):
        tiles.append(list(range(1536 + 128 * t, 1536 + 128 * (t + 1))))
    for base in (2304, 2560):
        for g in range(2):
            cc = list(range(base + 64 * g, base + 64 * g + 64))
            tiles.append(cc + cc)
    tiles.append(list(range(2048, 2176)))
    tiles.append(list(range(2176, 2304)))
    return tiles


def _lay_cols(w, cols):
    L = w.shape[0]
    idx = np.array(cols)
    sel = w[:, :, np.where(idx < 0, 0, idx)].copy()
    if (idx < 0).any():
        sel[:, :, idx < 0] = 0.0
    a = sel.reshape(L, KC, 128, len(cols)).transpose(0, 2, 1, 3)
    return np.ascontiguousarray(a)


def prep_inputs(p, L, layer0=0):
    sl = slice(layer0, layer0 + L)
    c = _consts()
    m = {}
    gl = []
    for l in range(layer0, layer0 + L):
        gl += [p["norm_ffn1"][l], p["norm_mix"][l], p["norm_ffn2"][l]]
    gl.append(p["final_norm"])
    m["gains"] = _lay_gains(gl)
    m["gu1"] = _lay_gu(p["ffn1_gate"][sl], p["ffn1_up"][sl])
    m["gu2"] = _lay_gu(p["ffn2_gate"][sl], p["ffn2_up"][sl])
    m["wd1"] = _lay_wd(p["ffn1_down"][sl])
    m["wd2"] = _lay_wd(p["ffn2_down"][sl])
    w_in = p["w_in"][sl]
    tiles = _win_tiles()
    m["winF"] = np.ascontiguousarray(
        np.stack([_lay_cols(w_in, t).reshape(L, 128, KC * 128) for t in tiles], axis=1))
    m["winTm"] = _lay_cols(w_in, list(range(512, 768))).reshape(L, 128, KC * 256)
    m["winTd"] = _lay_cols(w_in, list(range(1280, 1536))).reshape(L, 128, KC * 256)
    ncols = list(range(2432, 2560)) + list(range(2688, 2816)) + list(range(2816, 2840)) + [-1] * 40
    m["winTn"] = _lay_cols(w_in, ncols).reshape(L, 128, KC * 320)
    wo = p["w_out"][sl].reshape(L, 4, 2, 128, 1024).transpose(0, 1, 3, 2, 4)
    m["woutp"] = np.ascontiguousarray(wo).reshape(L, 4, 128, 2048)
    w1 = p["nsa_cmp_w1"][sl].reshape(L, 2, 8, 4, 64, 256).transpose(0, 1, 2, 4, 3, 5)
    m["w1p"] = np.ascontiguousarray(w1).reshape(L, 2, 8, 64, 1024)
    w2 = p["nsa_cmp_w2"][sl]
    w2k = w2[:, 0].reshape(L, 2, 128, 64).transpose(0, 2, 1, 3)
    w2k = np.concatenate([w2k, w2k], axis=3)
    m["w2k"] = np.ascontiguousarray(w2k).reshape(L, 128, 256)
    w2v = w2[:, 1].reshape(L, 2, 128, 64).transpose(0, 2, 1, 3)
    m["w2v"] = np.ascontiguousarray(w2v).reshape(L, 128, 128)
    pe = p["nsa_cmp_pe"][sl]
    m["peT"] = np.ascontiguousarray(pe.transpose(0, 3, 1, 2)).reshape(L, 64, 64)
    dl = p["diff_lambda"][sl].reshape(L, 1, 128)
    m["dlrep"] = np.ascontiguousarray(np.broadcast_to(dl, (L, 128, 128)))
    sub = p["diff_subln"][sl].reshape(L, 1, 64)
    m["subrep"] = np.ascontiguousarray(np.broadcast_to(sub, (L, 128, 64)))
    m["tblrep"] = np.ascontiguousarray(np.broadcast_to(p["rel_bias"].reshape(1, 512), (128, 512)))
    for k_, v_ in c.items():
        if k_.startswith("c_"):
            m[k_] = v_
    return {k_: np.ascontiguousarray(v_, dtype=np.float32) for k_, v_ in m.items()}


_PROG_CACHE = {}


def _get_prog(n_layers, n_seq, do_final, parts, branches, layer0):
    key = (n_layers, n_seq, do_final, tuple(parts), tuple(branches), layer0)
    if key not in _PROG_CACHE:
        b = Builder(n_layers, n_seq, do_final, parts, branches, layer0)
        _PROG_CACHE[key] = b.build(_consts())
    return _PROG_CACHE[key]


def run_layers(x, p, n_layers, n_seq_per_core, do_final=True, parts=("ffn1", "mix", "ffn2"),
               branches=("moba", "diff", "nsa"), layer0=0, core_ids=None):
    L = n_layers
    nc, cnt = _get_prog(L, n_seq_per_core, do_final, parts, branches, layer0)
    B = x.shape[0]
    ncores = B // n_seq_per_core
    xT = np.ascontiguousarray(np.transpose(x, (0, 2, 1)))
    shared = prep_inputs(p, L, layer0)
    in_maps = []
    for c in range(ncores):
        m = dict(shared)
        m["xT"] = xT[c * n_seq_per_core:(c + 1) * n_seq_per_core]
        in_maps.append(m)
    res = run_bass_kernel_spmd(nc, in_maps, core_ids=list(range(ncores)) if core_ids is None else core_ids)
    o = np.concatenate([r["outT"] for r in res.results], axis=0)
    return np.ascontiguousarray(np.transpose(o, (0, 2, 1)))


def kernel(**inputs):
    x = np.asarray(inputs["x"], dtype=np.float32)
    p = {k: np.asarray(v, dtype=np.float32) for k, v in inputs.items() if k != "x"}
    return run_layers(x, p, DEPTH, x.shape[0] // NCORES)
```

```python
import math
import os
import numpy as np
import concourse.bass as bass
import concourse.mybir as mybir
from concourse.bass_utils import run_bass_kernel_spmd
from contextlib import ExitStack

F32 = mybir.dt.float32
BF16 = mybir.dt.bfloat16
AF = mybir.ActivationFunctionType
ALU = mybir.AluOpType
AX = mybir.AxisListType

S = 2048
D = 1024
KC = 8
FF = 2816
NFC = 22
DEPTH = 4
NCORES = 8
RMS_EPS = 1e-6
NEG = -30000.0


class _Op:
    __slots__ = ("eng", "fn", "reads", "writes", "isdma", "key", "n", "deps",
                 "signal", "val", "waits")


class Sched:
    ENGS = ("pe", "act", "dve", "pool", "sp")

    def __init__(self):
        self.ops = []
        self.per_eng = {e: [] for e in self.ENGS}
        self.last_w = {}
        self.readers = {}
        self.dma_count = {}
        self.pending_barrier = {e: None for e in self.ENGS}
        self.last_op = {e: None for e in self.ENGS}
        self.live_dma = []

    def _mk(self, eng, fn, reads, writes, isdma, key):
        op = _Op()
        op.eng = eng
        op.fn = fn
        op.reads = tuple(reads)
        op.writes = tuple(writes)
        op.isdma = isdma
        op.key = key
        op.deps = []
        op.signal = False
        op.val = 0
        op.waits = []
        deps = set()
        for t in op.reads:
            w = self.last_w.get(t)
            if w is not None:
                deps.add((w, "raw"))
        for t in op.writes:
            w = self.last_w.get(t)
            if w is not None:
                deps.add((w, "waw"))
            for r in self.readers.get(t, ()):
                deps.add((r, "war"))
        pb = self.pending_barrier[eng]
        if pb is not None:
            for o in pb:
                deps.add((o, "raw"))
            self.pending_barrier[eng] = None
        for (p, kind) in deps:
            if p is op:
                continue
            if (not p.isdma) and (not isdma) and p.eng == eng:
                if kind != "raw" or eng == "pe":
                    continue
            op.deps.append(p)
        for t in op.writes:
            self.last_w[t] = op
            self.readers[t] = []
        for t in op.reads:
            self.readers.setdefault(t, []).append(op)
        if isdma:
            c = self.dma_count.get(key, 0) + 1
            self.dma_count[key] = c
            op.val = 16 * c
            op.signal = True
            self.live_dma.append(op)
        self.ops.append(op)
        self.per_eng[eng].append(op)
        if not isdma:
            self.last_op[eng] = op
        return op

    def add(self, eng, fn, reads=(), writes=()):
        return self._mk(eng, fn, reads, writes, False, None)

    def dma(self, queue, fn, reads=(), writes=(), key=None):
        assert key is not None
        return self._mk(queue, fn, reads, writes, True, key)

    def barrier(self):
        lst = [o for o in self.last_op.values() if o is not None]
        lst += self.live_dma
        self.live_dma = []
        for e in self.ENGS:
            prev = self.pending_barrier[e]
            self.pending_barrier[e] = list(lst) + (prev or [])

    def final_wait(self, eng, tokens):
        return self._mk(eng, None, tokens, (), False, None)

    def analyse(self):
        seqno = {}
        cnt = {e: 0 for e in self.ENGS}
        for op in self.ops:
            if not op.isdma:
                cnt[op.eng] += 1
                seqno[id(op)] = cnt[op.eng]
        seen = {e: {} for e in self.ENGS}
        need = []
        for op in self.ops:
            sd = seen[op.eng]
            best = {}
            for p in op.deps:
                if p.isdma:
                    k = ("d", p.key)
                    v = p.val
                else:
                    k = ("e", p.eng)
                    v = seqno[id(p)]
                if sd.get(k, 0) >= v:
                    continue
                if k not in best or best[k][0] < v:
                    best[k] = (v, p)
            lst = []
            for k, (v, p) in best.items():
                sd[k] = v
                p.signal = True
                lst.append(p)
            need.append(lst)
        cnt = {e: 0 for e in self.ENGS}
        for op in self.ops:
            if (not op.isdma) and op.signal:
                cnt[op.eng] += 1
                op.val = cnt[op.eng]
        for op, lst in zip(self.ops, need):
            op.waits = [(("d", p.key) if p.isdma else ("e", p.eng), p.val) for p in lst]
        return cnt

    def emit(self, nc, stack):
        cnt = self.analyse()
        sems = {}
        for e in self.ENGS:
            sems[("e", e)] = stack.enter_context(nc.semaphore("s_" + e))
        for k in self.dma_count:
            sems[("d", k)] = stack.enter_context(nc.semaphore("d_" + str(k)))
        block = stack.enter_context(nc.Block())

        def run(engname):
            def body(eng):
                for op in self.per_eng[engname]:
                    for (k, v) in op.waits:
                        eng.wait_ge(sems[k], v)
                    if op.fn is None:
                        continue
                    inst = op.fn(eng)
                    if op.isdma:
                        inst.then_inc(sems[("d", op.key)], 16)
                    elif op.signal:
                        inst.then_inc(sems[("e", engname)], 1)
            return body

        block.tensor(run("pe"))
        block.scalar(run("act"))
        block.vector(run("dve"))
        block.gpsimd(run("pool"))
        block.sync(run("sp"))
        return cnt


SB_BASE = 16512
SB_END = 229376

SC8 = 0.125
SCD = 32.0 ** -0.5
GELU_C = 1.5957691216057308


def _nbytes(dt):
    return 4 if dt == F32 else 2


def V(t, ftot, p0, npart, f0, dims):
    return bass.AP(t, p0 * ftot + f0, [[ftot, npart]] + [[s, c] for (s, c) in dims])


class Builder:
    def __init__(self, n_layers, n_seq, do_final=True, parts=("ffn1", "mix", "ffn2"),
                 branches=("moba", "diff", "nsa"), layer0=0):
        self.L = n_layers
        self.NS = n_seq
        self.do_final = do_final
        self.parts = parts
        self.branches = branches
        self.layer0 = layer0
        self.nc = bass.Bass("TRN2", target_bir_lowering=False)
        self.sc = Sched()
        self.stack = ExitStack()
        self.off = SB_BASE
        self.nalloc = 0
        self.rr = {}
        self.pq = []

    def din(self, name, shape, dt=F32):
        return self.nc.dram_tensor(name, list(shape), dt, kind="ExternalInput").ap()

    def dout(self, name, shape, dt=F32):
        return self.nc.dram_tensor(name, list(shape), dt, kind="ExternalOutput").ap()

    def sb(self, name, shape, dt):
        n = 1
        for s_ in shape[1:]:
            n *= s_
        nb = (n * _nbytes(dt) + 63) // 64 * 64
        off = self.off
        self.off += nb
        assert self.off <= SB_END, ("SBUF overflow", name, self.off)
        self.nalloc += 1
        return self.nc.alloc_sbuf_tensor_at("%s_%d" % (name, self.nalloc), list(shape), dt, offset=off)

    def ps(self, name, shape, dt=F32):
        return self.stack.enter_context(self.nc.psum_tensor(name, list(shape), dt))

    def rot(self, name, n):
        i = self.rr.get(name, 0)
        self.rr[name] = i + 1
        return i % n

    def mm(self, out, lhsT, rhs, start, stop, reads, writes):
        self.sc.add("pe", lambda e: e.matmul(out, lhsT, rhs, start=start, stop=stop,
                                             skip_group_check=True), reads, writes)

    def actf(self, out, in_, func, reads, writes, scale=1.0, bias=0.0):
        self.sc.add("act", lambda e: e.activation(out=out, in_=in_, func=func, bias=bias, scale=scale),
                    reads, writes)

    def tt(self, out, in0, in1, op, reads, writes, eng="dve"):
        self.sc.add(eng, lambda e: e.tensor_tensor(out=out, in0=in0, in1=in1, op=op), reads, writes)

    def ts(self, out, in0, s1, s2, op0, op1, reads, writes, eng="dve"):
        if op1 is None:
            self.sc.add(eng, lambda e: e.tensor_scalar(out=out, in0=in0, scalar1=s1, scalar2=None, op0=op0),
                        reads, writes)
        else:
            self.sc.add(eng, lambda e: e.tensor_scalar(out=out, in0=in0, scalar1=s1, scalar2=s2,
                                                        op0=op0, op1=op1), reads, writes)

    def stt(self, out, in0, scalar, in1, op0, op1, reads, writes, eng="dve"):
        self.sc.add(eng, lambda e: e.scalar_tensor_tensor(out=out, in0=in0, scalar=scalar, in1=in1,
                                                           op0=op0, op1=op1), reads, writes)

    def cp(self, out, in_, reads, writes, eng="dve"):
        if eng == "act":
            self.actf(out, in_, AF.Copy, reads, writes)
        else:
            self.sc.add(eng, lambda e: e.tensor_copy(out=out, in_=in_), reads, writes)

    def recip(self, out, in_, reads, writes):
        self.sc.add("dve", lambda e: e.reciprocal(out=out, in_=in_), reads, writes)

    def red(self, out, in_, op, reads, writes):
        self.sc.add("dve", lambda e: e.tensor_reduce(out=out, in_=in_, axis=AX.X, op=op), reads, writes)

    def memset(self, ap, val, writes, eng="dve"):
        self.sc.add(eng, lambda e: e.memset(ap, val), (), writes)

    def dmaq(self, queue, out, in_, reads, writes, key):
        self.sc.dma(queue, lambda e: e.dma_start(out=out, in_=in_), reads, writes, key)

    def declare(self):
        L, NS = self.L, self.NS
        d = self.din
        self.xT_d = d("xT", [NS, D, S])
        self.out_d = self.dout("outT", [NS, D, S])
        self.gains_d = d("gains", [128, (3 * L + 1) * KC])
        self.gu_d = [d("gu%d" % i, [L, NFC, 128, 2 * KC * 128]) for i in (1, 2)]
        self.wd_d = [d("wd%d" % i, [L, KC, 128, NFC * 128]) for i in (1, 2)]
        self.winF_d = d("winF", [L, 20, 128, KC * 128])
        self.winTm_d = d("winTm", [L, 128, KC * 256])
        self.winTd_d = d("winTd", [L, 128, KC * 256])
        self.winTn_d = d("winTn", [L, 128, KC * 320])
        self.wout_d = d("woutp", [L, 4, 128, 2 * 1024])
        self.w1_d = d("w1p", [L, 2, 8, 64, 4 * 256])
        self.w2k_d = d("w2k", [L, 128, 2 * 128])
        self.w2v_d = d("w2v", [L, 128, 2 * 64])
        self.peT_d = d("peT", [L, 64, 2 * 32])
        self.dl_d = d("dlrep", [L, 128, 128])
        self.sub_d = d("subrep", [L, 128, 64])
        self.tbl_d = d("tblrep", [128, 512])
        self.cE_d = d("c_E", [128, 2 * 32 * 128])
        self.cid_d = d("c_ident", [128, 128])
        self.ccaus_d = d("c_caus", [128, 128])
        self.cwedge_d = d("c_wedge", [128, 128])
        self.cmcmp_d = d("c_mcmp", [128, 2048])
        self.cbim_d = d("c_bim", [8, 2048])
        self.cbin_d = d("c_bin", [32, 2048])
        self.cmA_d = d("c_mA", [128, 128])
        self.cmB_d = d("c_mB", [128, 128])
        self.cmN_d = d("c_mN", [128, 128])
        self.cnA_d = d("c_nA", [128, 512])
        self.cnB_d = d("c_nB", [128, 512])
        self.cov_d = d("c_ov", [128, 32])

        sb = self.sb
        self.xT = sb("xT_s", [128, KC, S], F32)
        self.hT = sb("hT_s", [128, KC, S], BF16)
        self.Tb = sb("Tb", [128, 16 * 2 * 128], BF16)
        self.ident = sb("ident", [128, 128], BF16)
        self.ones_bf = sb("ones_bf", [128, 128], BF16)
        self.gains = sb("gains_s", [128, (3 * L + 1) * KC], F32)
        self.lam = sb("lam", [128, 8], F32)
        self.arena0 = self.off

        self.off = self.arena0
        self.E_s = sb("E_s", [128, 2 * 32 * 128], BF16)
        self.tbl_s = sb("tbl_s", [128, 512], F32)
        self.Tacc = sb("Tacc", [128, 2 * 16 * 128], F32)
        self.Ttmp = sb("Ttmp", [128, 16 * 128], F32)
        self.caus_s = sb("caus_s", [128, 128], F32)

        self.off = self.arena0
        self.act_s = sb("act_s", [128, NFC, 1024], BF16)
        self.wgu = [sb("wgu_s%d" % i, [128, 2 * KC * 128], BF16) for i in range(3)]
        self.wd = [sb("wd_s%d" % i, [128, NFC * 128], BF16) for i in range(2)]
        self.sg = [sb("sg%d" % i, [128, 512], F32) for i in range(2)]
        self.sq = [sb("sq%d" % i, [128, 512], BF16) for i in range(2)]
        self.rstd = [sb("rstd%d" % i, [128, 512], F32) for i in range(2)]
        self.ffn_end = self.off

        self.off = self.arena0
        self.PT = [sb("PT%d" % i, [128, 512], BF16) for i in range(4)]
        self.small = sb("small", [128, 256], F32)
        self.maskT = [sb("maskT%d" % i, [32, 512], BF16) for i in range(2)]
        self.mb = sb("mb", [128, 128], BF16)
        self.cmpbuf = sb("cmpbuf", [128, 1024], BF16)
        u0 = self.off
        self.winbuf = [sb("winbuf%d" % i, [128, KC * 128], BF16) for i in range(2)]
        self.wT = sb("wT", [128, KC * 512], BF16)
        e1 = self.off
        self.off = u0
        self.w1buf = [sb("w1buf%d" % i, [64, 4 * 256], BF16) for i in range(2)]
        self.HT = sb("HT", [128, 2 * 2 * 128], BF16)
        self.w2k_s = sb("w2k_s", [128, 2 * 128], BF16)
        self.w2v_s = sb("w2v_s", [128, 2 * 64], BF16)
        self.peT_s = sb("peT_s", [64, 64], BF16)
        self.b1_s = sb("b1_s", [128, 2], F32)
        self.gx = [sb("gx%d" % i, [128, 128], F32) for i in range(3)]
        assert self.off <= e1
        self.off = e1
        k0 = self.off
        self.kcT = [sb("kcT%d" % i, [64, S], BF16) for i in range(4)]
        e2 = self.off
        self.off = k0
        self.sq_m = [sb("sqm%d" % i, [128, 512], BF16) for i in range(2)]
        self.rstd_m = [sb("rstdm%d" % i, [128, 512], F32) for i in range(2)]
        assert self.off <= e2
        self.off = u0
        self.otok = sb("otok", [128, 16 * 256], BF16)
        self.oT = sb("oT", [128, 2 * S], BF16)
        self.wout_s = sb("wout_s", [128, 2 * 1024], BF16)
        self.tmp = [sb("tmp%d" % i, [128, 256], F32) for i in range(4)]
        self.ocomb = sb("ocomb", [128, 4 * 256], F32)
        e3 = self.off
        self.off = max(e2, e3)
        self.mix0 = self.off

        self.off = self.mix0
        self.pjM = [sb("pjM%d" % i, [128, S], BF16) for i in range(4)]
        self.VM = sb("VM", [128, 16 * 4 * 65], BF16)
        self.bim = sb("bim", [8, 2048], BF16)
        self.mA = sb("mA", [128, 128], F32)
        self.mB = sb("mB", [128, 128], F32)
        self.mN = sb("mN", [128, 128], F32)
        self.kmT = sb("kmT", [128, 4 * 8], BF16)
        self.kmF = sb("kmF", [128, 4 * 8], F32)
        self.maskM = [sb("maskM%d" % i, [8, 512], BF16) for i in range(8)]
        self.moba_end = self.off

        self.off = self.mix0
        self.pjD = [sb("pjD%d" % i, [128, S], BF16) for i in range(6)]
        self.VD = sb("VD", [128, 16 * 4 * 65], BF16)
        self.dl_s = sb("dl_s", [128, 128], F32)
        self.sub_s = sb("sub_s", [128, 64], F32)
        self.diff_end = self.off

        self.off = self.mix0
        self.pjN = [sb("pjN%d" % i, [128, S], BF16) for i in range(8)]
        self.VS = sb("VS", [128, 16 * 2 * 65], BF16)
        self.VW = sb("VW", [128, 16 * 2 * 65], BF16)
        self.gate_s = sb("gate_s", [128, 16 * 24], F32)
        self.mcmp = sb("mcmp", [128, 2048], BF16)
        self.bin_ = sb("bin", [32, 2048], BF16)
        self.nA = sb("nA", [128, 512], F32)
        self.nB = sb("nB", [128, 512], F32)
        self.kcmpT = sb("kcmpT", [128, 2 * 128], BF16)
        self.vcaug = sb("vcaug", [128, 2 * 97], BF16)
        self.wedge = sb("wedge", [128, 128], BF16)
        self.impacc = sb("impacc", [128, 128], F32)
        self.nsa_end = self.off
        self.off = max(self.ffn_end, self.moba_end, self.diff_end, self.nsa_end)
        print("SBUF end", self.off, "of", SB_END, "ffn", self.ffn_end, "moba", self.moba_end,
              "diff", self.diff_end, "nsa", self.nsa_end)

        self.bank = [self.ps("bank%d" % i, [128, 512], F32) for i in range(8)]

    def bankv(self, b, p0, npart, f0, dims):
        return V(self.bank[b], 512, p0, npart, f0, dims)

    def setup(self, consts):
        sc = self.sc
        self.dmaq("sp", self.gains[:], self.gains_d[:, :], [], ["gains"], "gains")
        self.memset(self.ones_bf[:], 1.0, ["ones"])
        self.dmaq("pool", self.ident[:], self.cid_d[:, :], [], ["ident"], "ident")
        self.dmaq("pool", self.E_s[:], self.cE_d[:, :], [], ["E_s"], "E_s")
        self.dmaq("sp", self.tbl_s[:], self.tbl_d[:, :], [], ["tbl_s"], "tbl_s")
        self.dmaq("sp", self.caus_s[:], self.ccaus_d[:, :], [], ["caus_s"], "caus_s")
        tb3 = V(self.tbl_s, 512, 0, 128, 0, [(16, 32), (1, 16)])
        t31 = V(self.tbl_s, 512, 0, 128, 31 * 16, [(0, 32), (1, 16)])
        self.tt(tb3, tb3, t31, ALU.subtract, ["tbl_s"], ["tbl_s"])
        first = {0: True, 1: True}
        for di in range(2):
            for b in range(31):
                if not consts["E_nonzero"][di][b]:
                    continue
                Eb = V(self.E_s, 8192, 0, 128, (di * 32 + b) * 128, [(0, 16), (1, 128)])
                tv = V(self.tbl_s, 512, 0, 128, b * 16, [(1, 16), (0, 128)])
                acc = V(self.Tacc, 4096, 0, 128, di * 2048, [(128, 16), (1, 128)])
                if first[di]:
                    self.tt(acc, Eb, tv, ALU.mult, ["E_s", "tbl_s"], [("Tacc", di)])
                    first[di] = False
                else:
                    tmp = V(self.Ttmp, 2048, 0, 128, 0, [(128, 16), (1, 128)])
                    self.tt(tmp, Eb, tv, ALU.mult, ["E_s", "tbl_s"], ["Ttmp"])
                    self.tt(acc, acc, tmp, ALU.add, [("Tacc", di), "Ttmp"], [("Tacc", di)])
        for h in range(16):
            inv = (1.0 / SCD) if 4 <= h < 8 else (1.0 / SC8)
            for di in range(2):
                src = V(self.Tacc, 4096, 0, 128, di * 2048 + h * 128, [(1, 128)])
                dst = V(self.Tb, 4096, 0, 128, (h * 2 + di) * 128, [(1, 128)])
                if di == 0:
                    self.stt(dst, src, inv, self.caus_s[:], ALU.mult, ALU.add,
                             [("Tacc", di), "caus_s"], ["Tb"])
                else:
                    self.ts(dst, src, inv, None, ALU.mult, None, [("Tacc", di)], ["Tb"])
        sc.barrier()

    def Tbv(self, h, di):
        return V(self.Tb, 4096, 0, 128, (h * 2 + di) * 128, [(1, 128)])

    def load_x(self, s):
        for c in range(KC):
            self.dmaq("sp", self.xT[:, c, :], self.xT_d[s, c * 128:(c + 1) * 128, :],
                      [], [("xT", c, tt) for tt in range(4)], "xload%d" % c)

    def store_out(self, s):
        toks = []
        for c in range(KC):
            self.dmaq("sp", self.out_d[s, c * 128:(c + 1) * 128, :], self.xT[:, c, :],
                      [("xT", c, tt) for tt in range(4)], [("outd", s, c)], "xstore%d" % c)
            toks.append(("outd", s, c))
        return toks

    def norm(self, gidx, final=False, mixer=False):
        sqs = self.sq_m if mixer else self.sq
        rstds = self.rstd_m if mixer else self.rstd
        for tt in range(4):
            tsl = slice(tt * 512, (tt + 1) * 512)
            bi = 6 + self.rot("nbank", 2)
            bank = self.bank[bi]
            btok = ("bank", bi)
            for c in range(KC):
                qi = self.rot("sq", 2)
                sq = sqs[qi]
                self.actf(sq[:], self.xT[:, c, tsl], AF.Square, [("xT", c, tt)], [("sq", qi)])
                self.mm(bank[:], self.ones_bf[:], sq[:], c == 0, c == KC - 1,
                        [("sq", qi), "ones"], [btok])
            ri = self.rot("rstd", 2)
            rstd = rstds[ri]
            self.actf(rstd[:], bank[:], AF.Sqrt, [btok], [("rstd", ri)], scale=1.0 / D, bias=RMS_EPS)
            self.recip(rstd[:], rstd[:], [("rstd", ri)], [("rstd", ri)])
            for c in range(KC):
                gcol = self.gains[:, gidx * KC + c: gidx * KC + c + 1]
                if not final:
                    self.stt(self.hT[:, c, tsl], self.xT[:, c, tsl], gcol, rstd[:], ALU.mult, ALU.mult,
                             [("xT", c, tt), ("rstd", ri), "gains"], [("hT", c, tt)])
                else:
                    self.stt(self.xT[:, c, tsl], self.xT[:, c, tsl], gcol, rstd[:], ALU.mult, ALU.mult,
                             [("xT", c, tt), ("rstd", ri), "gains"], [("xT", c, tt)])

    def ffn(self, l, which):
        gu_d = self.gu_d[which]
        wd_d = self.wd_d[which]
        for half in range(2):
            for fc in range(NFC):
                wi = self.rot("wgu", 3)
                w = self.wgu[wi]
                self.dmaq("pool", w[:], gu_d[l, fc, :, :], [], [("wgu", wi)], "wgu%d" % wi)
                for sub in range(2):
                    tt = half * 2 + sub
                    tsl = slice(tt * 512, (tt + 1) * 512)
                    gb = self.rot("abank", 3) * 2
                    gbank, ubank = self.bank[gb], self.bank[gb + 1]
                    for c in range(KC):
                        self.mm(gbank[:], w[:, c * 128:(c + 1) * 128], self.hT[:, c, tsl],
                                c == 0, c == KC - 1, [("wgu", wi), ("hT", c, tt)], [("bank", gb)])
                    for c in range(KC):
                        self.mm(ubank[:], w[:, (KC + c) * 128:(KC + c + 1) * 128], self.hT[:, c, tsl],
                                c == 0, c == KC - 1, [("wgu", wi), ("hT", c, tt)], [("bank", gb + 1)])
                    si = self.rot("sg", 2)
                    sg = self.sg[si]
                    self.actf(sg[:], gbank[:], AF.Silu, [("bank", gb)], [("sg", si)])
                    self.tt(self.act_s[:, fc, sub * 512:(sub + 1) * 512], ubank[:], sg[:], ALU.mult,
                            [("bank", gb + 1), ("sg", si)], [("act", fc, sub)])
            for dc in range(KC):
                wi = self.rot("wd", 2)
                w = self.wd[wi]
                self.dmaq("pool", w[:], wd_d[l, dc, :, :], [], [("wd", wi)], "wd%d" % wi)
                for sub in range(2):
                    tt = half * 2 + sub
                    tsl = slice(tt * 512, (tt + 1) * 512)
                    yb = 6 + self.rot("nbank", 2)
                    ybank = self.bank[yb]
                    for fc in range(NFC):
                        self.mm(ybank[:], w[:, fc * 128:(fc + 1) * 128],
                                self.act_s[:, fc, sub * 512:(sub + 1) * 512],
                                fc == 0, fc == NFC - 1, [("wd", wi), ("act", fc, sub)], [("bank", yb)])
                    self.stt(self.xT[:, dc, tsl], ybank[:], 0.5, self.xT[:, dc, tsl], ALU.mult, ALU.add,
                             [("bank", yb), ("xT", dc, tt)], [("xT", dc, tt)])

    def proj_F(self, l, tile, subs):
        wi = self.rot("winbuf", 2)
        w = self.winbuf[wi]
        self.dmaq("pool", w[:], self.winF_d[l, tile, :, :], [], [("winbuf", wi)], "winbuf%d" % wi)
        for tt in range(4):
            tsl = slice(tt * 512, (tt + 1) * 512)
            for (c0, c1, dst, dtok) in subs:
                M = c1 - c0
                bi = self.rot("pbank", 6)
                bank = self.bank[bi]
                for c in range(KC):
                    self.mm(bank[0:M, :], w[:, c * 128 + c0: c * 128 + c1], self.hT[:, c, tsl],
                            c == 0, c == KC - 1, [("winbuf", wi), ("hT", c, tt)], [("bank", bi)])
                eng = "act" if self.rot("pev", 2) == 0 else "dve"
                self.cp(dst[0:M, tsl], bank[0:M, :], [("bank", bi)], [dtok + (tt,)], eng=eng)

    def proj_T(self, l, wd_ap, ncols, evac):
        wT = self.wT
        self.dmaq("pool", wT[:, 0:KC * ncols], wd_ap, [], ["wT"], "wT")
        for kt in range(16):
            bi = self.rot("pbank", 6)
            bank = self.bank[bi]
            for c in range(KC):
                self.mm(bank[:, 0:ncols], self.hT[:, c, kt * 128:(kt + 1) * 128],
                        wT[:, c * ncols:(c + 1) * ncols], c == 0, c == KC - 1,
                        ["wT", ("hT", c, kt // 4)], [("bank", bi)])
            evac(kt, bi)

    LOOK = 2

    def flush(self):
        while self.pq:
            self.pq.pop(0)()

    def attn(self, Q, kt_list, kap, qap, vap, nv, scale, extras, ob, otoks,
             kparts=128, extra_reads=(), post=None):
        last = {}
        for (kt, j0, j1) in kt_list:
            for j in range(j0, j1):
                last[j] = kt
        obank = self.bank[ob]
        state = {"first": True}
        n = len(kt_list)
        for idx, (kt, j0, j1) in enumerate(kt_list):
            si = self.rot("sbank", 3)
            sbank = self.bank[si]
            c0, c1 = j0 * 128, j1 * 128
            ex = extras(kt, j0, j1)
            self.mm(sbank[0:kparts, c0:c1], kap(kt), qap(Q * 512 + c0, Q * 512 + c1),
                    True, len(ex) == 0, list(extra_reads), [("bank", si)])
            for i, (e0, e1, l_ap, r_ap, rd) in enumerate(ex):
                self.mm(sbank[0:kparts, e0:e1], l_ap, r_ap, False, i == len(ex) - 1,
                        list(rd), [("bank", si)])
            pi = self.rot("PT", 4)
            pt = self.PT[pi]
            self.actf(pt[0:kparts, c0:c1], sbank[0:kparts, c0:c1], AF.Exp, [("bank", si)], [("pt", pi)],
                      scale=scale)

            def stageB(kt=kt, j0=j0, j1=j1, pt=pt, pi=pi, islast=(idx == n - 1)):
                for j in range(j0, j1):
                    self.mm(obank[:, j * 128: j * 128 + nv], pt[0:kparts, j * 128:(j + 1) * 128], vap(kt),
                            state["first"], last[j] == kt, [("pt", pi)] + list(otoks), [("bank", ob)])
                    state["first"] = False
                if islast and post is not None:
                    post()
            self.pq.append(stageB)
            while len(self.pq) > self.LOOK:
                self.pq.pop(0)()

    def outproj(self, l, grp):
        self.flush()
        self.dmaq("pool", self.wout_s[:], self.wout_d[l, grp, :, :], [], ["wout"], "wout")
        for f in range(2):
            for Qi in range(4):
                bi = 6 + self.rot("nbank", 2)
                bank = self.bank[bi]
                for j in range(4):
                    J = Qi * 4 + j
                    src = V(self.otok, 4096, 0, 128, J * 256 + f * 128, [(1, 128)])
                    self.mm(bank[:, j * 128:(j + 1) * 128], src, self.ident[:], True, True,
                            [("otok", J), "ident"], [("bank", bi)])
                eng = "act" if self.rot("pev", 2) == 0 else "dve"
                dst = V(self.oT, 4096, 0, 128, f * 2048 + Qi * 512, [(1, 512)])
                self.cp(dst, bank[:], [("bank", bi)], [("oT", f, Qi)], eng=eng)
        for dc in range(KC):
            for tt in range(4):
                bi = 6 + self.rot("nbank", 2)
                bank = self.bank[bi]
                for f in range(2):
                    self.mm(bank[:], self.wout_s[:, f * 1024 + dc * 128: f * 1024 + (dc + 1) * 128],
                            V(self.oT, 4096, 0, 128, f * 2048 + tt * 512, [(1, 512)]),
                            f == 0, f == 1, ["wout", ("oT", f, tt)], [("bank", bi)])
                tsl = slice(tt * 512, (tt + 1) * 512)
                self.tt(self.xT[:, dc, tsl], bank[:], self.xT[:, dc, tsl], ALU.add,
                        [("bank", bi), ("xT", dc, tt)], [("xT", dc, tt)])

    def rank_mask(self, score_ap_fn, n, nj, K, mult_tab, mb_out_fn, stok, mtok):
        for j in range(nj):
            a = score_ap_fn(j, [(0, n), (1, n)])
            b = score_ap_fn(j, [(1, n), (0, n)])
            cmpv = V(self.cmpbuf, 1024, 0, 128, 0, [(n, n), (1, n)])
            self.tt(cmpv, a, b, ALU.is_gt, [stok], ["cmpbuf"])
            rk = V(self.small, 256, 0, 128, 192, [(1, n)])
            self.red(rk, cmpv, ALU.add, ["cmpbuf"], ["rank"])
            if callable(mult_tab):
                self.stt(mb_out_fn(j), rk, K - 0.5, mult_tab(j), ALU.is_ge, ALU.mult,
                         ["rank", "tabs"], [mtok])
            else:
                self.ts(mb_out_fn(j), rk, K - 0.5, mult_tab, ALU.is_ge, ALU.mult, ["rank"], [mtok])

    def pass_moba(self, l):
        sc = self.sc
        pj = self.pjM
        for t in range(4):
            self.proj_F(l, t, [(0, 128, pj[t], ("pjM", t))])
        self.memset(self.VM[:], 1.0, ["VM"])

        def evac(kt, bi):
            dst = V(self.VM, 4160, 0, 128, kt * 260, [(65, 4), (1, 64)])
            src = V(self.bank[bi], 512, 0, 128, 0, [(64, 4), (1, 64)])
            self.cp(dst, src, [("bank", bi)], ["VM"], eng="act" if kt % 2 else "dve")
        self.proj_T(l, self.winTm_d[l, :, :], 256, evac)
        self.dmaq("pool", self.bim[:], self.cbim_d[:, :], [], ["bim"], "bim")
        self.dmaq("sp", self.mA[:], self.cmA_d[:, :], [], ["tabs"], "mA")
        self.dmaq("sp", self.mB[:], self.cmB_d[:, :], [], ["tabs"], "mB")
        self.dmaq("sp", self.mN[:], self.cmN_d[:, :], [], ["tabs"], "mN")
        sc.barrier()
        for h in range(4):
            r0 = 64 * (h % 2)
            kt_t = pj[2 + h // 2]
            src = V(kt_t, 2048, r0, 64, 0, [(256, 8), (1, 256)])
            dstf = V(self.kmF, 32, r0, 64, h * 8, [(1, 8)])
            self.red(dstf, src, ALU.add, [("pjM", 2 + h // 2, tt) for tt in range(4)], [("kmF", h)])
            dst = V(self.kmT, 32, r0, 64, h * 8, [(1, 8)])
            self.ts(dst, dstf, 1.0 / 256, None, ALU.mult, None, [("kmF", h)], [("kmT", h)])
        mask_of = {}
        for h in range(4):
            r0 = 64 * (h % 2)
            qtile = pj[h // 2]
            qtoks = [("pjM", h // 2, tt) for tt in range(4)]
            for Q in (2, 3):
                gb = 7
                for j in range(4):
                    J = Q * 4 + j
                    self.mm(self.bank[gb][:, j * 8:(j + 1) * 8],
                            V(qtile, 2048, r0, 64, J * 128, [(1, 128)]),
                            V(self.kmT, 32, r0, 64, h * 8, [(1, 8)]), True, True,
                            qtoks + [("kmT", h)], [("bank", gb)])
                scv = V(self.small, 256, 0, 128, 0, [(1, 32)])
                Av = V(self.mA, 128, 0, 128, Q * 32, [(1, 32)])
                Bv = V(self.mB, 128, 0, 128, Q * 32, [(1, 32)])
                self.tt(scv, self.bank[gb][:, 0:32], Av, ALU.mult, [("bank", gb), "tabs"], ["score"])
                self.tt(scv, scv, Bv, ALU.add, ["score", "tabs"], ["score"])
                self.rank_mask(lambda j, dims: V(self.small, 256, 0, 128, j * 8, dims), 8, 4, 3,
                               lambda j, Q=Q: V(self.mN, 128, 0, 128, Q * 32 + j * 8, [(1, 8)]),
                               lambda j: V(self.mb, 128, 0, 128, j * 8, [(1, 8)]), "score", "mb")
                mi = h * 2 + (Q - 2)
                for j in range(4):
                    self.mm(self.bank[gb][0:8, j * 128:(j + 1) * 128],
                            V(self.mb, 128, 0, 128, j * 8, [(1, 8)]), self.ident[:], True, True,
                            ["mb", "ident"], [("bank", gb)])
                self.cp(self.maskM[mi][0:8, :], self.bank[gb][0:8, :], [("bank", gb)], [("maskM", mi)])
                mask_of[(h, Q)] = mi
        for h in range(4):
            r0 = 64 * (h % 2)
            qtile = pj[h // 2]
            ktile = pj[2 + h // 2]
            qtoks = [("pjM", h // 2, tt) for tt in range(4)]
            ktoks = [("pjM", 2 + h // 2, tt) for tt in range(4)]
            for Q in range(4):
                mi = mask_of.get((h, Q))
                ob = 3 + self.rot("obank", 3)

                def extras(kt, j0, j1, Q=Q, h=h, mi=mi):
                    ex = self.toep_extras(h, Q, kt, j0, j1)
                    if mi is not None and (kt // 2) < (4 * Q + 3) // 2:
                        ex.append((j0 * 128, j1 * 128, self.bim[0:8, kt * 128:(kt + 1) * 128],
                                   self.maskM[mi][0:8, j0 * 128:j1 * 128], ["bim", ("maskM", mi)]))
                    return ex

                def post(Q=Q, h=h, ob=ob):
                    rbase = 64 + self.rot("rs", 4) * 16
                    rs = V(self.small, 256, 0, 128, rbase, [(1, 4)])
                    self.recip(rs, self.osum(ob), [("bank", ob)], ["rs"])
                    rsb = V(self.small, 256, 0, 128, rbase, [(1, 4), (0, 64)])
                    dst = V(self.otok, 4096, 0, 128, Q * 4 * 256 + h * 64, [(256, 4), (1, 64)])
                    self.tt(dst, self.obv(ob), rsb, ALU.mult, [("bank", ob), "rs"],
                            [("otok", Q * 4 + j) for j in range(4)])
                self.attn(Q, self.causal_kts(Q),
                          lambda kt, ktile=ktile, r0=r0: V(ktile, 2048, r0, 64, kt * 128, [(1, 128)]),
                          lambda c0, c1, qtile=qtile, r0=r0: V(qtile, 2048, r0, 64, c0, [(1, c1 - c0)]),
                          lambda kt, h=h: V(self.VM, 4160, 0, 128, kt * 260 + h * 65, [(1, 65)]),
                          65, SC8, extras, ob, ["VM"], extra_reads=qtoks + ktoks, post=post)
        self.outproj(l, 0)

    def mixer(self, l):
        sc = self.sc
        sc.barrier()
        self.norm(3 * l + 1, mixer=True)
        if "moba" in self.branches:
            sc.barrier()
            self.pass_moba(l)
        if "diff" in self.branches:
            sc.barrier()
            self.pass_diff(l)
        if "nsa" in self.branches:
            sc.barrier()
            self.pass_nsa(l)
        sc.barrier()

    def build(self, consts):
        self.declare()
        self.setup(consts)
        outtoks = []
        for s in range(self.NS):
            self.load_x(s)
            for l in range(self.L):
                if "ffn1" in self.parts:
                    self.norm(3 * l + 0)
                    self.ffn(l, 0)
                if "mix" in self.parts:
                    self.mixer(l)
                if "ffn2" in self.parts:
                    self.norm(3 * l + 2)
                    self.ffn(l, 1)
            if self.do_final:
                self.norm(3 * self.L, final=True)
            outtoks += self.store_out(s)
        self.sc.final_wait("sp", outtoks)
        cnt = self.sc.emit(self.nc, self.stack)
        self.stack.close()
        return self.nc, cnt

    def causal_kts(self, Q):
        return [(kt, max(0, kt - 4 * Q), 4) for kt in range(4 * Q + 4)]

    def toep_extras(self, hb, Q, kt, j0, j1):
        ex = []
        for j in range(j0, j1):
            dlt = 4 * Q + j - kt
            if dlt in (0, 1):
                ex.append((j * 128, (j + 1) * 128, self.ident[:], self.Tbv(hb, dlt), ["ident", "Tb"]))
        return ex

    def obv(self, ob):
        return V(self.bank[ob], 512, 0, 128, 0, [(128, 4), (1, 64)])

    def osum(self, ob):
        return V(self.bank[ob], 512, 0, 128, 64, [(128, 4)])

    def pass_diff(self, l):
        sc = self.sc
        lam_init = 0.8 - 0.6 * math.exp(-0.3 * (l + self.layer0))
        pj = self.pjD
        for t in range(6):
            self.proj_F(l, 4 + t, [(0, 128, pj[t], ("pjD", t))])
        self.memset(self.VD[:], 1.0, ["VD"])

        def evac(kt, bi):
            dst = V(self.VD, 4160, 0, 128, kt * 260, [(65, 4), (1, 64)])
            src = V(self.bank[bi], 512, 0, 128, 0, [(64, 4), (1, 64)])
            self.cp(dst, src, [("bank", bi)], ["VD"], eng="act" if kt % 2 else "dve")
        self.proj_T(l, self.winTd_d[l, :, :], 256, evac)
        self.dmaq("sp", self.dl_s[:], self.dl_d[l, :, :], [], ["dl_s"], "dl_s")
        self.dmaq("sp", self.sub_s[:], self.sub_d[l, :, :], [], ["sub_s"], "sub_s")
        sc.barrier()
        t0 = V(self.tmp[0], 256, 0, 128, 0, [(1, 32)])
        self.tt(t0, self.dl_s[:, 0:32], self.dl_s[:, 32:64], ALU.mult, ["dl_s"], ["tmp0"])
        self.red(self.lam[:, 0:1], t0, ALU.add, ["tmp0"], ["lam"])
        self.tt(t0, self.dl_s[:, 64:96], self.dl_s[:, 96:128], ALU.mult, ["dl_s", "lam"], ["tmp0"])
        self.red(self.lam[:, 1:2], t0, ALU.add, ["tmp0"], ["lam"])
        self.actf(self.lam[:, 2:4], self.lam[:, 0:2], AF.Exp, ["lam"], ["lam"])
        self.tt(self.lam[:, 4:5], self.lam[:, 2:3], self.lam[:, 3:4], ALU.subtract, ["lam"], ["lam"])
        self.ts(self.lam[:, 5:6], self.lam[:, 4:5], lam_init, -1.0, ALU.add, ALU.mult, ["lam"], ["lam"])
        sub_b = V(self.sub_s, 64, 0, 128, 0, [(0, 4), (1, 64)])
        tv = [V(self.tmp[i], 256, 0, 128, 0, [(64, 4), (1, 64)]) for i in range(4)]
        bc = lambda base: V(self.small, 256, 0, 128, base, [(1, 4), (0, 64)])
        for h in range(4):
            for Q in range(4):
                obs = [3 + self.rot("obank", 3), 3 + self.rot("obank", 3)]

                def post(h=h, Q=Q, obs=obs):
                    rb = 64 + self.rot("rs", 4) * 16
                    rs0 = V(self.small, 256, 0, 128, rb, [(1, 4)])
                    rs1 = V(self.small, 256, 0, 128, rb + 4, [(1, 4)])
                    nl = V(self.small, 256, 0, 128, rb + 8, [(1, 4)])
                    ssv = V(self.small, 256, 0, 128, rb + 12, [(1, 4)])
                    self.recip(rs0, self.osum(obs[0]), [("bank", obs[0])], ["rs"])
                    self.recip(rs1, self.osum(obs[1]), [("bank", obs[1])], ["rs"])
                    self.ts(nl, rs1, self.lam[:, 5:6], None, ALU.mult, None, ["rs", "lam"], ["rs"])
                    self.tt(tv[0], self.obv(obs[0]), bc(rb), ALU.mult, [("bank", obs[0]), "rs"], ["tmp0"])
                    self.tt(tv[1], self.obv(obs[1]), bc(rb + 8), ALU.mult, [("bank", obs[1]), "rs"], ["tmp1"])
                    self.tt(tv[0], tv[0], tv[1], ALU.add, ["tmp0", "tmp1"], ["tmp0"])
                    self.tt(tv[2], tv[0], tv[0], ALU.mult, ["tmp0"], ["tmp2"])
                    self.red(ssv, tv[2], ALU.add, ["tmp2"], ["rs"])
                    self.actf(ssv, ssv, AF.Sqrt, ["rs"], ["rs"], scale=1.0 / 64, bias=RMS_EPS)
                    self.recip(ssv, ssv, ["rs"], ["rs"])
                    self.stt(tv[1], tv[0], 1.0 - lam_init, bc(rb + 12), ALU.mult, ALU.mult,
                             ["tmp0", "rs"], ["tmp1"])
                    dst = V(self.otok, 4096, 0, 128, Q * 4 * 256 + h * 64, [(256, 4), (1, 64)])
                    self.tt(dst, tv[1], sub_b, ALU.mult, ["tmp1", "sub_s"],
                            [("otok", Q * 4 + j) for j in range(4)])
                for m in range(2):
                    gi = 2 * h + m
                    r0 = 32 * (gi % 3)
                    qtile = pj[gi // 3]
                    ktile = pj[3 + gi // 3]
                    self.attn(Q, self.causal_kts(Q),
                              lambda kt, ktile=ktile, r0=r0: V(ktile, 2048, r0, 32, kt * 128, [(1, 128)]),
                              lambda c0, c1, qtile=qtile, r0=r0: V(qtile, 2048, r0, 32, c0, [(1, c1 - c0)]),
                              lambda kt, h=h: V(self.VD, 4160, 0, 128, kt * 260 + h * 65, [(1, 65)]),
                              65, SCD, lambda kt, j0, j1, h=h, Q=Q: self.toep_extras(4 + h, Q, kt, j0, j1),
                              obs[m], ["VD"], post=(post if m == 1 else None))
        self.outproj(l, 1)

    def pass_nsa(self, l):
        sc = self.sc
        pj = self.pjN
        for t in range(8):
            self.proj_F(l, 10 + t, [(0, 128, pj[t], ("pjN", t))])
        self.proj_F(l, 18, [(0, 64, self.kcT[0], ("kcT", 0)), (64, 128, self.kcT[1], ("kcT", 1))])
        self.proj_F(l, 19, [(0, 64, self.kcT[2], ("kcT", 2)), (64, 128, self.kcT[3], ("kcT", 3))])
        self.memset(self.VS[:], 1.0, ["VS"])
        self.memset(self.VW[:], 1.0, ["VW"])

        def evac(kt, bi):
            bank = self.bank[bi]
            dbg = os.environ.get("NSA_DBG", "")
            if "a" in dbg:
                self.cp(self.gate_s[:, kt * 24:(kt + 1) * 24], bank[:, 256:280], [("bank", bi)], ["gate"])
                return
            if "c" in dbg:
                self.cp(V(self.VS, 2080, 0, 128, kt * 130, [(65, 2), (1, 64)]),
                        V(bank, 512, 0, 128, 0, [(64, 2), (1, 64)]), [("bank", bi)], ["VS"], eng="dve")
                self.cp(V(self.VW, 2080, 0, 128, kt * 130, [(65, 2), (1, 64)]),
                        V(bank, 512, 0, 128, 128, [(64, 2), (1, 64)]), [("bank", bi)], ["VW"], eng="dve")
                return
            if "d" in dbg:
                self.actf(self.gate_s[:, kt * 24:(kt + 1) * 24], bank[:, 256:280], AF.Tanh, [("bank", bi)], ["gate"], scale=0.5)
                return
            if "b" in dbg:
                self.cp(V(self.VS, 2080, 0, 128, kt * 130, [(65, 2), (1, 64)]),
                        V(bank, 512, 0, 128, 0, [(64, 2), (1, 64)]), [("bank", bi)], ["VS"], eng="dve")
                return
            self.cp(V(self.VS, 2080, 0, 128, kt * 130, [(65, 2), (1, 64)]),
                    V(bank, 512, 0, 128, 0, [(64, 2), (1, 64)]), [("bank", bi)], ["VS"], eng="dve")
            self.cp(V(self.VW, 2080, 0, 128, kt * 130, [(65, 2), (1, 64)]),
                    V(bank, 512, 0, 128, 128, [(64, 2), (1, 64)]), [("bank", bi)], ["VW"], eng="dve")
            self.cp(self.gate_s[:, kt * 24:(kt + 1) * 24], bank[:, 256:280], [("bank", bi)], ["gate"])
        stage = int(os.environ.get("NSA_STAGE", "99"))
        if stage < 1:
            return
        self.proj_T(l, self.winTn_d[l, :, :], 320, evac)
        self.actf(self.gate_s[:], self.gate_s[:], AF.Tanh, ["gate"], ["gate"], scale=0.5)
        self.ts(self.gate_s[:], self.gate_s[:], 0.5, 0.5, ALU.mult, ALU.add, ["gate"], ["gate"])
        if stage < 2:
            return
        self.dmaq("pool", self.mcmp[:], self.cmcmp_d[:, :], [], ["mcmp"], "mcmp")
        self.dmaq("pool", self.bin_[:], self.cbin_d[:, :], [], ["bin"], "bin")
        self.dmaq("pool", self.wedge[:], self.cwedge_d[:, :], [], ["wedge"], "wedge")
        self.dmaq("sp", self.nA[:], self.cnA_d[:, :], [], ["tabs"], "nA")
        self.dmaq("sp", self.nB[:], self.cnB_d[:, :], [], ["tabs"], "nB")
        self.memset(self.vcaug[:], 1.0, ["vcaug"])
        for g in range(2):
            self.dmaq("pool", self.vcaug[:, g * 97 + 65: g * 97 + 97], self.cov_d[:, :], ["vcaug"], ["vcaug"],
                      "vcaug")
        sc.barrier()
        if stage < 3:
            return
        self.dmaq("pool", self.w2k_s[:], self.w2k_d[l, :, :], [], ["w2k"], "w2k")
        self.dmaq("pool", self.w2v_s[:], self.w2v_d[l, :, :], [], ["w2v"], "w2v")
        self.dmaq("pool", self.peT_s[:], self.peT_d[l, :, :], [], ["peT"], "peT")
        b1first = True
        for i in range(2):
            ab = i
            afirst = True
            for lg in range(8):
                wi = self.rot("w1buf", 2)
                wb = self.w1buf[wi]
                self.dmaq("pool", wb[:], self.w1_d[l, i, lg, :, :], [], [("w1buf", wi)], "w1buf%d" % wi)
                for ll in range(4):
                    li = lg * 4 + ll
                    for hc in range(2):
                        lhsT = wb[0:64, ll * 256 + hc * 128: ll * 256 + (hc + 1) * 128]
                        for g in range(2):
                            self.mm(self.bank[ab][:, (g * 2 + hc) * 128:(g * 2 + hc) * 128 + 127], lhsT,
                                    V(self.kcT[i * 2 + g], 2048, 0, 64, li, [(16, 127)]),
                                    afirst, li == 31, [("w1buf", wi)], [("bank", ab)])
                            afirst = False
                        self.mm(self.bank[2][:, i * 2 + hc: i * 2 + hc + 1], lhsT,
                                self.peT_s[0:64, i * 32 + li: i * 32 + li + 1],
                                b1first, li == 31, [("w1buf", wi), "peT"], [("bank", 2)])
                        b1first = False
            self.cp(self.b1_s[:, 0:2], self.bank[2][:, i * 2: i * 2 + 2], [("bank", 2)], ["b1"])
            for g in range(2):
                for hc in range(2):
                    acc = self.bank[ab][:, (g * 2 + hc) * 128:(g * 2 + hc) * 128 + 127]
                    xs, x2, sgm = self.gx[0][:, 0:127], self.gx[1][:, 0:127], self.gx[2][:, 0:127]
                    self.actf(xs, acc, AF.Identity, [("bank", ab), "b1"], ["gx0"], bias=self.b1_s[:, hc:hc + 1])
                    self.tt(x2, xs, xs, ALU.mult, ["gx0"], ["gx1"])
                    self.ts(x2, x2, 0.044715, 1.0, ALU.mult, ALU.add, ["gx1"], ["gx1"])
                    self.tt(x2, x2, xs, ALU.mult, ["gx1", "gx0"], ["gx1"])
                    self.actf(sgm, x2, AF.Tanh, ["gx1"], ["gx2"], scale=GELU_C * 0.5)
                    self.ts(sgm, sgm, 0.5, 0.5, ALU.mult, ALU.add, ["gx2"], ["gx2"])
                    hdst = V(self.HT, 512, 0, 128, (g * 2 + hc) * 128, [(1, 127)])
                    self.tt(hdst, xs, sgm, ALU.mult, ["gx0", "gx2"], [("HT", g, hc)])
            for g in range(2):
                if i == 0:
                    for hc in range(2):
                        self.mm(self.bank[3][:, g * 128: g * 128 + 127], self.w2k_s[:, hc * 128:(hc + 1) * 128],
                                V(self.HT, 512, 0, 128, (g * 2 + hc) * 128, [(1, 127)]),
                                g == 0 and hc == 0, hc == 1, ["w2k", ("HT", g, hc)], [("bank", 3)])
                else:
                    for hc in range(2):
                        self.mm(self.bank[4][0:127, g * 64:(g + 1) * 64],
                                V(self.HT, 512, 0, 128, (g * 2 + hc) * 128, [(1, 127)]),
                                self.w2v_s[:, hc * 64:(hc + 1) * 64],
                                g == 0 and hc == 0, hc == 1, ["w2v", ("HT", g, hc)], [("bank", 4)])
            if i == 0:
                for g in range(2):
                    self.cp(self.kcmpT[:, g * 128: g * 128 + 127], self.bank[3][:, g * 128: g * 128 + 127],
                            [("bank", 3)], ["kcmpT"])
            else:
                for g in range(2):
                    self.cp(self.vcaug[0:127, g * 97: g * 97 + 64], self.bank[4][0:127, g * 64:(g + 1) * 64],
                            [("bank", 4)], ["vcaug"])
        sc.barrier()
        if stage < 4:
            return
        bc64 = lambda base: V(self.small, 256, 0, 128, base, [(1, 4), (0, 64)])
        bc32 = lambda base: V(self.small, 256, 0, 128, base, [(1, 4), (0, 32)])
        tv = [V(self.tmp[i], 256, 0, 128, 0, [(64, 4), (1, 64)]) for i in range(4)]
        for g in range(2):
            for Q in range(4):
                for r in range(4):
                    h = 4 * g + r
                    r0 = 64 * (h % 2)
                    qtile = pj[h // 2]
                    ob = 3 + self.rot("obank", 3)

                    def post(r=r, h=h, ob=ob, Q=Q):
                        rb = 64 + self.rot("rs", 4) * 16
                        rs = V(self.small, 256, 0, 128, rb, [(1, 4)])
                        fv = V(self.small, 256, 0, 128, rb + 4, [(1, 4)])
                        self.ts(rs, self.osum(ob), 1e-30, None, ALU.add, None, [("bank", ob)], ["rs"])
                        self.recip(rs, rs, ["rs"], ["rs"])
                        impv = V(self.bank[ob], 512, 0, 128, 65, [(128, 4), (1, 32)])
                        iacc = V(self.impacc, 128, 0, 128, 0, [(32, 4), (1, 32)])
                        if r == 0:
                            self.tt(iacc, impv, bc32(rb), ALU.mult, [("bank", ob), "rs"], ["impacc"])
                        else:
                            itmp = V(self.tmp[3], 256, 0, 128, 0, [(32, 4), (1, 32)])
                            self.tt(itmp, impv, bc32(rb), ALU.mult, [("bank", ob), "rs"], ["tmp3"])
                            self.tt(iacc, iacc, itmp, ALU.add, ["impacc", "tmp3"], ["impacc"])
                        gv = V(self.gate_s, 384, 0, 128, Q * 96 + h * 3 + 0, [(24, 4)])
                        self.tt(fv, rs, gv, ALU.mult, ["rs", "gate"], ["rs"])
                        oc = V(self.ocomb, 1024, 0, 128, r * 256, [(64, 4), (1, 64)])
                        self.tt(oc, self.obv(ob), bc64(rb + 4), ALU.mult, [("bank", ob), "rs"], [("ocomb", r)])
                    self.attn(Q, [(0, 0, 4)],
                              lambda kt, r0=r0, g=g: V(self.kcmpT, 256, r0, 64, g * 128, [(1, 127)]),
                              lambda c0, c1, qtile=qtile, r0=r0: V(qtile, 2048, r0, 64, c0, [(1, c1 - c0)]),
                              lambda kt, g=g: V(self.vcaug, 194, 0, 127, g * 97, [(1, 97)]),
                              97, SC8,
                              lambda kt, j0, j1, Q=Q: [(0, 512, self.ident[0:127, 0:127],
                                                        self.mcmp[0:127, Q * 512:(Q + 1) * 512],
                                                        ["ident", "mcmp"])],
                              ob, ["vcaug"], kparts=127, extra_reads=["kcmpT"], post=post)
                if stage < 5:
                    continue
                mi = None
                if Q >= 2:
                    self.flush()
                    scv = V(self.impacc, 128, 0, 128, 0, [(1, 128)])
                    self.tt(scv, scv, self.nA[:, Q * 128:(Q + 1) * 128], ALU.mult, ["impacc", "tabs"], ["impacc"])
                    self.tt(scv, scv, self.nB[:, Q * 128:(Q + 1) * 128], ALU.add, ["impacc", "tabs"], ["impacc"])
                    self.rank_mask(lambda j, dims: V(self.impacc, 128, 0, 128, j * 32, dims), 32, 4, 16,
                                   NEG, lambda j: V(self.mb, 128, 0, 128, j * 32, [(1, 32)]), "impacc", "mb")
                    mi = self.rot("maskT", 2)
                if stage < 6:
                    continue
                for r in range(4):
                    h = 4 * g + r
                    r0 = 64 * (h % 2)
                    qtile = pj[h // 2]
                    ktile = pj[6 + g]
                    ob = 3 + self.rot("obank", 3)
                    kts = []
                    for kt in range(max(0, 4 * Q - 4), 4 * Q + 4):
                        j0 = max(0, kt - 4 * Q)
                        j1 = min(4, kt - 4 * Q + 5)
                        kts.append((kt, j0, j1))

                    def extras(kt, j0, j1, h=h, Q=Q):
                        ex = self.toep_extras(8 + h, Q, kt, j0, j1)
                        for j in range(j0, j1):
                            if 4 * Q + j - kt == 4:
                                ex.append((j * 128, (j + 1) * 128, self.ident[:], self.wedge[:],
                                           ["ident", "wedge"]))
                        return ex

                    def post(r=r, h=h, ob=ob, Q=Q):
                        rb = 64 + self.rot("rs", 4) * 16
                        rs = V(self.small, 256, 0, 128, rb, [(1, 4)])
                        fv = V(self.small, 256, 0, 128, rb + 4, [(1, 4)])
                        self.recip(rs, self.osum(ob), [("bank", ob)], ["rs"])
                        gv = V(self.gate_s, 384, 0, 128, Q * 96 + h * 3 + 2, [(24, 4)])
                        self.tt(fv, rs, gv, ALU.mult, ["rs", "gate"], ["rs"])
                        oc = V(self.ocomb, 1024, 0, 128, r * 256, [(64, 4), (1, 64)])
                        self.tt(tv[1], self.obv(ob), bc64(rb + 4), ALU.mult, [("bank", ob), "rs"], ["tmp1"])
                        self.tt(oc, oc, tv[1], ALU.add, [("ocomb", r), "tmp1"], [("ocomb", r)])
                    self.attn(Q, kts,
                              lambda kt, ktile=ktile, r0=r0: V(ktile, 2048, r0, 64, kt * 128, [(1, 128)]),
                              lambda c0, c1, qtile=qtile, r0=r0: V(qtile, 2048, r0, 64, c0, [(1, c1 - c0)]),
                              lambda kt, g=g: V(self.VW, 2080, 0, 128, kt * 130 + g * 65, [(1, 65)]),
                              65, SC8, extras, ob, ["VW"], post=post)
                if stage < 7:
                    continue
                if mi is not None:
                    self.flush()
                    for j in range(4):
                        self.mm(self.bank[7][0:32, j * 128:(j + 1) * 128],
                                V(self.mb, 128, 0, 128, j * 32, [(1, 32)]), self.ident[:], True, True,
                                ["mb", "ident"], [("bank", 7)])
                    self.cp(self.maskT[mi][0:32, :], self.bank[7][0:32, :], [("bank", 7)], [("maskT", mi)])
                for r in range(4):
                    h = 4 * g + r
                    r0 = 64 * (h % 2)
                    qtile = pj[h // 2]
                    ktile = pj[4 + g]
                    ob = 3 + self.rot("obank", 3)

                    def extras(kt, j0, j1, h=h, mi=mi, Q=Q):
                        ex = self.toep_extras(8 + h, Q, kt, j0, j1)
                        if mi is not None:
                            ex.append((j0 * 128, j1 * 128, self.bin_[0:32, kt * 128:(kt + 1) * 128],
                                       self.maskT[mi][0:32, j0 * 128:j1 * 128], ["bin", ("maskT", mi)]))
                        return ex

                    def post(r=r, h=h, ob=ob, Q=Q):
                        rb = 64 + self.rot("rs", 4) * 16
                        rs = V(self.small, 256, 0, 128, rb, [(1, 4)])
                        fv = V(self.small, 256, 0, 128, rb + 4, [(1, 4)])
                        self.recip(rs, self.osum(ob), [("bank", ob)], ["rs"])
                        gv = V(self.gate_s, 384, 0, 128, Q * 96 + h * 3 + 1, [(24, 4)])
                        self.tt(fv, rs, gv, ALU.mult, ["rs", "gate"], ["rs"])
                        oc = V(self.ocomb, 1024, 0, 128, r * 256, [(64, 4), (1, 64)])
                        self.tt(tv[0], self.obv(ob), bc64(rb + 4), ALU.mult, [("bank", ob), "rs"], ["tmp0"])
                        dst = V(self.otok, 4096, 0, 128, Q * 4 * 256 + r * 64, [(256, 4), (1, 64)])
                        self.tt(dst, oc, tv[0], ALU.add, [("ocomb", r), "tmp0"],
                                [("otok", Q * 4 + j) for j in range(4)])
                    self.attn(Q, self.causal_kts(Q),
                              lambda kt, ktile=ktile, r0=r0: V(ktile, 2048, r0, 64, kt * 128, [(1, 128)]),
                              lambda c0, c1, qtile=qtile, r0=r0: V(qtile, 2048, r0, 64, c0, [(1, c1 - c0)]),
                              lambda kt, g=g: V(self.VS, 2080, 0, 128, kt * 130 + g * 65, [(1, 65)]),
                              65, SC8, extras, ob, ["VS"], post=post)
            self.flush()
            if stage >= 8:
                self.outproj(l, 2 + g)


def _rel_bucket(dist):
    n = np.maximum(dist, 0)
    max_exact = 16
    n_f = np.maximum(n, max_exact).astype(np.float32)
    large = max_exact + (np.log(n_f / np.float32(max_exact)) / np.float32(np.log(128 / 16))
                         * np.float32(16)).astype(np.int32)
    return np.where(n < max_exact, n, np.minimum(large, 31))


def make_consts():
    c = {}
    k = np.arange(128)[:, None]
    q = np.arange(128)[None, :]
    E = np.zeros((128, 2, 32, 128), np.float32)
    for di in range(2):
        dist = di * 128 + q - k
        bk = _rel_bucket(dist)
        for b in range(32):
            E[:, di, b, :] = ((bk == b) & (dist >= 0)).astype(np.float32)
    c["E_nonzero"] = [[bool(E[:, di, b, :].any()) for b in range(32)] for di in range(2)]
    c["c_E"] = E.reshape(128, -1)
    c["c_ident"] = np.eye(128, dtype=np.float32)
    c["c_caus"] = np.where(q >= k, 0.0, NEG).astype(np.float32)
    c["c_wedge"] = np.where(q < k, 0.0, NEG).astype(np.float32)
    n = np.arange(128)[:, None]
    qq = np.arange(S)[None, :]
    mc = np.where(16 * n + 31 <= qq, 0.0, NEG).astype(np.float32)
    mc[127, :] = 0.0
    c["c_mcmp"] = mc
    kk = np.arange(S)[None, :]
    c["c_bim"] = (kk // 256 == np.arange(8)[:, None]).astype(np.float32)
    c["c_bin"] = (kk // 64 == np.arange(32)[:, None]).astype(np.float32)
    p = np.arange(128)[:, None, None]
    J = np.arange(16)[None, :, None]
    nn = np.arange(8)[None, None, :]
    own = (J * 128 + p) // 256
    A = (nn < own).astype(np.float32)
    B = np.where(nn < own, 0.0, -1e9 - 1e6 * nn).astype(np.float32)
    c["c_mA"] = A.reshape(128, 128)
    c["c_mB"] = np.broadcast_to(B, (128, 16, 8)).reshape(128, 128).copy()
    c["c_mN"] = (NEG * A).reshape(128, 128).astype(np.float32)
    jj = np.arange(32)[None, None, :]
    cur = (J * 128 + p) // 64
    valid = jj <= cur
    forced = (jj == 0) | (jj > cur - 2)
    A = (valid & ~forced).astype(np.float32)
    B = np.where(valid & forced, 1e4 + jj, np.where(~valid, -1e9 - 1e6 * jj, 0.0)).astype(np.float32)
    c["c_nA"] = A.reshape(128, 512)
    c["c_nB"] = np.broadcast_to(B, (128, 16, 32)).reshape(128, 512).copy()
    n_cmp = 127
    cs = np.arange(n_cmp) * 16
    ss = np.arange(32) * 64
    ov = np.clip(np.minimum(cs[:, None] + 32, ss[None, :] + 64) - np.maximum(cs[:, None], ss[None, :]),
                 0, None) / 32
    ovp = np.zeros((128, 32), np.float32)
    ovp[:127] = ov
    c["c_ov"] = ovp
    return c


_CONSTS = None


def _consts():
    global _CONSTS
    if _CONSTS is None:
        _CONSTS = make_consts()
    return _CONSTS


def _lay_gu(g, u):
    L = g.shape[0]
    a = np.stack([g, u], axis=1)
    a = a.reshape(L, 2, KC, 128, NFC, 128)
    a = a.transpose(0, 4, 3, 1, 2, 5)
    return np.ascontiguousarray(a).reshape(L, NFC, 128, 2 * KC * 128)


def _lay_wd(w):
    L = w.shape[0]
    a = w.reshape(L, NFC, 128, KC, 128)
    a = a.transpose(0, 3, 2, 1, 4)
    return np.ascontiguousarray(a).reshape(L, KC, 128, NFC * 128)


def _lay_gains(vecs):
    cols = [v.reshape(KC, 128).T for v in vecs]
    return np.ascontiguousarray(np.concatenate(cols, axis=1)).astype(np.float32)


def _win_tiles():
    tiles = []
    for t in range(2):
        tiles.append(list(range(0 + 128 * t, 128 * (t + 1))))
    for t in range(2):
        tiles.append(list(range(256 + 128 * t, 256 + 128 * (t + 1))))
    for base in (768, 1024):
        for t in range(3):
            cols = []
            for slot in range(4):
                gi = t * 3 + slot
                if slot < 3 and gi < 8:
                    cols += list(range(base + gi * 32, base + gi * 32 + 32))
                else:
                    cols += [-1] * 32
            tiles.append(cols)
    for t in range(4):
        tiles.append(list(range(1536 + 128 * t, 1536 + 128 * (t + 1))))
    for base in (2304, 2560):
        for g in range(2):
            cc = list(range(base + 64 * g, base + 64 * g + 64))
            tiles.append(cc + cc)
    tiles.append(list(range(2048, 2176)))
    tiles.append(list(range(2176, 2304)))
    return tiles


def _lay_cols(w, cols):
    L = w.shape[0]
    idx = np.array(cols)
    sel = w[:, :, np.where(idx < 0, 0, idx)].copy()
    if (idx < 0).any():
        sel[:, :, idx < 0] = 0.0
    a = sel.reshape(L, KC, 128, len(cols)).transpose(0, 2, 1, 3)
    return np.ascontiguousarray(a)


def prep_inputs(p, L, layer0=0):
    sl = slice(layer0, layer0 + L)
    c = _consts()
    m = {}
    gl = []
    for l in range(layer0, layer0 + L):
        gl += [p["norm_ffn1"][l], p["norm_mix"][l], p["norm_ffn2"][l]]
    gl.append(p["final_norm"])
    m["gains"] = _lay_gains(gl)
    m["gu1"] = _lay_gu(p["ffn1_gate"][sl], p["ffn1_up"][sl])
    m["gu2"] = _lay_gu(p["ffn2_gate"][sl], p["ffn2_up"][sl])
    m["wd1"] = _lay_wd(p["ffn1_down"][sl])
    m["wd2"] = _lay_wd(p["ffn2_down"][sl])
    w_in = p["w_in"][sl]
    tiles = _win_tiles()
    m["winF"] = np.ascontiguousarray(
        np.stack([_lay_cols(w_in, t).reshape(L, 128, KC * 128) for t in tiles], axis=1))
    m["winTm"] = _lay_cols(w_in, list(range(512, 768))).reshape(L, 128, KC * 256)
    m["winTd"] = _lay_cols(w_in, list(range(1280, 1536))).reshape(L, 128, KC * 256)
    ncols = list(range(2432, 2560)) + list(range(2688, 2816)) + list(range(2816, 2840)) + [-1] * 40
    m["winTn"] = _lay_cols(w_in, ncols).reshape(L, 128, KC * 320)
    wo = p["w_out"][sl].reshape(L, 4, 2, 128, 1024).transpose(0, 1, 3, 2, 4)
    m["woutp"] = np.ascontiguousarray(wo).reshape(L, 4, 128, 2048)
    w1 = p["nsa_cmp_w1"][sl].reshape(L, 2, 8, 4, 64, 256).transpose(0, 1, 2, 4, 3, 5)
    m["w1p"] = np.ascontiguousarray(w1).reshape(L, 2, 8, 64, 1024)
    w2 = p["nsa_cmp_w2"][sl]
    w2k = w2[:, 0].reshape(L, 2, 128, 64).transpose(0, 2, 1, 3)
    w2k = np.concatenate([w2k, w2k], axis=3)
    m["w2k"] = np.ascontiguousarray(w2k).reshape(L, 128, 256)
    w2v = w2[:, 1].reshape(L, 2, 128, 64).transpose(0, 2, 1, 3)
    m["w2v"] = np.ascontiguousarray(w2v).reshape(L, 128, 128)
    pe = p["nsa_cmp_pe"][sl]
    m["peT"] = np.ascontiguousarray(pe.transpose(0, 3, 1, 2)).reshape(L, 64, 64)
    dl = p["diff_lambda"][sl].reshape(L, 1, 128)
    m["dlrep"] = np.ascontiguousarray(np.broadcast_to(dl, (L, 128, 128)))
    sub = p["diff_subln"][sl].reshape(L, 1, 64)
    m["subrep"] = np.ascontiguousarray(np.broadcast_to(sub, (L, 128, 64)))
    m["tblrep"] = np.ascontiguousarray(np.broadcast_to(p["rel_bias"].reshape(1, 512), (128, 512)))
    for k_, v_ in c.items():
        if k_.startswith("c_"):
            m[k_] = v_
    return {k_: np.ascontiguousarray(v_, dtype=np.float32) for k_, v_ in m.items()}


_PROG_CACHE = {}


def _get_prog(n_layers, n_seq, do_final, parts, branches, layer0):
    key = (n_layers, n_seq, do_final, tuple(parts), tuple(branches), layer0)
    if key not in _PROG_CACHE:
        b = Builder(n_layers, n_seq, do_final, parts, branches, layer0)
        _PROG_CACHE[key] = b.build(_consts())
    return _PROG_CACHE[key]


def run_layers(x, p, n_layers, n_seq_per_core, do_final=True, parts=("ffn1", "mix", "ffn2"),
               branches=("moba", "diff", "nsa"), layer0=0, core_ids=None):
    L = n_layers
    nc, cnt = _get_prog(L, n_seq_per_core, do_final, parts, branches, layer0)
    B = x.shape[0]
    ncores = B // n_seq_per_core
    xT = np.ascontiguousarray(np.transpose(x, (0, 2, 1)))
    shared = prep_inputs(p, L, layer0)
    in_maps = []
    for c in range(ncores):
        m = dict(shared)
        m["xT"] = xT[c * n_seq_per_core:(c + 1) * n_seq_per_core]
        in_maps.append(m)
    res = run_bass_kernel_spmd(nc, in_maps, core_ids=list(range(ncores)) if core_ids is None else core_ids)
    o = np.concatenate([r["outT"] for r in res.results], axis=0)
    return np.ascontiguousarray(np.transpose(o, (0, 2, 1)))


def kernel(**inputs):
    x = np.asarray(inputs["x"], dtype=np.float32)
    p = {k: np.asarray(v, dtype=np.float32) for k, v in inputs.items() if k != "x"}
    return run_layers(x, p, DEPTH, x.shape[0] // NCORES)
```

```python
import math
import os
import numpy as np
import concourse.bass as bass
import concourse.mybir as mybir
from concourse.bass_utils import run_bass_kernel_spmd
from contextlib import ExitStack

F32 = mybir.dt.float32
BF16 = mybir.dt.bfloat16
AF = mybir.ActivationFunctionType
ALU = mybir.AluOpType
AX = mybir.AxisListType

S = 2048
D = 1024
KC = 8
FF = 2816
NFC = 22
DEPTH = 4
NCORES = 8
RMS_EPS = 1e-6
NEG = -30000.0


SAME_ENG_NOSYNC = ("pe",)


class _Op:
    __slots__ = ("eng", "fn", "reads", "writes", "isdma", "key", "n", "deps",
                 "signal", "val", "waits")


class Sched:
    ENGS = ("pe", "act", "dve", "pool", "sp")

    def __init__(self):
        self.ops = []
        self.per_eng = {e: [] for e in self.ENGS}
        self.last_w = {}
        self.readers = {}
        self.dma_count = {}
        self.pending_barrier = {e: None for e in self.ENGS}
        self.last_op = {e: None for e in self.ENGS}
        self.live_dma = []

    def _mk(self, eng, fn, reads, writes, isdma, key):
        op = _Op()
        op.eng = eng
        op.fn = fn
        op.reads = tuple(reads)
        op.writes = tuple(writes)
        op.isdma = isdma
        op.key = key
        op.deps = []
        op.signal = False
        op.val = 0
        op.waits = []
        deps = set()
        for t in op.reads:
            w = self.last_w.get(t)
            if w is not None:
                deps.add((w, "raw"))
        for t in op.writes:
            w = self.last_w.get(t)
            if w is not None:
                deps.add((w, "waw"))
            for r in self.readers.get(t, ()):
                deps.add((r, "war"))
        pb = self.pending_barrier[eng]
        if pb is not None:
            for o in pb:
                deps.add((o, "raw"))
            self.pending_barrier[eng] = None
        for (p, kind) in deps:
            if p is op:
                continue
            if (not p.isdma) and (not isdma) and p.eng == eng:
                if kind != "raw" or eng in SAME_ENG_NOSYNC:
                    continue
            op.deps.append(p)
        for t in op.writes:
            self.last_w[t] = op
            self.readers[t] = []
        for t in op.reads:
            self.readers.setdefault(t, []).append(op)
        if isdma:
            c = self.dma_count.get(key, 0) + 1
            self.dma_count[key] = c
            op.val = 16 * c
            op.signal = True
            self.live_dma.append(op)
        self.ops.append(op)
        self.per_eng[eng].append(op)
        if not isdma:
            self.last_op[eng] = op
        return op

    def add(self, eng, fn, reads=(), writes=()):
        return self._mk(eng, fn, reads, writes, False, None)

    def dma(self, queue, fn, reads=(), writes=(), key=None):
        assert key is not None
        return self._mk(queue, fn, reads, writes, True, key)

    def barrier(self):
        lst = [o for o in self.last_op.values() if o is not None]
        lst += self.live_dma
        self.live_dma = []
        for e in self.ENGS:
            prev = self.pending_barrier[e]
            self.pending_barrier[e] = list(lst) + (prev or [])

    def final_wait(self, eng, tokens):
        return self._mk(eng, None, tokens, (), False, None)

    def analyse(self):
        seqno = {}
        cnt = {e: 0 for e in self.ENGS}
        for op in self.ops:
            if not op.isdma:
                cnt[op.eng] += 1
                seqno[id(op)] = cnt[op.eng]
        seen = {e: {} for e in self.ENGS}
        need = []
        for op in self.ops:
            sd = seen[op.eng]
            best = {}
            for p in op.deps:
                if p.isdma:
                    k = ("d", p.key)
                    v = p.val
                else:
                    k = ("e", p.eng)
                    v = seqno[id(p)]
                if sd.get(k, 0) >= v:
                    continue
                if k not in best or best[k][0] < v:
                    best[k] = (v, p)
            lst = []
            for k, (v, p) in best.items():
                sd[k] = v
                p.signal = True
                lst.append(p)
            need.append(lst)
        cnt = {e: 0 for e in self.ENGS}
        for op in self.ops:
            if (not op.isdma) and op.signal:
                cnt[op.eng] += 1
                op.val = cnt[op.eng]
        for op, lst in zip(self.ops, need):
            op.waits = [(("d", p.key) if p.isdma else ("e", p.eng), p.val) for p in lst]
        return cnt

    def emit(self, nc, stack):
        cnt = self.analyse()
        sems = {}
        for e in self.ENGS:
            sems[("e", e)] = stack.enter_context(nc.semaphore("s_" + e))
        for k in self.dma_count:
            sems[("d", k)] = stack.enter_context(nc.semaphore("d_" + str(k)))
        block = stack.enter_context(nc.Block())

        def run(engname):
            def body(eng):
                for op in self.per_eng[engname]:
                    for (k, v) in op.waits:
                        eng.wait_ge(sems[k], v)
                    if op.fn is None:
                        continue
                    inst = op.fn(eng)
                    if op.isdma:
                        inst.then_inc(sems[("d", op.key)], 16)
                    elif op.signal:
                        inst.then_inc(sems[("e", engname)], 1)
            return body

        block.tensor(run("pe"))
        block.scalar(run("act"))
        block.vector(run("dve"))
        block.gpsimd(run("pool"))
        block.sync(run("sp"))
        return cnt


SB_BASE = 16512
SB_END = 229376

SC8 = 0.125
SCD = 32.0 ** -0.5
GELU_C = 1.5957691216057308


def _nbytes(dt):
    return 4 if dt == F32 else 2


def V(t, ftot, p0, npart, f0, dims):
    return bass.AP(t, p0 * ftot + f0, [[ftot, npart]] + [[s, c] for (s, c) in dims])


class Builder:
    def __init__(self, n_layers, n_seq, do_final=True, parts=("ffn1", "mix", "ffn2"),
                 branches=("moba", "diff", "nsa"), layer0=0):
        self.L = n_layers
        self.NS = n_seq
        self.do_final = do_final
        self.parts = parts
        self.branches = branches
        self.layer0 = layer0
        self.nc = bass.Bass("TRN2", target_bir_lowering=False)
        self.sc = Sched()
        self.stack = ExitStack()
        self.off = SB_BASE
        self.nalloc = 0
        self.rr = {}
        self.pq = []
        self.pbank_all = True

    def din(self, name, shape, dt=F32):
        return self.nc.dram_tensor(name, list(shape), dt, kind="ExternalInput").ap()

    def dout(self, name, shape, dt=F32):
        return self.nc.dram_tensor(name, list(shape), dt, kind="ExternalOutput").ap()

    def sb(self, name, shape, dt):
        n = 1
        for s_ in shape[1:]:
            n *= s_
        nb = (n * _nbytes(dt) + 63) // 64 * 64
        off = self.off
        self.off += nb
        assert self.off <= SB_END, ("SBUF overflow", name, self.off)
        self.nalloc += 1
        return self.nc.alloc_sbuf_tensor_at("%s_%d" % (name, self.nalloc), list(shape), dt, offset=off)

    def ps(self, name, shape, dt=F32):
        return self.stack.enter_context(self.nc.psum_tensor(name, list(shape), dt))

    def rot(self, name, n):
        i = self.rr.get(name, 0)
        self.rr[name] = i + 1
        return i % n

    def mm(self, out, lhsT, rhs, start, stop, reads, writes):
        self.sc.add("pe", lambda e: e.matmul(out, lhsT, rhs, start=start, stop=stop,
                                             skip_group_check=True), reads, writes)

    def actf(self, out, in_, func, reads, writes, scale=1.0, bias=0.0):
        self.sc.add("act", lambda e: e.activation(out=out, in_=in_, func=func, bias=bias, scale=scale),
                    reads, writes)

    def tt(self, out, in0, in1, op, reads, writes, eng="dve"):
        self.sc.add(eng, lambda e: e.tensor_tensor(out=out, in0=in0, in1=in1, op=op), reads, writes)

    def ts(self, out, in0, s1, s2, op0, op1, reads, writes, eng="dve"):
        if op1 is None:
            self.sc.add(eng, lambda e: e.tensor_scalar(out=out, in0=in0, scalar1=s1, scalar2=None, op0=op0),
                        reads, writes)
        else:
            self.sc.add(eng, lambda e: e.tensor_scalar(out=out, in0=in0, scalar1=s1, scalar2=s2,
                                                        op0=op0, op1=op1), reads, writes)

    def stt(self, out, in0, scalar, in1, op0, op1, reads, writes, eng="dve"):
        self.sc.add(eng, lambda e: e.scalar_tensor_tensor(out=out, in0=in0, scalar=scalar, in1=in1,
                                                           op0=op0, op1=op1), reads, writes)

    def cp(self, out, in_, reads, writes, eng="dve"):
        if eng == "act":
            self.actf(out, in_, AF.Copy, reads, writes)
        else:
            self.sc.add(eng, lambda e: e.tensor_copy(out=out, in_=in_), reads, writes)

    def recip(self, out, in_, reads, writes):
        self.sc.add("dve", lambda e: e.reciprocal(out=out, in_=in_), reads, writes)

    def red(self, out, in_, op, reads, writes):
        self.sc.add("dve", lambda e: e.tensor_reduce(out=out, in_=in_, axis=AX.X, op=op), reads, writes)

    def memset(self, ap, val, writes, eng="dve"):
        self.sc.add(eng, lambda e: e.memset(ap, val), (), writes)

    def dmaq(self, queue, out, in_, reads, writes, key):
        self.sc.dma(queue, lambda e: e.dma_start(out=out, in_=in_), reads, writes, key)

    def declare(self):
        L, NS = self.L, self.NS
        d = self.din
        self.xT_d = d("xT", [NS, D, S])
        self.out_d = self.dout("outT", [NS, D, S])
        self.gains_d = d("gains", [128, (3 * L + 1) * KC])
        self.gu_d = [d("gu%d" % i, [L, NFC, 128, 2 * KC * 128]) for i in (1, 2)]
        self.wd_d = [d("wd%d" % i, [L, KC, 128, NFC * 128]) for i in (1, 2)]
        self.winF_d = d("winF", [L, 20, 128, KC * 128])
        self.winTm_d = d("winTm", [L, 128, KC * 256])
        self.winTd_d = d("winTd", [L, 128, KC * 256])
        self.winTn_d = d("winTn", [L, 128, KC * 320])
        self.wout_d = d("woutp", [L, 4, 128, 2 * 1024])
        self.w1_d = d("w1p", [L, 2, 8, 64, 4 * 256])
        self.w2k_d = d("w2k", [L, 128, 2 * 128])
        self.w2v_d = d("w2v", [L, 128, 2 * 64])
        self.peT_d = d("peT", [L, 64, 2 * 32])
        self.dl_d = d("dlrep", [L, 128, 128])
        self.sub_d = d("subrep", [L, 128, 64])
        self.tbl_d = d("tblrep", [128, 512])
        self.cE_d = d("c_E", [128, 2 * 32 * 128])
        self.cid_d = d("c_ident", [128, 128])
        self.ccaus_d = d("c_caus", [128, 128])
        self.cwedge_d = d("c_wedge", [128, 128])
        self.cmcmp_d = d("c_mcmp", [128, 2048])
        self.cbim_d = d("c_bim", [128, 2048])
        self.cbin_d = d("c_bin", [128, 2048])
        self.cmA_d = d("c_mA", [128, 128])
        self.cmB_d = d("c_mB", [128, 128])
        self.cmN_d = d("c_mN", [128, 128])
        self.cnA_d = d("c_nA", [128, 512])
        self.cnB_d = d("c_nB", [128, 512])
        self.cov_d = d("c_ov", [128, 32])

        sb = self.sb
        self.xT = sb("xT_s", [128, KC, S], F32)
        self.hT = sb("hT_s", [128, KC, S], BF16)
        self.Tb = sb("Tb", [128, 16 * 2 * 128], BF16)
        self.ident = sb("ident", [128, 128], BF16)
        self.ones_bf = sb("ones_bf", [128, 128], BF16)
        self.gains = sb("gains_s", [128, (3 * L + 1) * KC], F32)
        self.lam = sb("lam", [128, 8], F32)
        self.arena0 = self.off

        self.off = self.arena0
        self.E_s = sb("E_s", [128, 2 * 32 * 128], BF16)
        self.tbl_s = sb("tbl_s", [128, 512], F32)
        self.Tacc = sb("Tacc", [128, 2 * 16 * 128], F32)
        self.Ttmp = sb("Ttmp", [128, 16 * 128], F32)
        self.caus_s = sb("caus_s", [128, 128], F32)

        self.off = self.arena0
        self.act_s = sb("act_s", [128, NFC, 1024], BF16)
        self.wgu = [sb("wgu_s%d" % i, [128, 2 * KC * 128], BF16) for i in range(3)]
        self.wd = [sb("wd_s%d" % i, [128, NFC * 128], BF16) for i in range(2)]
        self.sg = [sb("sg%d" % i, [128, 512], F32) for i in range(2)]
        self.sq = [sb("sq%d" % i, [128, 512], BF16) for i in range(2)]
        self.rstd = [sb("rstd%d" % i, [128, 512], F32) for i in range(2)]
        self.ffn_end = self.off

        self.off = self.arena0
        self.PT = [sb("PT%d" % i, [128, 512], BF16) for i in range(4)]
        self.small = sb("small", [128, 256], F32)
        self.maskT = [sb("maskT%d" % i, [128, 512], BF16) for i in range(2)]
        self.mb = sb("mb", [128, 128], BF16)
        self.cmpbuf = sb("cmpbuf", [128, 1024], BF16)
        self.winbuf = [sb("winbuf%d" % i, [128, KC * 128], BF16) for i in range(2)]
        u0 = self.off
        self.wT = sb("wT", [128, KC * 512], BF16)
        e1 = self.off
        self.off = u0
        self.w1buf = [sb("w1buf%d" % i, [64, 4 * 256], BF16) for i in range(2)]
        self.HT = sb("HT", [128, 2 * 2 * 128], BF16)
        self.w2k_s = sb("w2k_s", [128, 2 * 128], BF16)
        self.w2v_s = sb("w2v_s", [128, 2 * 64], BF16)
        self.peT_s = sb("peT_s", [64, 64], BF16)
        self.b1_s = sb("b1_s", [128, 2], F32)
        self.gx = [sb("gx%d" % i, [128, 128], F32) for i in range(3)]
        assert self.off <= e1
        self.off = e1
        k0 = self.off
        self.kcT = [sb("kcT%d" % i, [64, S], BF16) for i in range(4)]
        e2 = self.off
        self.off = k0
        self.sq_m = [sb("sqm%d" % i, [128, 512], BF16) for i in range(2)]
        self.rstd_m = [sb("rstdm%d" % i, [128, 512], F32) for i in range(2)]
        assert self.off <= e2
        self.off = u0
        self.otok = sb("otok", [128, 16 * 256], BF16)
        self.oT = sb("oT", [128, 2 * S], BF16)
        self.wout_s = sb("wout_s", [128, 2 * 1024], BF16)
        self.tmp = [sb("tmp%d" % i, [128, 256], F32) for i in range(4)]
        self.ocomb = sb("ocomb", [128, 4 * 256], F32)
        e3 = self.off
        self.off = max(e2, e3)
        self.mix0 = self.off

        self.off = self.mix0
        self.pjM = [sb("pjM%d" % i, [128, S], BF16) for i in range(6)]
        self.VM = sb("VM", [128, 16 * 4 * 65], BF16)
        self.bim = sb("bim", [128, 2048], BF16)
        self.mA = sb("mA", [128, 128], F32)
        self.mB = sb("mB", [128, 128], F32)
        self.mN = sb("mN", [128, 128], F32)
        self.kmT = sb("kmT", [128, 4 * 8], BF16)
        self.kmF = sb("kmF", [128, 4 * 8], F32)
        self.maskM = [sb("maskM%d" % i, [128, 512], BF16) for i in range(8)]
        self.moba_end = self.off

        self.off = self.mix0
        self.pjD = [sb("pjD%d" % i, [128, S], BF16) for i in range(11)]
        self.VD = sb("VD", [128, 16 * 4 * 65], BF16)
        self.dl_s = sb("dl_s", [128, 128], F32)
        self.sub_s = sb("sub_s", [128, 64], F32)
        self.diff_end = self.off

        self.off = self.mix0
        self.pjN = [sb("pjN%d" % i, [128, S], BF16) for i in range(6)]
        self.VS = sb("VS", [128, 16 * 2 * 65], BF16)
        self.VW = sb("VW", [128, 16 * 2 * 65], BF16)
        self.gate_s = sb("gate_s", [128, 16 * 24], F32)
        self.mcmp = sb("mcmp", [128, 2048], BF16)
        self.bin_ = sb("bin", [128, 2048], BF16)
        self.nA = sb("nA", [128, 512], F32)
        self.nB = sb("nB", [128, 512], F32)
        self.kcmpT = sb("kcmpT", [128, 2 * 128], BF16)
        self.vcaug = sb("vcaug", [128, 2 * 97], BF16)
        self.wedge = sb("wedge", [128, 128], BF16)
        self.impacc = sb("impacc", [128, 128], F32)
        self.nsa_end = self.off
        self.off = max(self.ffn_end, self.moba_end, self.diff_end, self.nsa_end)
        print("SBUF end", self.off, "of", SB_END, "ffn", self.ffn_end, "moba", self.moba_end,
              "diff", self.diff_end, "nsa", self.nsa_end)

        self.bank = [self.ps("bank%d" % i, [128, 512], F32) for i in range(8)]

    def bankv(self, b, p0, npart, f0, dims):
        return V(self.bank[b], 512, p0, npart, f0, dims)

    def setup(self, consts):
        sc = self.sc
        self.dmaq("sp", self.gains[:], self.gains_d[:, :], [], ["gains"], "gains")
        self.memset(self.ones_bf[:], 1.0, ["ones"])
        self.dmaq("pool", self.ident[:], self.cid_d[:, :], [], ["ident"], "ident")
        self.dmaq("pool", self.E_s[:], self.cE_d[:, :], [], ["E_s"], "E_s")
        self.dmaq("sp", self.tbl_s[:], self.tbl_d[:, :], [], ["tbl_s"], "tbl_s")
        self.dmaq("sp", self.caus_s[:], self.ccaus_d[:, :], [], ["caus_s"], "caus_s")
        tb3 = V(self.tbl_s, 512, 0, 128, 0, [(16, 32), (1, 16)])
        t31 = V(self.tbl_s, 512, 0, 128, 31 * 16, [(0, 32), (1, 16)])
        self.tt(tb3, tb3, t31, ALU.subtract, ["tbl_s"], ["tbl_s"])
        first = {0: True, 1: True}
        for di in range(2):
            for b in range(31):
                if not consts["E_nonzero"][di][b]:
                    continue
                Eb = V(self.E_s, 8192, 0, 128, (di * 32 + b) * 128, [(0, 16), (1, 128)])
                tv = V(self.tbl_s, 512, 0, 128, b * 16, [(1, 16), (0, 128)])
                acc = V(self.Tacc, 4096, 0, 128, di * 2048, [(128, 16), (1, 128)])
                if first[di]:
                    self.tt(acc, Eb, tv, ALU.mult, ["E_s", "tbl_s"], [("Tacc", di)])
                    first[di] = False
                else:
                    tmp = V(self.Ttmp, 2048, 0, 128, 0, [(128, 16), (1, 128)])
                    self.tt(tmp, Eb, tv, ALU.mult, ["E_s", "tbl_s"], ["Ttmp"])
                    self.tt(acc, acc, tmp, ALU.add, [("Tacc", di), "Ttmp"], [("Tacc", di)])
        for h in range(16):
            inv = (1.0 / SCD) if 4 <= h < 8 else (1.0 / SC8)
            for di in range(2):
                src = V(self.Tacc, 4096, 0, 128, di * 2048 + h * 128, [(1, 128)])
                dst = V(self.Tb, 4096, 0, 128, (h * 2 + di) * 128, [(1, 128)])
                if di == 0:
                    self.stt(dst, src, inv, self.caus_s[:], ALU.mult, ALU.add,
                             [("Tacc", di), "caus_s"], ["Tb"])
                else:
                    self.ts(dst, src, inv, None, ALU.mult, None, [("Tacc", di)], ["Tb"])
        sc.barrier()

    def Tbv(self, h, di):
        return V(self.Tb, 4096, 0, 128, (h * 2 + di) * 128, [(1, 128)])

    def load_x(self, s):
        for c in range(KC):
            self.dmaq("sp", self.xT[:, c, :], self.xT_d[s, c * 128:(c + 1) * 128, :],
                      [], [("xT", c, tt) for tt in range(4)], "xload%d" % c)

    def store_out(self, s):
        toks = []
        for c in range(KC):
            self.dmaq("sp", self.out_d[s, c * 128:(c + 1) * 128, :], self.xT[:, c, :],
                      [("xT", c, tt) for tt in range(4)], [("outd", s, c)], "xstore%d" % c)
            toks.append(("outd", s, c))
        return toks

    def norm(self, gidx, final=False, mixer=False):
        sqs = self.sq_m if mixer else self.sq
        rstds = self.rstd_m if mixer else self.rstd
        for tt in range(4):
            tsl = slice(tt * 512, (tt + 1) * 512)
            bi = 6 + self.rot("nbank", 2)
            bank = self.bank[bi]
            btok = ("bank", bi)
            for c in range(KC):
                qi = self.rot("sq", 2)
                sq = sqs[qi]
                self.actf(sq[:], self.xT[:, c, tsl], AF.Square, [("xT", c, tt)], [("sq", qi)])
                self.mm(bank[:], self.ones_bf[:], sq[:], c == 0, c == KC - 1,
                        [("sq", qi), "ones"], [btok])
            ri = self.rot("rstd", 2)
            rstd = rstds[ri]
            self.actf(rstd[:], bank[:], AF.Sqrt, [btok], [("rstd", ri)], scale=1.0 / D, bias=RMS_EPS)
            self.recip(rstd[:], rstd[:], [("rstd", ri)], [("rstd", ri)])
            for c in range(KC):
                gcol = self.gains[:, gidx * KC + c: gidx * KC + c + 1]
                if not final:
                    self.stt(self.hT[:, c, tsl], self.xT[:, c, tsl], gcol, rstd[:], ALU.mult, ALU.mult,
                             [("xT", c, tt), ("rstd", ri), "gains"], [("hT", c, tt)])
                else:
                    self.stt(self.xT[:, c, tsl], self.xT[:, c, tsl], gcol, rstd[:], ALU.mult, ALU.mult,
                             [("xT", c, tt), ("rstd", ri), "gains"], [("xT", c, tt)])

    def ffn(self, l, which):
        gu_d = self.gu_d[which]
        wd_d = self.wd_d[which]
        for half in range(2):
            for fc in range(NFC):
                wi = self.rot("wgu", 3)
                w = self.wgu[wi]
                self.dmaq("pool", w[:], gu_d[l, fc, :, :], [], [("wgu", wi)], "wgu%d" % wi)
                for sub in range(2):
                    tt = half * 2 + sub
                    tsl = slice(tt * 512, (tt + 1) * 512)
                    gb = self.rot("abank", 3) * 2
                    gbank, ubank = self.bank[gb], self.bank[gb + 1]
                    for c in range(KC):
                        self.mm(gbank[:], w[:, c * 128:(c + 1) * 128], self.hT[:, c, tsl],
                                c == 0, c == KC - 1, [("wgu", wi), ("hT", c, tt)], [("bank", gb)])
                    for c in range(KC):
                        self.mm(ubank[:], w[:, (KC + c) * 128:(KC + c + 1) * 128], self.hT[:, c, tsl],
                                c == 0, c == KC - 1, [("wgu", wi), ("hT", c, tt)], [("bank", gb + 1)])
                    si = self.rot("sg", 2)
                    sg = self.sg[si]
                    self.actf(sg[:], gbank[:], AF.Silu, [("bank", gb)], [("sg", si)])
                    self.tt(self.act_s[:, fc, sub * 512:(sub + 1) * 512], ubank[:], sg[:], ALU.mult,
                            [("bank", gb + 1), ("sg", si)], [("act", fc, sub)])
            for dc in range(KC):
                wi = self.rot("wd", 2)
                w = self.wd[wi]
                self.dmaq("pool", w[:], wd_d[l, dc, :, :], [], [("wd", wi)], "wd%d" % wi)
                for sub in range(2):
                    tt = half * 2 + sub
                    tsl = slice(tt * 512, (tt + 1) * 512)
                    yb = 6 + self.rot("nbank", 2)
                    ybank = self.bank[yb]
                    for fc in range(NFC):
                        self.mm(ybank[:], w[:, fc * 128:(fc + 1) * 128],
                                self.act_s[:, fc, sub * 512:(sub + 1) * 512],
                                fc == 0, fc == NFC - 1, [("wd", wi), ("act", fc, sub)], [("bank", yb)])
                    self.stt(self.xT[:, dc, tsl], ybank[:], 0.5, self.xT[:, dc, tsl], ALU.mult, ALU.add,
                             [("bank", yb), ("xT", dc, tt)], [("xT", dc, tt)])

    def proj_F(self, l, tile, subs):
        wi = self.rot("winbuf", 2)
        w = self.winbuf[wi]
        self.dmaq("pool", w[:], self.winF_d[l, tile, :, :], [], [("winbuf", wi)], "winbuf%d" % wi)
        for tt in range(4):
            tsl = slice(tt * 512, (tt + 1) * 512)
            for (c0, c1, dst, dtok) in subs:
                M = c1 - c0
                bi = self.rot("pbank", 6)
                bank = self.bank[bi]
                for c in range(KC):
                    self.mm(bank[0:M, :], w[:, c * 128 + c0: c * 128 + c1], self.hT[:, c, tsl],
                            c == 0, c == KC - 1, [("winbuf", wi), ("hT", c, tt)], [("bank", bi)])
                eng = "act" if self.rot("pev", 2) == 0 else "dve"
                self.cp(dst[0:M, tsl], bank[0:M, :], [("bank", bi)], [dtok + (tt,)], eng=eng)

    def proj_Fs(self, l, tile, evacs):
        wi = self.rot("winbuf", 2)
        w = self.winbuf[wi]
        self.dmaq("pool", w[:], self.winF_d[l, tile, :, :], [], [("winbuf", wi)], "winbuf%d" % wi)
        for tt in range(4):
            tsl = slice(tt * 512, (tt + 1) * 512)
            bi = self.rot("pbank", 6) if self.pbank_all else 6 + self.rot("nbank", 2)
            bank = self.bank[bi]
            for c in range(KC):
                self.mm(bank[:, :], w[:, c * 128:(c + 1) * 128], self.hT[:, c, tsl],
                        c == 0, c == KC - 1, [("winbuf", wi), ("hT", c, tt)], [("bank", bi)])
            eng = "act" if self.rot("pev", 2) == 0 else "dve"
            for (r0, r1, dst, dtok) in evacs:
                self.cp(dst[r0:r1, tsl], bank[r0:r1, :], [("bank", bi)], [dtok + (tt,)], eng=eng)

    def proj_T(self, l, wd_ap, ncols, evac):
        wT = self.wT
        self.dmaq("pool", wT[:, 0:KC * ncols], wd_ap, [], ["wT"], "wT")
        for kt in range(16):
            bi = self.rot("pbank", 6)
            bank = self.bank[bi]
            for c in range(KC):
                self.mm(bank[:, 0:ncols], self.hT[:, c, kt * 128:(kt + 1) * 128],
                        wT[:, c * ncols:(c + 1) * ncols], c == 0, c == KC - 1,
                        ["wT", ("hT", c, kt // 4)], [("bank", bi)])
            evac(kt, bi)

    LOOK = 2

    def flush(self):
        while self.pq:
            self.pq.pop(0)()

    def attn(self, Q, kt_list, kap, qap, vap, nv, scale, extras, ob, otoks,
             kparts=128, extra_reads=(), post=None):
        last = {}
        for (kt, j0, j1) in kt_list:
            for j in range(j0, j1):
                last[j] = kt
        obank = self.bank[ob]
        state = {"first": True}
        n = len(kt_list)
        for idx, (kt, j0, j1) in enumerate(kt_list):
            si = self.rot("sbank", 3)
            sbank = self.bank[si]
            c0, c1 = j0 * 128, j1 * 128
            ex = extras(kt, j0, j1)
            self.mm(sbank[0:kparts, c0:c1], kap(kt), qap(Q * 512 + c0, Q * 512 + c1),
                    True, len(ex) == 0, list(extra_reads), [("bank", si)])
            for i, (e0, e1, l_ap, r_ap, rd) in enumerate(ex):
                self.mm(sbank[0:kparts, e0:e1], l_ap, r_ap, False, i == len(ex) - 1,
                        list(rd), [("bank", si)])
            pi = self.rot("PT", 4)
            pt = self.PT[pi]
            self.actf(pt[0:kparts, c0:c1], sbank[0:kparts, c0:c1], AF.Exp, [("bank", si)], [("pt", pi)],
                      scale=scale)

            def stageB(kt=kt, j0=j0, j1=j1, pt=pt, pi=pi, islast=(idx == n - 1)):
                for j in range(j0, j1):
                    self.mm(obank[:, j * 128: j * 128 + nv], pt[0:kparts, j * 128:(j + 1) * 128], vap(kt),
                            state["first"], last[j] == kt, [("pt", pi)] + list(otoks), [("bank", ob)])
                    state["first"] = False
                if islast and post is not None:
                    post()
            self.pq.append(stageB)
            while len(self.pq) > self.LOOK:
                self.pq.pop(0)()

    def outproj(self, l, grp):
        self.flush()
        self.dmaq("pool", self.wout_s[:], self.wout_d[l, grp, :, :], [], ["wout"], "wout")
        for f in range(2):
            for Qi in range(4):
                bi = 6 + self.rot("nbank", 2)
                bank = self.bank[bi]
                for j in range(4):
                    J = Qi * 4 + j
                    src = V(self.otok, 4096, 0, 128, J * 256 + f * 128, [(1, 128)])
                    self.mm(bank[:, j * 128:(j + 1) * 128], src, self.ident[:], True, True,
                            [("otok", J), "ident"], [("bank", bi)])
                eng = "act" if self.rot("pev", 2) == 0 else "dve"
                dst = V(self.oT, 4096, 0, 128, f * 2048 + Qi * 512, [(1, 512)])
                self.cp(dst, bank[:], [("bank", bi)], [("oT", f, Qi)], eng=eng)
        for dc in range(KC):
            for tt in range(4):
                bi = 6 + self.rot("nbank", 2)
                bank = self.bank[bi]
                for f in range(2):
                    self.mm(bank[:], self.wout_s[:, f * 1024 + dc * 128: f * 1024 + (dc + 1) * 128],
                            V(self.oT, 4096, 0, 128, f * 2048 + tt * 512, [(1, 512)]),
                            f == 0, f == 1, ["wout", ("oT", f, tt)], [("bank", bi)])
                tsl = slice(tt * 512, (tt + 1) * 512)
                self.tt(self.xT[:, dc, tsl], bank[:], self.xT[:, dc, tsl], ALU.add,
                        [("bank", bi), ("xT", dc, tt)], [("xT", dc, tt)])

    def rank_mask(self, score_ap_fn, n, nj, K, mult_tab, mb_out_fn, stok, mtok):
        for j in range(nj):
            a = score_ap_fn(j, [(0, n), (1, n)])
            b = score_ap_fn(j, [(1, n), (0, n)])
            cmpv = V(self.cmpbuf, 1024, 0, 128, 0, [(n, n), (1, n)])
            self.tt(cmpv, a, b, ALU.is_gt, [stok], ["cmpbuf"])
            rk = V(self.small, 256, 0, 128, 192, [(1, n)])
            self.red(rk, cmpv, ALU.add, ["cmpbuf"], ["rank"])
            if callable(mult_tab):
                self.stt(mb_out_fn(j), rk, K - 0.5, mult_tab(j), ALU.is_ge, ALU.mult,
                         ["rank", "tabs"], [mtok])
            else:
                self.ts(mb_out_fn(j), rk, K - 0.5, mult_tab, ALU.is_ge, ALU.mult, ["rank"], [mtok])

    def pass_moba(self, l):
        sc = self.sc
        pj = self.pjM
        for h in range(4):
            self.memset(pj[2 + h][:], 0.0, [("pjM", 2 + h, tt) for tt in range(4)], eng="pool")
        for t in range(2):
            self.proj_F(l, t, [(0, 128, pj[t], ("pjM", t))])
        for t in range(2):
            self.proj_Fs(l, 2 + t, [(0, 64, pj[2 + 2 * t], ("pjM", 2 + 2 * t)),
                                    (64, 128, pj[3 + 2 * t], ("pjM", 3 + 2 * t))])
        self.memset(self.VM[:], 1.0, ["VM"])

        def evac(kt, bi):
            dst = V(self.VM, 4160, 0, 128, kt * 260, [(65, 4), (1, 64)])
            src = V(self.bank[bi], 512, 0, 128, 0, [(64, 4), (1, 64)])
            self.cp(dst, src, [("bank", bi)], ["VM"], eng="act" if kt % 2 else "dve")
        self.proj_T(l, self.winTm_d[l, :, :], 256, evac)
        self.dmaq("pool", self.bim[:], self.cbim_d[:, :], [], ["bim"], "bim")
        for i in range(8):
            self.memset(self.maskM[i][:], 0.0, [("maskM", i)], eng="pool")
        self.dmaq("sp", self.mA[:], self.cmA_d[:, :], [], ["tabs"], "mA")
        self.dmaq("sp", self.mB[:], self.cmB_d[:, :], [], ["tabs"], "mB")
        self.dmaq("sp", self.mN[:], self.cmN_d[:, :], [], ["tabs"], "mN")
        sc.barrier()
        for h in range(4):
            src = V(pj[2 + h], 2048, 0, 128, 0, [(256, 8), (1, 256)])
            dstf = V(self.kmF, 32, 0, 128, h * 8, [(1, 8)])
            self.red(dstf, src, ALU.add, [("pjM", 2 + h, tt) for tt in range(4)], [("kmF", h)])
            dst = V(self.kmT, 32, 0, 128, h * 8, [(1, 8)])
            self.ts(dst, dstf, 1.0 / 256, None, ALU.mult, None, [("kmF", h)], [("kmT", h)])
        mask_of = {}
        for h in range(4):
            r0 = 64 * (h % 2)
            qtile = pj[h // 2]
            qtoks = [("pjM", h // 2, tt) for tt in range(4)]
            for Q in (2, 3):
                gb = 7
                for j in range(4):
                    J = Q * 4 + j
                    self.mm(self.bank[gb][:, j * 8:(j + 1) * 8],
                            V(qtile, 2048, 0, 128, J * 128, [(1, 128)]),
                            V(self.kmT, 32, 0, 128, h * 8, [(1, 8)]), True, True,
                            qtoks + [("kmT", h)], [("bank", gb)])
                scv = V(self.small, 256, 0, 128, 0, [(1, 32)])
                Av = V(self.mA, 128, 0, 128, Q * 32, [(1, 32)])
                Bv = V(self.mB, 128, 0, 128, Q * 32, [(1, 32)])
                self.tt(scv, self.bank[gb][:, 0:32], Av, ALU.mult, [("bank", gb), "tabs"], ["score"])
                self.tt(scv, scv, Bv, ALU.add, ["score", "tabs"], ["score"])
                self.rank_mask(lambda j, dims: V(self.small, 256, 0, 128, j * 8, dims), 8, 4, 3,
                               lambda j, Q=Q: V(self.mN, 128, 0, 128, Q * 32 + j * 8, [(1, 8)]),
                               lambda j: V(self.mb, 128, 0, 128, j * 8, [(1, 8)]), "score", "mb")
                mi = h * 2 + (Q - 2)
                for j in range(4):
                    self.mm(self.bank[gb][0:8, j * 128:(j + 1) * 128],
                            V(self.mb, 128, 0, 128, j * 8, [(1, 8)]), self.ident[:], True, True,
                            ["mb", "ident"], [("bank", gb)])
                self.cp(self.maskM[mi][0:8, :], self.bank[gb][0:8, :], [("bank", gb)], [("maskM", mi)])
                mask_of[(h, Q)] = mi
        for h in range(4):
            r0 = 64 * (h % 2)
            qtile = pj[h // 2]
            ktile = pj[2 + h]
            qtoks = [("pjM", h // 2, tt) for tt in range(4)]
            ktoks = [("pjM", 2 + h, tt) for tt in range(4)]
            for Q in range(4):
                mi = mask_of.get((h, Q))
                ob = 3 + self.rot("obank", 3)

                def extras(kt, j0, j1, Q=Q, h=h, mi=mi):
                    ex = self.toep_extras(h, Q, kt, j0, j1)
                    if mi is not None and (kt // 2) < (4 * Q + 3) // 2:
                        ex.append((j0 * 128, j1 * 128, self.bim[:, kt * 128:(kt + 1) * 128],
                                   self.maskM[mi][:, j0 * 128:j1 * 128], ["bim", ("maskM", mi)]))
                    return ex

                def post(Q=Q, h=h, ob=ob):
                    rbase = 64 + self.rot("rs", 4) * 16
                    rs = V(self.small, 256, 0, 128, rbase, [(1, 4)])
                    self.recip(rs, self.osum(ob), [("bank", ob)], ["rs"])
                    rsb = V(self.small, 256, 0, 128, rbase, [(1, 4), (0, 64)])
                    dst = V(self.otok, 4096, 0, 128, Q * 4 * 256 + h * 64, [(256, 4), (1, 64)])
                    self.tt(dst, self.obv(ob), rsb, ALU.mult, [("bank", ob), "rs"],
                            [("otok", Q * 4 + j) for j in range(4)])
                self.attn(Q, self.causal_kts(Q),
                          lambda kt, ktile=ktile: V(ktile, 2048, 0, 128, kt * 128, [(1, 128)]),
                          lambda c0, c1, qtile=qtile: V(qtile, 2048, 0, 128, c0, [(1, c1 - c0)]),
                          lambda kt, h=h: V(self.VM, 4160, 0, 128, kt * 260 + h * 65, [(1, 65)]),
                          65, SC8, extras, ob, ["VM"], extra_reads=qtoks + ktoks, post=post)
        self.outproj(l, 0)

    def mixer(self, l):
        sc = self.sc
        sc.barrier()
        self.norm(3 * l + 1, mixer=True)
        if "moba" in self.branches:
            sc.barrier()
            self.pass_moba(l)
        if "diff" in self.branches:
            sc.barrier()
            self.pass_diff(l)
        if "nsa" in self.branches:
            sc.barrier()
            self.pass_nsa(l)
        sc.barrier()

    def build(self, consts):
        self.declare()
        self.setup(consts)
        outtoks = []
        for s in range(self.NS):
            self.load_x(s)
            for l in range(self.L):
                if "ffn1" in self.parts:
                    self.norm(3 * l + 0)
                    self.ffn(l, 0)
                if "mix" in self.parts:
                    self.mixer(l)
                if "ffn2" in self.parts:
                    self.norm(3 * l + 2)
                    self.ffn(l, 1)
            if self.do_final:
                self.norm(3 * self.L, final=True)
            outtoks += self.store_out(s)
        self.sc.final_wait("sp", outtoks)
        cnt = self.sc.emit(self.nc, self.stack)
        self.stack.close()
        return self.nc, cnt

    def causal_kts(self, Q):
        return [(kt, max(0, kt - 4 * Q), 4) for kt in range(4 * Q + 4)]

    def toep_extras(self, hb, Q, kt, j0, j1):
        ex = []
        for j in range(j0, j1):
            dlt = 4 * Q + j - kt
            if dlt in (0, 1):
                ex.append((j * 128, (j + 1) * 128, self.ident[:], self.Tbv(hb, dlt), ["ident", "Tb"]))
        return ex

    def obv(self, ob):
        return V(self.bank[ob], 512, 0, 128, 0, [(128, 4), (1, 64)])

    def osum(self, ob):
        return V(self.bank[ob], 512, 0, 128, 64, [(128, 4)])

    def pass_diff(self, l):
        sc = self.sc
        lam_init = 0.8 - 0.6 * math.exp(-0.3 * (l + self.layer0))
        pj = self.pjD
        for gi in range(8):
            self.memset(pj[3 + gi][:], 0.0, [("pjD", 3 + gi, tt) for tt in range(4)], eng="pool")
        for t in range(3):
            self.proj_F(l, 4 + t, [(0, 128, pj[t], ("pjD", t))])
        for t in range(3):
            ev = []
            for slot in range(3):
                gi = 3 * t + slot
                if gi < 8:
                    ev.append((32 * slot, 32 * slot + 32, pj[3 + gi], ("pjD", 3 + gi)))
            self.proj_Fs(l, 7 + t, ev)
        self.memset(self.VD[:], 1.0, ["VD"])

        def evac(kt, bi):
            dst = V(self.VD, 4160, 0, 128, kt * 260, [(65, 4), (1, 64)])
            src = V(self.bank[bi], 512, 0, 128, 0, [(64, 4), (1, 64)])
            self.cp(dst, src, [("bank", bi)], ["VD"], eng="act" if kt % 2 else "dve")
        self.proj_T(l, self.winTd_d[l, :, :], 256, evac)
        self.dmaq("sp", self.dl_s[:], self.dl_d[l, :, :], [], ["dl_s"], "dl_s")
        self.dmaq("sp", self.sub_s[:], self.sub_d[l, :, :], [], ["sub_s"], "sub_s")
        sc.barrier()
        t0 = V(self.tmp[0], 256, 0, 128, 0, [(1, 32)])
        self.tt(t0, self.dl_s[:, 0:32], self.dl_s[:, 32:64], ALU.mult, ["dl_s"], ["tmp0"])
        self.red(self.lam[:, 0:1], t0, ALU.add, ["tmp0"], ["lam"])
        self.tt(t0, self.dl_s[:, 64:96], self.dl_s[:, 96:128], ALU.mult, ["dl_s", "lam"], ["tmp0"])
        self.red(self.lam[:, 1:2], t0, ALU.add, ["tmp0"], ["lam"])
        self.actf(self.lam[:, 2:4], self.lam[:, 0:2], AF.Exp, ["lam"], ["lam"])
        self.tt(self.lam[:, 4:5], self.lam[:, 2:3], self.lam[:, 3:4], ALU.subtract, ["lam"], ["lam"])
        self.ts(self.lam[:, 5:6], self.lam[:, 4:5], lam_init, -1.0, ALU.add, ALU.mult, ["lam"], ["lam"])
        sub_b = V(self.sub_s, 64, 0, 128, 0, [(0, 4), (1, 64)])
        tv = [V(self.tmp[i], 256, 0, 128, 0, [(64, 4), (1, 64)]) for i in range(4)]
        bc = lambda base: V(self.small, 256, 0, 128, base, [(1, 4), (0, 64)])
        for h in range(4):
            for Q in range(4):
                obs = [3 + self.rot("obank", 3), 3 + self.rot("obank", 3)]

                def post(h=h, Q=Q, obs=obs):
                    rb = 64 + self.rot("rs", 4) * 16
                    rs0 = V(self.small, 256, 0, 128, rb, [(1, 4)])
                    rs1 = V(self.small, 256, 0, 128, rb + 4, [(1, 4)])
                    nl = V(self.small, 256, 0, 128, rb + 8, [(1, 4)])
                    ssv = V(self.small, 256, 0, 128, rb + 12, [(1, 4)])
                    self.recip(rs0, self.osum(obs[0]), [("bank", obs[0])], ["rs"])
                    self.recip(rs1, self.osum(obs[1]), [("bank", obs[1])], ["rs"])
                    self.ts(nl, rs1, self.lam[:, 5:6], None, ALU.mult, None, ["rs", "lam"], ["rs"])
                    self.tt(tv[0], self.obv(obs[0]), bc(rb), ALU.mult, [("bank", obs[0]), "rs"], ["tmp0"])
                    self.tt(tv[1], self.obv(obs[1]), bc(rb + 8), ALU.mult, [("bank", obs[1]), "rs"], ["tmp1"])
                    self.tt(tv[0], tv[0], tv[1], ALU.add, ["tmp0", "tmp1"], ["tmp0"])
                    self.tt(tv[2], tv[0], tv[0], ALU.mult, ["tmp0"], ["tmp2"])
                    self.red(ssv, tv[2], ALU.add, ["tmp2"], ["rs"])
                    self.actf(ssv, ssv, AF.Sqrt, ["rs"], ["rs"], scale=1.0 / 64, bias=RMS_EPS)
                    self.recip(ssv, ssv, ["rs"], ["rs"])
                    self.stt(tv[1], tv[0], 1.0 - lam_init, bc(rb + 12), ALU.mult, ALU.mult,
                             ["tmp0", "rs"], ["tmp1"])
                    dst = V(self.otok, 4096, 0, 128, Q * 4 * 256 + h * 64, [(256, 4), (1, 64)])
                    self.tt(dst, tv[1], sub_b, ALU.mult, ["tmp1", "sub_s"],
                            [("otok", Q * 4 + j) for j in range(4)])
                for m in range(2):
                    gi = 2 * h + m
                    r0 = 32 * (gi % 3)
                    qtile = pj[gi // 3]
                    ktile = pj[3 + gi]
                    self.attn(Q, self.causal_kts(Q),
                              lambda kt, ktile=ktile: V(ktile, 2048, 0, 128, kt * 128, [(1, 128)]),
                              lambda c0, c1, qtile=qtile: V(qtile, 2048, 0, 128, c0, [(1, c1 - c0)]),
                              lambda kt, h=h: V(self.VD, 4160, 0, 128, kt * 260 + h * 65, [(1, 65)]),
                              65, SCD, lambda kt, j0, j1, h=h, Q=Q: self.toep_extras(4 + h, Q, kt, j0, j1),
                              obs[m], ["VD"], post=(post if m == 1 else None))
        self.outproj(l, 1)

    def pass_nsa(self, l):
        sc = self.sc
        pj = self.pjN
        self.proj_F(l, 18, [(0, 64, self.kcT[0], ("kcT", 0)), (64, 128, self.kcT[1], ("kcT", 1))])
        self.proj_F(l, 19, [(0, 64, self.kcT[2], ("kcT", 2)), (64, 128, self.kcT[3], ("kcT", 3))])
        self.memset(self.VS[:], 1.0, ["VS"])
        self.memset(self.VW[:], 1.0, ["VW"])

        def evac(kt, bi):
            bank = self.bank[bi]
            dbg = os.environ.get("NSA_DBG", "")
            if "a" in dbg:
                self.cp(self.gate_s[:, kt * 24:(kt + 1) * 24], bank[:, 256:280], [("bank", bi)], ["gate"])
                return
            if "c" in dbg:
                self.cp(V(self.VS, 2080, 0, 128, kt * 130, [(65, 2), (1, 64)]),
                        V(bank, 512, 0, 128, 0, [(64, 2), (1, 64)]), [("bank", bi)], ["VS"], eng="dve")
                self.cp(V(self.VW, 2080, 0, 128, kt * 130, [(65, 2), (1, 64)]),
                        V(bank, 512, 0, 128, 128, [(64, 2), (1, 64)]), [("bank", bi)], ["VW"], eng="dve")
                return
            if "d" in dbg:
                self.actf(self.gate_s[:, kt * 24:(kt + 1) * 24], bank[:, 256:280], AF.Tanh, [("bank", bi)], ["gate"], scale=0.5)
                return
            if "b" in dbg:
                self.cp(V(self.VS, 2080, 0, 128, kt * 130, [(65, 2), (1, 64)]),
                        V(bank, 512, 0, 128, 0, [(64, 2), (1, 64)]), [("bank", bi)], ["VS"], eng="dve")
                return
            self.cp(V(self.VS, 2080, 0, 128, kt * 130, [(65, 2), (1, 64)]),
                    V(bank, 512, 0, 128, 0, [(64, 2), (1, 64)]), [("bank", bi)], ["VS"], eng="dve")
            self.cp(V(self.VW, 2080, 0, 128, kt * 130, [(65, 2), (1, 64)]),
                    V(bank, 512, 0, 128, 128, [(64, 2), (1, 64)]), [("bank", bi)], ["VW"], eng="dve")
            self.cp(self.gate_s[:, kt * 24:(kt + 1) * 24], bank[:, 256:280], [("bank", bi)], ["gate"])
        stage = int(os.environ.get("NSA_STAGE", "99"))
        if stage < 1:
            return
        self.proj_T(l, self.winTn_d[l, :, :], 320, evac)
        self.actf(self.gate_s[:], self.gate_s[:], AF.Tanh, ["gate"], ["gate"], scale=0.5)
        self.ts(self.gate_s[:], self.gate_s[:], 0.5, 0.5, ALU.mult, ALU.add, ["gate"], ["gate"])
        if stage < 2:
            return
        self.dmaq("pool", self.mcmp[:], self.cmcmp_d[:, :], [], ["mcmp"], "mcmp")
        self.dmaq("pool", self.bin_[:], self.cbin_d[:, :], [], ["bin"], "bin")
        for i in range(2):
            self.memset(self.maskT[i][:], 0.0, [("maskT", i)], eng="pool")
        self.dmaq("pool", self.wedge[:], self.cwedge_d[:, :], [], ["wedge"], "wedge")
        self.dmaq("sp", self.nA[:], self.cnA_d[:, :], [], ["tabs"], "nA")
        self.dmaq("sp", self.nB[:], self.cnB_d[:, :], [], ["tabs"], "nB")
        self.memset(self.vcaug[:], 1.0, ["vcaug"])
        for g in range(2):
            self.dmaq("pool", self.vcaug[:, g * 97 + 65: g * 97 + 97], self.cov_d[:, :], ["vcaug"], ["vcaug"],
                      "vcaug")
        sc.barrier()
        if stage < 3:
            return
        self.dmaq("pool", self.w2k_s[:], self.w2k_d[l, :, :], [], ["w2k"], "w2k")
        self.dmaq("pool", self.w2v_s[:], self.w2v_d[l, :, :], [], ["w2v"], "w2v")
        self.dmaq("pool", self.peT_s[:], self.peT_d[l, :, :], [], ["peT"], "peT")
        b1first = True
        for i in range(2):
            ab = i
            afirst = True
            for lg in range(8):
                wi = self.rot("w1buf", 2)
                wb = self.w1buf[wi]
                self.dmaq("pool", wb[:], self.w1_d[l, i, lg, :, :], [], [("w1buf", wi)], "w1buf%d" % wi)
                for ll in range(4):
                    li = lg * 4 + ll
                    for hc in range(2):
                        lhsT = wb[0:64, ll * 256 + hc * 128: ll * 256 + (hc + 1) * 128]
                        for g in range(2):
                            self.mm(self.bank[ab][:, (g * 2 + hc) * 128:(g * 2 + hc) * 128 + 127], lhsT,
                                    V(self.kcT[i * 2 + g], 2048, 0, 64, li, [(16, 127)]),
                                    afirst, li == 31, [("w1buf", wi)], [("bank", ab)])
                            afirst = False
                        self.mm(self.bank[2][:, i * 2 + hc: i * 2 + hc + 1], lhsT,
                                self.peT_s[0:64, i * 32 + li: i * 32 + li + 1],
                                b1first, li == 31, [("w1buf", wi), "peT"], [("bank", 2)])
                        b1first = False
            self.cp(self.b1_s[:, 0:2], self.bank[2][:, i * 2: i * 2 + 2], [("bank", 2)], ["b1"])
            for g in range(2):
                for hc in range(2):
                    acc = self.bank[ab][:, (g * 2 + hc) * 128:(g * 2 + hc) * 128 + 127]
                    xs, x2, sgm = self.gx[0][:, 0:127], self.gx[1][:, 0:127], self.gx[2][:, 0:127]
                    self.actf(xs, acc, AF.Identity, [("bank", ab), "b1"], ["gx0"], bias=self.b1_s[:, hc:hc + 1])
                    self.tt(x2, xs, xs, ALU.mult, ["gx0"], ["gx1"])
                    self.ts(x2, x2, 0.044715, 1.0, ALU.mult, ALU.add, ["gx1"], ["gx1"])
                    self.tt(x2, x2, xs, ALU.mult, ["gx1", "gx0"], ["gx1"])
                    self.actf(sgm, x2, AF.Tanh, ["gx1"], ["gx2"], scale=GELU_C * 0.5)
                    self.ts(sgm, sgm, 0.5, 0.5, ALU.mult, ALU.add, ["gx2"], ["gx2"])
                    hdst = V(self.HT, 512, 0, 128, (g * 2 + hc) * 128, [(1, 127)])
                    self.tt(hdst, xs, sgm, ALU.mult, ["gx0", "gx2"], [("HT", g, hc)])
            for g in range(2):
                if i == 0:
                    for hc in range(2):
                        self.mm(self.bank[3][:, g * 128: g * 128 + 127], self.w2k_s[:, hc * 128:(hc + 1) * 128],
                                V(self.HT, 512, 0, 128, (g * 2 + hc) * 128, [(1, 127)]),
                                g == 0 and hc == 0, hc == 1, ["w2k", ("HT", g, hc)], [("bank", 3)])
                else:
                    for hc in range(2):
                        self.mm(self.bank[4][0:127, g * 64:(g + 1) * 64],
                                V(self.HT, 512, 0, 128, (g * 2 + hc) * 128, [(1, 127)]),
                                self.w2v_s[:, hc * 64:(hc + 1) * 64],
                                g == 0 and hc == 0, hc == 1, ["w2v", ("HT", g, hc)], [("bank", 4)])
            if i == 0:
                for g in range(2):
                    self.cp(self.kcmpT[:, g * 128: g * 128 + 127], self.bank[3][:, g * 128: g * 128 + 127],
                            [("bank", 3)], ["kcmpT"])
            else:
                for g in range(2):
                    self.cp(self.vcaug[0:127, g * 97: g * 97 + 64], self.bank[4][0:127, g * 64:(g + 1) * 64],
                            [("bank", 4)], ["vcaug"])
        sc.barrier()
        if stage < 4:
            return
        bc64 = lambda base: V(self.small, 256, 0, 128, base, [(1, 4), (0, 64)])
        bc32 = lambda base: V(self.small, 256, 0, 128, base, [(1, 4), (0, 32)])
        tv = [V(self.tmp[i], 256, 0, 128, 0, [(64, 4), (1, 64)]) for i in range(4)]
        for g in range(2):
            self.pbank_all = False
            for t in range(2, 6):
                self.memset(pj[t][:], 0.0, [("pjN", t, tt) for tt in range(4)], eng="pool")
            for t in range(2):
                self.proj_Fs(l, 10 + 2 * g + t, [(0, 128, pj[t], ("pjN", t))])
            self.proj_Fs(l, 14 + g, [(0, 64, pj[2], ("pjN", 2)), (64, 128, pj[3], ("pjN", 3))])
            self.proj_Fs(l, 16 + g, [(0, 64, pj[4], ("pjN", 4)), (64, 128, pj[5], ("pjN", 5))])
            self.pbank_all = True
            alltoks = [("pjN", t, tt) for t in range(6) for tt in range(4)]
            for Q in range(4):
                for r in range(4):
                    h = 4 * g + r
                    r0 = 64 * (h % 2)
                    qtile = pj[r // 2]
                    ob = 3 + self.rot("obank", 3)

                    def post(r=r, h=h, ob=ob, Q=Q):
                        rb = 64 + self.rot("rs", 4) * 16
                        rs = V(self.small, 256, 0, 128, rb, [(1, 4)])
                        fv = V(self.small, 256, 0, 128, rb + 4, [(1, 4)])
                        self.ts(rs, self.osum(ob), 1e-30, None, ALU.add, None, [("bank", ob)], ["rs"])
                        self.recip(rs, rs, ["rs"], ["rs"])
                        impv = V(self.bank[ob], 512, 0, 128, 65, [(128, 4), (1, 32)])
                        iacc = V(self.impacc, 128, 0, 128, 0, [(32, 4), (1, 32)])
                        if r == 0:
                            self.tt(iacc, impv, bc32(rb), ALU.mult, [("bank", ob), "rs"], ["impacc"])
                        else:
                            itmp = V(self.tmp[3], 256, 0, 128, 0, [(32, 4), (1, 32)])
                            self.tt(itmp, impv, bc32(rb), ALU.mult, [("bank", ob), "rs"], ["tmp3"])
                            self.tt(iacc, iacc, itmp, ALU.add, ["impacc", "tmp3"], ["impacc"])
                        gv = V(self.gate_s, 384, 0, 128, Q * 96 + h * 3 + 0, [(24, 4)])
                        self.tt(fv, rs, gv, ALU.mult, ["rs", "gate"], ["rs"])
                        oc = V(self.ocomb, 1024, 0, 128, r * 256, [(64, 4), (1, 64)])
                        self.tt(oc, self.obv(ob), bc64(rb + 4), ALU.mult, [("bank", ob), "rs"], [("ocomb", r)])
                    self.attn(Q, [(0, 0, 4)],
                              lambda kt, r0=r0, g=g: V(self.kcmpT, 256, r0, 64, g * 128, [(1, 127)]),
                              lambda c0, c1, qtile=qtile, r0=r0: V(qtile, 2048, r0, 64, c0, [(1, c1 - c0)]),
                              lambda kt, g=g: V(self.vcaug, 194, 0, 127, g * 97, [(1, 97)]),
                              97, SC8,
                              lambda kt, j0, j1, Q=Q: [(0, 512, self.ident[0:127, 0:127],
                                                        self.mcmp[0:127, Q * 512:(Q + 1) * 512],
                                                        ["ident", "mcmp"])],
                              ob, ["vcaug"], kparts=127, extra_reads=["kcmpT"] + alltoks, post=post)
                if stage < 5:
                    continue
                mi = None
                if Q >= 2:
                    self.flush()
                    scv = V(self.impacc, 128, 0, 128, 0, [(1, 128)])
                    self.tt(scv, scv, self.nA[:, Q * 128:(Q + 1) * 128], ALU.mult, ["impacc", "tabs"], ["impacc"])
                    self.tt(scv, scv, self.nB[:, Q * 128:(Q + 1) * 128], ALU.add, ["impacc", "tabs"], ["impacc"])
                    self.rank_mask(lambda j, dims: V(self.impacc, 128, 0, 128, j * 32, dims), 32, 4, 16,
                                   NEG, lambda j: V(self.mb, 128, 0, 128, j * 32, [(1, 32)]), "impacc", "mb")
                    mi = self.rot("maskT", 2)
                if stage < 6:
                    continue
                for r in range(4):
                    h = 4 * g + r
                    r0 = 64 * (h % 2)
                    qtile = pj[r // 2]
                    ktile = pj[4 + (r % 2)]
                    ob = 3 + self.rot("obank", 3)
                    kts = []
                    for kt in range(max(0, 4 * Q - 4), 4 * Q + 4):
                        j0 = max(0, kt - 4 * Q)
                        j1 = min(4, kt - 4 * Q + 5)
                        kts.append((kt, j0, j1))

                    def extras(kt, j0, j1, h=h, Q=Q):
                        ex = self.toep_extras(8 + h, Q, kt, j0, j1)
                        for j in range(j0, j1):
                            if 4 * Q + j - kt == 4:
                                ex.append((j * 128, (j + 1) * 128, self.ident[:], self.wedge[:],
                                           ["ident", "wedge"]))
                        return ex

                    def post(r=r, h=h, ob=ob, Q=Q):
                        rb = 64 + self.rot("rs", 4) * 16
                        rs = V(self.small, 256, 0, 128, rb, [(1, 4)])
                        fv = V(self.small, 256, 0, 128, rb + 4, [(1, 4)])
                        self.recip(rs, self.osum(ob), [("bank", ob)], ["rs"])
                        gv = V(self.gate_s, 384, 0, 128, Q * 96 + h * 3 + 2, [(24, 4)])
                        self.tt(fv, rs, gv, ALU.mult, ["rs", "gate"], ["rs"])
                        oc = V(self.ocomb, 1024, 0, 128, r * 256, [(64, 4), (1, 64)])
                        self.tt(tv[1], self.obv(ob), bc64(rb + 4), ALU.mult, [("bank", ob), "rs"], ["tmp1"])
                        self.tt(oc, oc, tv[1], ALU.add, [("ocomb", r), "tmp1"], [("ocomb", r)])
                    self.attn(Q, kts,
                              lambda kt, ktile=ktile: V(ktile, 2048, 0, 128, kt * 128, [(1, 128)]),
                              lambda c0, c1, qtile=qtile: V(qtile, 2048, 0, 128, c0, [(1, c1 - c0)]),
                              lambda kt, g=g: V(self.VW, 2080, 0, 128, kt * 130 + g * 65, [(1, 65)]),
                              65, SC8, extras, ob, ["VW"], extra_reads=alltoks, post=post)
                if stage < 7:
                    continue
                if mi is not None:
                    self.flush()
                    for j in range(4):
                        self.mm(self.bank[7][0:32, j * 128:(j + 1) * 128],
                                V(self.mb, 128, 0, 128, j * 32, [(1, 32)]), self.ident[:], True, True,
                                ["mb", "ident"], [("bank", 7)])
                    self.cp(self.maskT[mi][0:32, :], self.bank[7][0:32, :], [("bank", 7)], [("maskT", mi)])
                for r in range(4):
                    h = 4 * g + r
                    r0 = 64 * (h % 2)
                    qtile = pj[r // 2]
                    ktile = pj[2 + (r % 2)]
                    ob = 3 + self.rot("obank", 3)

                    def extras(kt, j0, j1, h=h, mi=mi, Q=Q):
                        ex = self.toep_extras(8 + h, Q, kt, j0, j1)
                        if mi is not None:
                            ex.append((j0 * 128, j1 * 128, self.bin_[:, kt * 128:(kt + 1) * 128],
                                       self.maskT[mi][:, j0 * 128:j1 * 128], ["bin", ("maskT", mi)]))
                        return ex

                    def post(r=r, h=h, ob=ob, Q=Q):
                        rb = 64 + self.rot("rs", 4) * 16
                        rs = V(self.small, 256, 0, 128, rb, [(1, 4)])
                        fv = V(self.small, 256, 0, 128, rb + 4, [(1, 4)])
                        self.recip(rs, self.osum(ob), [("bank", ob)], ["rs"])
                        gv = V(self.gate_s, 384, 0, 128, Q * 96 + h * 3 + 1, [(24, 4)])
                        self.tt(fv, rs, gv, ALU.mult, ["rs", "gate"], ["rs"])
                        oc = V(self.ocomb, 1024, 0, 128, r * 256, [(64, 4), (1, 64)])
                        self.tt(tv[0], self.obv(ob), bc64(rb + 4), ALU.mult, [("bank", ob), "rs"], ["tmp0"])
                        dst = V(self.otok, 4096, 0, 128, Q * 4 * 256 + r * 64, [(256, 4), (1, 64)])
                        self.tt(dst, oc, tv[0], ALU.add, [("ocomb", r), "tmp0"],
                                [("otok", Q * 4 + j) for j in range(4)])
                    self.attn(Q, self.causal_kts(Q),
                              lambda kt, ktile=ktile: V(ktile, 2048, 0, 128, kt * 128, [(1, 128)]),
                              lambda c0, c1, qtile=qtile: V(qtile, 2048, 0, 128, c0, [(1, c1 - c0)]),
                              lambda kt, g=g: V(self.VS, 2080, 0, 128, kt * 130 + g * 65, [(1, 65)]),
                              65, SC8, extras, ob, ["VS"], extra_reads=alltoks, post=post)
            self.flush()
            if stage >= 8:
                self.outproj(l, 2 + g)


def _rel_bucket(dist):
    n = np.maximum(dist, 0)
    max_exact = 16
    n_f = np.maximum(n, max_exact).astype(np.float32)
    large = max_exact + (np.log(n_f / np.float32(max_exact)) / np.float32(np.log(128 / 16))
                         * np.float32(16)).astype(np.int32)
    return np.where(n < max_exact, n, np.minimum(large, 31))


def make_consts():
    c = {}
    k = np.arange(128)[:, None]
    q = np.arange(128)[None, :]
    E = np.zeros((128, 2, 32, 128), np.float32)
    for di in range(2):
        dist = di * 128 + q - k
        bk = _rel_bucket(dist)
        for b in range(32):
            E[:, di, b, :] = ((bk == b) & (dist >= 0)).astype(np.float32)
    c["E_nonzero"] = [[bool(E[:, di, b, :].any()) for b in range(32)] for di in range(2)]
    c["c_E"] = E.reshape(128, -1)
    c["c_ident"] = np.eye(128, dtype=np.float32)
    c["c_caus"] = np.where(q >= k, 0.0, NEG).astype(np.float32)
    c["c_wedge"] = np.where(q < k, 0.0, NEG).astype(np.float32)
    n = np.arange(128)[:, None]
    qq = np.arange(S)[None, :]
    mc = np.where(16 * n + 31 <= qq, 0.0, NEG).astype(np.float32)
    mc[127, :] = 0.0
    c["c_mcmp"] = mc
    kk = np.arange(S)[None, :]
    c["c_bim"] = (kk // 256 == np.arange(128)[:, None]).astype(np.float32)
    c["c_bin"] = (kk // 64 == np.arange(128)[:, None]).astype(np.float32)
    p = np.arange(128)[:, None, None]
    J = np.arange(16)[None, :, None]
    nn = np.arange(8)[None, None, :]
    own = (J * 128 + p) // 256
    A = (nn < own).astype(np.float32)
    B = np.where(nn < own, 0.0, -1e9 - 1e6 * nn).astype(np.float32)
    c["c_mA"] = A.reshape(128, 128)
    c["c_mB"] = np.broadcast_to(B, (128, 16, 8)).reshape(128, 128).copy()
    c["c_mN"] = (NEG * A).reshape(128, 128).astype(np.float32)
    jj = np.arange(32)[None, None, :]
    cur = (J * 128 + p) // 64
    valid = jj <= cur
    forced = (jj == 0) | (jj > cur - 2)
    A = (valid & ~forced).astype(np.float32)
    B = np.where(valid & forced, 1e4 + jj, np.where(~valid, -1e9 - 1e6 * jj, 0.0)).astype(np.float32)
    c["c_nA"] = A.reshape(128, 512)
    c["c_nB"] = np.broadcast_to(B, (128, 16, 32)).reshape(128, 512).copy()
    n_cmp = 127
    cs = np.arange(n_cmp) * 16
    ss = np.arange(32) * 64
    ov = np.clip(np.minimum(cs[:, None] + 32, ss[None, :] + 64) - np.maximum(cs[:, None], ss[None, :]),
                 0, None) / 32
    ovp = np.zeros((128, 32), np.float32)
    ovp[:127] = ov
    c["c_ov"] = ovp
    return c


_CONSTS = None


def _consts():
    global _CONSTS
    if _CONSTS is None:
        _CONSTS = make_consts()
    return _CONSTS


def _lay_gu(g, u):
    L = g.shape[0]
    a = np.stack([g, u], axis=1)
    a = a.reshape(L, 2, KC, 128, NFC, 128)
    a = a.transpose(0, 4, 3, 1, 2, 5)
    return np.ascontiguousarray(a).reshape(L, NFC, 128, 2 * KC * 128)


def _lay_wd(w):
    L = w.shape[0]
    a = w.reshape(L, NFC, 128, KC, 128)
    a = a.transpose(0, 3, 2, 1, 4)
    return np.ascontiguousarray(a).reshape(L, KC, 128, NFC * 128)


def _lay_gains(vecs):
    cols = [v.reshape(KC, 128).T for v in vecs]
    return np.ascontiguousarray(np.concatenate(cols, axis=1)).astype(np.float32)


def _win_tiles():
    tiles = []
    for t in range(2):
        tiles.append(list(range(0 + 128 * t, 128 * (t + 1))))
    for t in range(2):
        tiles.append(list(range(256 + 128 * t, 256 + 128 * (t + 1))))
    for base in (768, 1024):
        for t in range(3):
            cols = []
            for slot in range(4):
                gi = t * 3 + slot
                if slot < 3 and gi < 8:
                    cols += list(range(base + gi * 32, base + gi * 32 + 32))
                else:
                    cols += [-1] * 32
            tiles.append(cols)
    for t in range(4):
        tiles.append(list(range(1536 + 128 * t, 1536 + 128 * (t + 1))))
    for base in (2304, 2560):
        for g in range(2):
            cc = list(range(base + 64 * g, base + 64 * g + 64))
            tiles.append(cc + cc)
    tiles.append(list(range(2048, 2176)))
    tiles.append(list(range(2176, 2304)))
    return tiles


def _lay_cols(w, cols):
    L = w.shape[0]
    idx = np.array(cols)
    sel = w[:, :, np.where(idx < 0, 0, idx)].copy()
    if (idx < 0).any():
        sel[:, :, idx < 0] = 0.0
    a = sel.reshape(L, KC, 128, len(cols)).transpose(0, 2, 1, 3)
    return np.ascontiguousarray(a)


def prep_inputs(p, L, layer0=0):
    sl = slice(layer0, layer0 + L)
    c = _consts()
    m = {}
    gl = []
    for l in range(layer0, layer0 + L):
        gl += [p["norm_ffn1"][l], p["norm_mix"][l], p["norm_ffn2"][l]]
    gl.append(p["final_norm"])
    m["gains"] = _lay_gains(gl)
    m["gu1"] = _lay_gu(p["ffn1_gate"][sl], p["ffn1_up"][sl])
    m["gu2"] = _lay_gu(p["ffn2_gate"][sl], p["ffn2_up"][sl])
    m["wd1"] = _lay_wd(p["ffn1_down"][sl])
    m["wd2"] = _lay_wd(p["ffn2_down"][sl])
    w_in = p["w_in"][sl]
    tiles = _win_tiles()
    m["winF"] = np.ascontiguousarray(
        np.stack([_lay_cols(w_in, t).reshape(L, 128, KC * 128) for t in tiles], axis=1))
    m["winTm"] = _lay_cols(w_in, list(range(512, 768))).reshape(L, 128, KC * 256)
    m["winTd"] = _lay_cols(w_in, list(range(1280, 1536))).reshape(L, 128, KC * 256)
    ncols = list(range(2432, 2560)) + list(range(2688, 2816)) + list(range(2816, 2840)) + [-1] * 40
    m["winTn"] = _lay_cols(w_in, ncols).reshape(L, 128, KC * 320)
    wo = p["w_out"][sl].reshape(L, 4, 2, 128, 1024).transpose(0, 1, 3, 2, 4)
    m["woutp"] = np.ascontiguousarray(wo).reshape(L, 4, 128, 2048)
    w1 = p["nsa_cmp_w1"][sl].reshape(L, 2, 8, 4, 64, 256).transpose(0, 1, 2, 4, 3, 5)
    m["w1p"] = np.ascontiguousarray(w1).reshape(L, 2, 8, 64, 1024)
    w2 = p["nsa_cmp_w2"][sl]
    w2k = w2[:, 0].reshape(L, 2, 128, 64).transpose(0, 2, 1, 3)
    w2k = np.concatenate([w2k, w2k], axis=3)
    m["w2k"] = np.ascontiguousarray(w2k).reshape(L, 128, 256)
    w2v = w2[:, 1].reshape(L, 2, 128, 64).transpose(0, 2, 1, 3)
    m["w2v"] = np.ascontiguousarray(w2v).reshape(L, 128, 128)
    pe = p["nsa_cmp_pe"][sl]
    m["peT"] = np.ascontiguousarray(pe.transpose(0, 3, 1, 2)).reshape(L, 64, 64)
    dl = p["diff_lambda"][sl].reshape(L, 1, 128)
    m["dlrep"] = np.ascontiguousarray(np.broadcast_to(dl, (L, 128, 128)))
    sub = p["diff_subln"][sl].reshape(L, 1, 64)
    m["subrep"] = np.ascontiguousarray(np.broadcast_to(sub, (L, 128, 64)))
    m["tblrep"] = np.ascontiguousarray(np.broadcast_to(p["rel_bias"].reshape(1, 512), (128, 512)))
    for k_, v_ in c.items():
        if k_.startswith("c_"):
            m[k_] = v_
    return {k_: np.ascontiguousarray(v_, dtype=np.float32) for k_, v_ in m.items()}


_PROG_CACHE = {}


def _get_prog(n_layers, n_seq, do_final, parts, branches, layer0):
    key = (n_layers, n_seq, do_final, tuple(parts), tuple(branches), layer0)
    if key not in _PROG_CACHE:
        b = Builder(n_layers, n_seq, do_final, parts, branches, layer0)
        _PROG_CACHE[key] = b.build(_consts())
    return _PROG_CACHE[key]


def run_layers(x, p, n_layers, n_seq_per_core, do_final=True, parts=("ffn1", "mix", "ffn2"),
               branches=("moba", "diff", "nsa"), layer0=0, core_ids=None):
    L = n_layers
    nc, cnt = _get_prog(L, n_seq_per_core, do_final, parts, branches, layer0)
    B = x.shape[0]
    ncores = B // n_seq_per_core
    xT = np.ascontiguousarray(np.transpose(x, (0, 2, 1)))
    shared = prep_inputs(p, L, layer0)
    in_maps = []
    for c in range(ncores):
        m = dict(shared)
        m["xT"] = xT[c * n_seq_per_core:(c + 1) * n_seq_per_core]
        in_maps.append(m)
    res = run_bass_kernel_spmd(nc, in_maps, core_ids=list(range(ncores)) if core_ids is None else core_ids)
    o = np.concatenate([r["outT"] for r in res.results], axis=0)
    return np.ascontiguousarray(np.transpose(o, (0, 2, 1)))


def kernel(**inputs):
    x = np.asarray(inputs["x"], dtype=np.float32)
    p = {k: np.asarray(v, dtype=np.float32) for k, v in inputs.items() if k != "x"}
    return run_layers(x, p, DEPTH, x.shape[0] // NCORES)
```

```python
import math
import os
import numpy as np
import concourse.bass as bass
import concourse.mybir as mybir
from concourse.bass_utils import run_bass_kernel_spmd
from contextlib import ExitStack

F32 = mybir.dt.float32
BF16 = mybir.dt.bfloat16
AF = mybir.ActivationFunctionType
ALU = mybir.AluOpType
AX = mybir.AxisListType

S = 2048
D = 1024
KC = 8
FF = 2816
NFC = 22
DEPTH = 4
NCORES = 8
RMS_EPS = 1e-6
NEG = -30000.0


SAME_ENG_NOSYNC = ("pe",)


class _Op:
    __slots__ = ("eng", "fn", "reads", "writes", "isdma", "key", "n", "deps",
                 "signal", "val", "waits")


class Sched:
    ENGS = ("pe", "act", "dve", "pool", "sp")

    def __init__(self):
        self.ops = []
        self.per_eng = {e: [] for e in self.ENGS}
        self.last_w = {}
        self.readers = {}
        self.dma_count = {}
        self.pending_barrier = {e: None for e in self.ENGS}
        self.pending_old = {e: None for e in self.ENGS}
        self.last_op = {e: None for e in self.ENGS}
        self.live_dma = []

    def _mk(self, eng, fn, reads, writes, isdma, key, nobarrier=False):
        op = _Op()
        op.eng = eng
        op.fn = fn
        op.reads = tuple(reads)
        op.writes = tuple(writes)
        op.isdma = isdma
        op.key = key
        op.deps = []
        op.signal = False
        op.val = 0
        op.waits = []
        deps = set()
        for t in op.reads:
            w = self.last_w.get(t)
            if w is not None:
                deps.add((w, "raw"))
        for t in op.writes:
            w = self.last_w.get(t)
            if w is not None:
                deps.add((w, "waw"))
            for r in self.readers.get(t, ()):
                deps.add((r, "war"))
        if nobarrier:
            for o in (self.pending_old[eng] or ()):
                deps.add((o, "raw"))
        else:
            pb = self.pending_barrier[eng]
            if pb is not None:
                for o in pb:
                    deps.add((o, "raw"))
                self.pending_barrier[eng] = None
                self.pending_old[eng] = None
        for (p, kind) in deps:
            if p is op:
                continue
            if (not p.isdma) and (not isdma) and p.eng == eng:
                if kind != "raw" or eng in SAME_ENG_NOSYNC:
                    continue
            op.deps.append(p)
        for t in op.writes:
            self.last_w[t] = op
            self.readers[t] = []
        for t in op.reads:
            self.readers.setdefault(t, []).append(op)
        if isdma:
            c = self.dma_count.get(key, 0) + 1
            self.dma_count[key] = c
            op.val = 16 * c
            op.signal = True
            self.live_dma.append(op)
        self.ops.append(op)
        self.per_eng[eng].append(op)
        if not isdma:
            self.last_op[eng] = op
        return op

    def add(self, eng, fn, reads=(), writes=()):
        return self._mk(eng, fn, reads, writes, False, None)

    def dma(self, queue, fn, reads=(), writes=(), key=None, nobarrier=False):
        assert key is not None
        return self._mk(queue, fn, reads, writes, True, key, nobarrier)

    def barrier(self):
        lst = [o for o in self.last_op.values() if o is not None]
        lst += self.live_dma
        self.live_dma = []
        for e in self.ENGS:
            prev = self.pending_barrier[e]
            self.pending_old[e] = list(prev) if prev else None
            self.pending_barrier[e] = list(lst) + (prev or [])

    def final_wait(self, eng, tokens):
        return self._mk(eng, None, tokens, (), False, None)

    def analyse(self):
        seqno = {}
        cnt = {e: 0 for e in self.ENGS}
        for op in self.ops:
            if not op.isdma:
                cnt[op.eng] += 1
                seqno[id(op)] = cnt[op.eng]
        seen = {e: {} for e in self.ENGS}
        need = []
        for op in self.ops:
            sd = seen[op.eng]
            best = {}
            for p in op.deps:
                if p.isdma:
                    k = ("d", p.key)
                    v = p.val
                else:
                    k = ("e", p.eng)
                    v = seqno[id(p)]
                if sd.get(k, 0) >= v:
                    continue
                if k not in best or best[k][0] < v:
                    best[k] = (v, p)
            lst = []
            for k, (v, p) in best.items():
                sd[k] = v
                p.signal = True
                lst.append(p)
            need.append(lst)
        cnt = {e: 0 for e in self.ENGS}
        for op in self.ops:
            if (not op.isdma) and op.signal:
                cnt[op.eng] += 1
                op.val = cnt[op.eng]
        for op, lst in zip(self.ops, need):
            op.waits = [(("d", p.key) if p.isdma else ("e", p.eng), p.val) for p in lst]
        return cnt

    def emit(self, nc, stack):
        cnt = self.analyse()
        sems = {}
        for e in self.ENGS:
            sems[("e", e)] = stack.enter_context(nc.semaphore("s_" + e))
        for k in self.dma_count:
            sems[("d", k)] = stack.enter_context(nc.semaphore("d_" + str(k)))
        block = stack.enter_context(nc.Block())

        def run(engname):
            def body(eng):
                for op in self.per_eng[engname]:
                    for (k, v) in op.waits:
                        eng.wait_ge(sems[k], v)
                    if op.fn is None:
                        continue
                    inst = op.fn(eng)
                    if op.isdma:
                        inst.then_inc(sems[("d", op.key)], 16)
                    elif op.signal:
                        inst.then_inc(sems[("e", engname)], 1)
            return body

        block.tensor(run("pe"))
        block.scalar(run("act"))
        block.vector(run("dve"))
        block.gpsimd(run("pool"))
        block.sync(run("sp"))
        return cnt


SB_BASE = 16512
SB_END = 229376

SC8 = 0.125
SCD = 32.0 ** -0.5
GELU_C = 1.5957691216057308


def _nbytes(dt):
    return 4 if dt == F32 else 2


def V(t, ftot, p0, npart, f0, dims):
    return bass.AP(t, p0 * ftot + f0, [[ftot, npart]] + [[s, c] for (s, c) in dims])


class Builder:
    def __init__(self, n_layers, n_seq, do_final=True, parts=("ffn1", "mix", "ffn2"),
                 branches=("moba", "diff", "nsa"), layer0=0):
        self.L = n_layers
        self.NS = n_seq
        self.do_final = do_final
        self.parts = parts
        self.branches = branches
        self.layer0 = layer0
        self.nc = bass.Bass("TRN2", target_bir_lowering=False)
        self.sc = Sched()
        self.stack = ExitStack()
        self.off = SB_BASE
        self.nalloc = 0
        self.rr = {}
        self.pq = []
        self.pbank_all = True
        self.prefetch_ok = False

    def din(self, name, shape, dt=F32):
        return self.nc.dram_tensor(name, list(shape), dt, kind="ExternalInput").ap()

    def dout(self, name, shape, dt=F32):
        return self.nc.dram_tensor(name, list(shape), dt, kind="ExternalOutput").ap()

    def sb(self, name, shape, dt):
        n = 1
        for s_ in shape[1:]:
            n *= s_
        nb = (n * _nbytes(dt) + 63) // 64 * 64
        off = self.off
        self.off += nb
        assert self.off <= SB_END, ("SBUF overflow", name, self.off)
        self.nalloc += 1
        return self.nc.alloc_sbuf_tensor_at("%s_%d" % (name, self.nalloc), list(shape), dt, offset=off)

    def ps(self, name, shape, dt=F32):
        return self.stack.enter_context(self.nc.psum_tensor(name, list(shape), dt))

    def rot(self, name, n):
        i = self.rr.get(name, 0)
        self.rr[name] = i + 1
        return i % n

    def mm(self, out, lhsT, rhs, start, stop, reads, writes):
        self.sc.add("pe", lambda e: e.matmul(out, lhsT, rhs, start=start, stop=stop,
                                             skip_group_check=True), reads, writes)

    def actf(self, out, in_, func, reads, writes, scale=1.0, bias=0.0):
        self.sc.add("act", lambda e: e.activation(out=out, in_=in_, func=func, bias=bias, scale=scale),
                    reads, writes)

    def tt(self, out, in0, in1, op, reads, writes, eng="dve"):
        self.sc.add(eng, lambda e: e.tensor_tensor(out=out, in0=in0, in1=in1, op=op), reads, writes)

    def ts(self, out, in0, s1, s2, op0, op1, reads, writes, eng="dve"):
        if op1 is None:
            self.sc.add(eng, lambda e: e.tensor_scalar(out=out, in0=in0, scalar1=s1, scalar2=None, op0=op0),
                        reads, writes)
        else:
            self.sc.add(eng, lambda e: e.tensor_scalar(out=out, in0=in0, scalar1=s1, scalar2=s2,
                                                        op0=op0, op1=op1), reads, writes)

    def stt(self, out, in0, scalar, in1, op0, op1, reads, writes, eng="dve"):
        self.sc.add(eng, lambda e: e.scalar_tensor_tensor(out=out, in0=in0, scalar=scalar, in1=in1,
                                                           op0=op0, op1=op1), reads, writes)

    def cp(self, out, in_, reads, writes, eng="dve"):
        if eng == "act":
            self.actf(out, in_, AF.Copy, reads, writes)
        else:
            self.sc.add(eng, lambda e: e.tensor_copy(out=out, in_=in_), reads, writes)

    def recip(self, out, in_, reads, writes):
        self.sc.add("dve", lambda e: e.reciprocal(out=out, in_=in_), reads, writes)

    def red(self, out, in_, op, reads, writes):
        self.sc.add("dve", lambda e: e.tensor_reduce(out=out, in_=in_, axis=AX.X, op=op), reads, writes)

    def memset(self, ap, val, writes, eng="dve"):
        self.sc.add(eng, lambda e: e.memset(ap, val), (), writes)

    def dmaq(self, queue, out, in_, reads, writes, key, nobarrier=False):
        self.sc.dma(queue, lambda e: e.dma_start(out=out, in_=in_), reads, writes, key, nobarrier)

    def declare(self):
        L, NS = self.L, self.NS
        d = self.din
        self.xT_d = d("xT", [NS, D, S])
        self.out_d = self.dout("outT", [NS, D, S])
        self.gains_d = d("gains", [128, (3 * L + 1) * KC])
        self.gu_d = [d("gu%d" % i, [L, NFC, 128, 2 * KC * 128]) for i in (1, 2)]
        self.wd_d = [d("wd%d" % i, [L, KC, 128, NFC * 128]) for i in (1, 2)]
        self.winF_d = d("winF", [L, 20, 128, KC * 128])
        self.winTm_d = d("winTm", [L, 128, KC * 256])
        self.winTd_d = d("winTd", [L, 128, KC * 256])
        self.winTn_d = d("winTn", [L, 128, KC * 320])
        self.wout_d = d("woutp", [L, 4, 128, 2 * 1024])
        self.w1_d = d("w1p", [L, 2, 8, 64, 4 * 256])
        self.w2k_d = d("w2k", [L, 128, 2 * 128])
        self.w2v_d = d("w2v", [L, 128, 2 * 64])
        self.peT_d = d("peT", [L, 64, 2 * 32])
        self.dl_d = d("dlrep", [L, 128, 128])
        self.sub_d = d("subrep", [L, 128, 64])
        self.tbl_d = d("tblrep", [128, 512])
        self.cE_d = d("c_E", [128, 2 * 32 * 128])
        self.cid_d = d("c_ident", [128, 128])
        self.ccaus_d = d("c_caus", [128, 128])
        self.cwedge_d = d("c_wedge", [128, 128])
        self.cmcmp_d = d("c_mcmp", [128, 2048])
        self.cbim_d = d("c_bim", [128, 2048])
        self.cbin_d = d("c_bin", [128, 2048])
        self.cmA_d = d("c_mA", [128, 128])
        self.cmB_d = d("c_mB", [128, 128])
        self.cmN_d = d("c_mN", [128, 128])
        self.cnA_d = d("c_nA", [128, 512])
        self.cnB_d = d("c_nB", [128, 512])
        self.cov_d = d("c_ov", [128, 32])

        sb = self.sb
        self.xT = sb("xT_s", [128, KC, S], F32)
        self.hT = sb("hT_s", [128, KC, S], BF16)
        self.Tb = sb("Tb", [128, 16 * 2 * 128], BF16)
        self.ident = sb("ident", [128, 128], BF16)
        self.ones_bf = sb("ones_bf", [128, 128], BF16)
        self.gains = sb("gains_s", [128, (3 * L + 1) * KC], F32)
        self.lam = sb("lam", [128, 8], F32)
        self.arena0 = self.off

        self.off = self.arena0
        self.E_s = sb("E_s", [128, 2 * 32 * 128], BF16)
        self.tbl_s = sb("tbl_s", [128, 512], F32)
        self.Tacc = sb("Tacc", [128, 2 * 16 * 128], F32)
        self.Ttmp = sb("Ttmp", [128, 16 * 128], F32)
        self.caus_s = sb("caus_s", [128, 128], F32)

        self.off = self.arena0
        self.act_s = sb("act_s", [128, NFC, 1024], BF16)
        self.wgu = [sb("wgu_s%d" % i, [128, 2 * KC * 128], BF16) for i in range(3)]
        self.wd = [sb("wd_s%d" % i, [128, NFC * 128], BF16) for i in range(2)]
        self.sg = [sb("sg%d" % i, [128, 512], F32) for i in range(2)]
        self.sq = [sb("sq%d" % i, [128, 512], BF16) for i in range(2)]
        self.rstd = [sb("rstd%d" % i, [128, 512], F32) for i in range(2)]
        self.ffn_end = self.off

        self.off = self.arena0
        self.PT = [sb("PT%d" % i, [128, 512], BF16) for i in range(4)]
        self.small = sb("small", [128, 256], F32)
        self.maskT = [sb("maskT%d" % i, [128, 512], BF16) for i in range(2)]
        self.mb = sb("mb", [128, 128], BF16)
        self.cmpbuf = sb("cmpbuf", [128, 1024], BF16)
        self.winbuf = [sb("winbuf%d" % i, [128, KC * 128], BF16) for i in range(2)]
        u0 = self.off
        self.wT = sb("wT", [128, KC * 512], BF16)
        e1 = self.off
        self.off = u0
        self.w1buf = [sb("w1buf%d" % i, [64, 4 * 256], BF16) for i in range(2)]
        self.HT = sb("HT", [128, 2 * 2 * 128], BF16)
        self.w2k_s = sb("w2k_s", [128, 2 * 128], BF16)
        self.w2v_s = sb("w2v_s", [128, 2 * 64], BF16)
        self.peT_s = sb("peT_s", [64, 64], BF16)
        self.b1_s = sb("b1_s", [128, 2], F32)
        self.gx = [sb("gx%d" % i, [128, 128], F32) for i in range(3)]
        assert self.off <= e1
        self.off = e1
        k0 = self.off
        self.kcT = [sb("kcT%d" % i, [64, S], BF16) for i in range(4)]
        e2 = self.off
        self.off = k0
        self.sq_m = [sb("sqm%d" % i, [128, 512], BF16) for i in range(2)]
        self.rstd_m = [sb("rstdm%d" % i, [128, 512], F32) for i in range(2)]
        assert self.off <= e2
        self.off = u0
        self.otok = sb("otok", [128, 16 * 256], BF16)
        self.oT = sb("oT", [128, 2 * S], BF16)
        self.wout_s = sb("wout_s", [128, 2 * 1024], BF16)
        self.tmp = [sb("tmp%d" % i, [128, 256], F32) for i in range(4)]
        self.ocomb = sb("ocomb", [128, 4 * 256], F32)
        e3 = self.off
        self.off = max(e2, e3)
        self.mix0 = self.off

        self.off = self.mix0
        self.pjM = [sb("pjM%d" % i, [128, S], BF16) for i in range(6)]
        self.VM = sb("VM", [128, 16 * 4 * 65], BF16)
        self.bim = sb("bim", [128, 2048], BF16)
        self.mA = sb("mA", [128, 128], F32)
        self.mB = sb("mB", [128, 128], F32)
        self.mN = sb("mN", [128, 128], F32)
        self.kmT = sb("kmT", [128, 4 * 8], BF16)
        self.kmF = sb("kmF", [128, 4 * 8], F32)
        self.maskM = [sb("maskM%d" % i, [128, 512], BF16) for i in range(8)]
        self.mbM = sb("mbM", [128, 256], BF16)
        self.moba_end = self.off

        self.off = self.mix0
        self.pjD = [sb("pjD%d" % i, [128, S], BF16) for i in range(11)]
        self.VD = sb("VD", [128, 16 * 4 * 65], BF16)
        self.dl_s = sb("dl_s", [128, 128], F32)
        self.sub_s = sb("sub_s", [128, 64], F32)
        self.diff_end = self.off

        self.off = self.mix0
        self.pjN = [sb("pjN%d" % i, [128, S], BF16) for i in range(6)]
        self.VS = sb("VS", [128, 16 * 2 * 65], BF16)
        self.VW = sb("VW", [128, 16 * 2 * 65], BF16)
        self.gate_s = sb("gate_s", [128, 16 * 24], F32)
        self.mcmp = sb("mcmp", [128, 2048], BF16)
        self.bin_ = sb("bin", [128, 2048], BF16)
        self.nA = sb("nA", [128, 512], F32)
        self.nB = sb("nB", [128, 512], F32)
        self.kcmpT = sb("kcmpT", [128, 2 * 128], BF16)
        self.vcaug = sb("vcaug", [128, 2 * 97], BF16)
        self.wedge = sb("wedge", [128, 128], BF16)
        self.impacc = sb("impacc", [128, 128], F32)
        self.nsa_end = self.off
        self.off = max(self.ffn_end, self.moba_end, self.diff_end, self.nsa_end)
        print("SBUF end", self.off, "of", SB_END, "ffn", self.ffn_end, "moba", self.moba_end,
              "diff", self.diff_end, "nsa", self.nsa_end)

        self.bank = [self.ps("bank%d" % i, [128, 512], F32) for i in range(8)]

    def bankv(self, b, p0, npart, f0, dims):
        return V(self.bank[b], 512, p0, npart, f0, dims)

    def setup(self, consts):
        sc = self.sc
        self.dmaq("sp", self.gains[:], self.gains_d[:, :], [], ["gains"], "gains")
        self.memset(self.ones_bf[:], 1.0, ["ones"])
        self.dmaq("pool", self.ident[:], self.cid_d[:, :], [], ["ident"], "ident")
        self.dmaq("pool", self.E_s[:], self.cE_d[:, :], [], ["E_s"], "E_s")
        self.dmaq("sp", self.tbl_s[:], self.tbl_d[:, :], [], ["tbl_s"], "tbl_s")
        self.dmaq("sp", self.caus_s[:], self.ccaus_d[:, :], [], ["caus_s"], "caus_s")
        tb3 = V(self.tbl_s, 512, 0, 128, 0, [(16, 32), (1, 16)])
        t31 = V(self.tbl_s, 512, 0, 128, 31 * 16, [(0, 32), (1, 16)])
        self.tt(tb3, tb3, t31, ALU.subtract, ["tbl_s"], ["tbl_s"])
        first = {0: True, 1: True}
        for di in range(2):
            for b in range(31):
                if not consts["E_nonzero"][di][b]:
                    continue
                Eb = V(self.E_s, 8192, 0, 128, (di * 32 + b) * 128, [(0, 16), (1, 128)])
                tv = V(self.tbl_s, 512, 0, 128, b * 16, [(1, 16), (0, 128)])
                acc = V(self.Tacc, 4096, 0, 128, di * 2048, [(128, 16), (1, 128)])
                if first[di]:
                    self.tt(acc, Eb, tv, ALU.mult, ["E_s", "tbl_s"], [("Tacc", di)])
                    first[di] = False
                else:
                    tmp = V(self.Ttmp, 2048, 0, 128, 0, [(128, 16), (1, 128)])
                    self.tt(tmp, Eb, tv, ALU.mult, ["E_s", "tbl_s"], ["Ttmp"])
                    self.tt(acc, acc, tmp, ALU.add, [("Tacc", di), "Ttmp"], [("Tacc", di)])
        for h in range(16):
            inv = (1.0 / SCD) if 4 <= h < 8 else (1.0 / SC8)
            for di in range(2):
                src = V(self.Tacc, 4096, 0, 128, di * 2048 + h * 128, [(1, 128)])
                dst = V(self.Tb, 4096, 0, 128, (h * 2 + di) * 128, [(1, 128)])
                if di == 0:
                    self.stt(dst, src, inv, self.caus_s[:], ALU.mult, ALU.add,
                             [("Tacc", di), "caus_s"], ["Tb"])
                else:
                    self.ts(dst, src, inv, None, ALU.mult, None, [("Tacc", di)], ["Tb"])
        sc.barrier()

    def Tbv(self, h, di):
        return V(self.Tb, 4096, 0, 128, (h * 2 + di) * 128, [(1, 128)])

    def load_x(self, s):
        for c in range(KC):
            self.dmaq("sp", self.xT[:, c, :], self.xT_d[s, c * 128:(c + 1) * 128, :],
                      [], [("xT", c, tt) for tt in range(4)], "xload%d" % c)

    def store_out(self, s):
        toks = []
        for c in range(KC):
            self.dmaq("sp", self.out_d[s, c * 128:(c + 1) * 128, :], self.xT[:, c, :],
                      [("xT", c, tt) for tt in range(4)], [("outd", s, c)], "xstore%d" % c)
            toks.append(("outd", s, c))
        return toks

    def norm(self, gidx, final=False, mixer=False):
        sqs = self.sq_m if mixer else self.sq
        rstds = self.rstd_m if mixer else self.rstd
        for tt in range(4):
            tsl = slice(tt * 512, (tt + 1) * 512)
            bi = 6 + self.rot("nbank", 2)
            bank = self.bank[bi]
            btok = ("bank", bi)
            for c in range(KC):
                qi = self.rot("sq", 2)
                sq = sqs[qi]
                self.actf(sq[:], self.xT[:, c, tsl], AF.Square, [("xT", c, tt)], [("sq", qi)])
                self.mm(bank[:], self.ones_bf[:], sq[:], c == 0, c == KC - 1,
                        [("sq", qi), "ones"], [btok])
            ri = self.rot("rstd", 2)
            rstd = rstds[ri]
            self.actf(rstd[:], bank[:], AF.Sqrt, [btok], [("rstd", ri)], scale=1.0 / D, bias=RMS_EPS)
            self.recip(rstd[:], rstd[:], [("rstd", ri)], [("rstd", ri)])
            for c in range(KC):
                gcol = self.gains[:, gidx * KC + c: gidx * KC + c + 1]
                if not final:
                    self.stt(self.hT[:, c, tsl], self.xT[:, c, tsl], gcol, rstd[:], ALU.mult, ALU.mult,
                             [("xT", c, tt), ("rstd", ri), "gains"], [("hT", c, tt)])
                else:
                    self.stt(self.xT[:, c, tsl], self.xT[:, c, tsl], gcol, rstd[:], ALU.mult, ALU.mult,
                             [("xT", c, tt), ("rstd", ri), "gains"], [("xT", c, tt)])

    def ffn(self, l, which):
        gu_d = self.gu_d[which]
        wd_d = self.wd_d[which]
        for half in range(2):
            for fc in range(NFC):
                wi = self.rot("wgu", 3)
                w = self.wgu[wi]
                self.dmaq("pool", w[:], gu_d[l, fc, :, :], [], [("wgu", wi)], "wgu%d" % wi)
                for sub in range(2):
                    tt = half * 2 + sub
                    tsl = slice(tt * 512, (tt + 1) * 512)
                    gb = self.rot("abank", 3) * 2
                    gbank, ubank = self.bank[gb], self.bank[gb + 1]
                    for c in range(KC):
                        self.mm(gbank[:], w[:, c * 128:(c + 1) * 128], self.hT[:, c, tsl],
                                c == 0, c == KC - 1, [("wgu", wi), ("hT", c, tt)], [("bank", gb)])
                    for c in range(KC):
                        self.mm(ubank[:], w[:, (KC + c) * 128:(KC + c + 1) * 128], self.hT[:, c, tsl],
                                c == 0, c == KC - 1, [("wgu", wi), ("hT", c, tt)], [("bank", gb + 1)])
                    si = self.rot("sg", 2)
                    sg = self.sg[si]
                    self.actf(sg[:], gbank[:], AF.Silu, [("bank", gb)], [("sg", si)])
                    self.tt(self.act_s[:, fc, sub * 512:(sub + 1) * 512], ubank[:], sg[:], ALU.mult,
                            [("bank", gb + 1), ("sg", si)], [("act", fc, sub)])
            for dc in range(KC):
                wi = self.rot("wd", 2)
                w = self.wd[wi]
                self.dmaq("pool", w[:], wd_d[l, dc, :, :], [], [("wd", wi)], "wd%d" % wi)
                for sub in range(2):
                    tt = half * 2 + sub
                    tsl = slice(tt * 512, (tt + 1) * 512)
                    yb = 6 + self.rot("nbank", 2)
                    ybank = self.bank[yb]
                    for fc in range(NFC):
                        self.mm(ybank[:], w[:, fc * 128:(fc + 1) * 128],
                                self.act_s[:, fc, sub * 512:(sub + 1) * 512],
                                fc == 0, fc == NFC - 1, [("wd", wi), ("act", fc, sub)], [("bank", yb)])
                    self.stt(self.xT[:, dc, tsl], ybank[:], 0.5, self.xT[:, dc, tsl], ALU.mult, ALU.add,
                             [("bank", yb), ("xT", dc, tt)], [("xT", dc, tt)])

    def proj_F(self, l, tile, subs):
        wi = self.rot("winbuf", 2)
        w = self.winbuf[wi]
        self.dmaq("pool", w[:], self.winF_d[l, tile, :, :], [], [("winbuf", wi)], "winbuf%d" % wi,
                  nobarrier=self.prefetch_ok)
        self.prefetch_ok = False
        for tt in range(4):
            tsl = slice(tt * 512, (tt + 1) * 512)
            for (c0, c1, dst, dtok) in subs:
                M = c1 - c0
                bi = self.rot("pbank", 6)
                bank = self.bank[bi]
                for c in range(KC):
                    self.mm(bank[0:M, :], w[:, c * 128 + c0: c * 128 + c1], self.hT[:, c, tsl],
                            c == 0, c == KC - 1, [("winbuf", wi), ("hT", c, tt)], [("bank", bi)])
                eng = "act" if self.rot("pev", 2) == 0 else "dve"
                self.cp(dst[0:M, tsl], bank[0:M, :], [("bank", bi)], [dtok + (tt,)], eng=eng)

    def proj_Fs(self, l, tile, evacs):
        wi = self.rot("winbuf", 2)
        w = self.winbuf[wi]
        self.dmaq("pool", w[:], self.winF_d[l, tile, :, :], [], [("winbuf", wi)], "winbuf%d" % wi,
                  nobarrier=self.prefetch_ok)
        self.prefetch_ok = False
        for tt in range(4):
            tsl = slice(tt * 512, (tt + 1) * 512)
            bi = self.rot("pbank", 6) if self.pbank_all else 6 + self.rot("nbank", 2)
            bank = self.bank[bi]
            for c in range(KC):
                self.mm(bank[:, :], w[:, c * 128:(c + 1) * 128], self.hT[:, c, tsl],
                        c == 0, c == KC - 1, [("winbuf", wi), ("hT", c, tt)], [("bank", bi)])
            eng = "act" if self.rot("pev", 2) == 0 else "dve"
            for (r0, r1, dst, dtok) in evacs:
                self.cp(dst[r0:r1, tsl], bank[r0:r1, :], [("bank", bi)], [dtok + (tt,)], eng=eng)

    def proj_T(self, l, wd_ap, ncols, evac):
        wT = self.wT
        self.dmaq("pool", wT[:, 0:KC * ncols], wd_ap, [], ["wT"], "wT")
        for kt in range(16):
            bi = self.rot("pbank", 6)
            bank = self.bank[bi]
            for c in range(KC):
                self.mm(bank[:, 0:ncols], self.hT[:, c, kt * 128:(kt + 1) * 128],
                        wT[:, c * ncols:(c + 1) * ncols], c == 0, c == KC - 1,
                        ["wT", ("hT", c, kt // 4)], [("bank", bi)])
            evac(kt, bi)

    LOOK = 2

    def flush(self):
        while self.pq:
            self.pq.pop(0)()

    def attn(self, Q, kt_list, kap, qap, vap, nv, scale, extras, ob, otoks,
             kparts=128, extra_reads=(), post=None):
        last = {}
        for (kt, j0, j1) in kt_list:
            for j in range(j0, j1):
                last[j] = kt
        obank = self.bank[ob]
        state = {"first": True}
        n = len(kt_list)
        for idx, (kt, j0, j1) in enumerate(kt_list):
            si = self.rot("sbank", 3)
            sbank = self.bank[si]
            c0, c1 = j0 * 128, j1 * 128
            ex = extras(kt, j0, j1)
            self.mm(sbank[0:kparts, c0:c1], kap(kt), qap(Q * 512 + c0, Q * 512 + c1),
                    True, len(ex) == 0, list(extra_reads), [("bank", si)])
            for i, (e0, e1, l_ap, r_ap, rd) in enumerate(ex):
                self.mm(sbank[0:kparts, e0:e1], l_ap, r_ap, False, i == len(ex) - 1,
                        list(rd), [("bank", si)])
            pi = self.rot("PT", 4)
            pt = self.PT[pi]
            self.actf(pt[0:kparts, c0:c1], sbank[0:kparts, c0:c1], AF.Exp, [("bank", si)], [("pt", pi)],
                      scale=scale)

            def stageB(kt=kt, j0=j0, j1=j1, pt=pt, pi=pi, islast=(idx == n - 1)):
                for j in range(j0, j1):
                    self.mm(obank[:, j * 128: j * 128 + nv], pt[0:kparts, j * 128:(j + 1) * 128], vap(kt),
                            state["first"], last[j] == kt, [("pt", pi)] + list(otoks), [("bank", ob)])
                    state["first"] = False
                if islast and post is not None:
                    post()
            self.pq.append(stageB)
            while len(self.pq) > self.LOOK:
                self.pq.pop(0)()

    def outproj(self, l, grp):
        self.flush()
        self.dmaq("pool", self.wout_s[:], self.wout_d[l, grp, :, :], [], ["wout"], "wout")
        for f in range(2):
            for Qi in range(4):
                bi = 6 + self.rot("nbank", 2)
                bank = self.bank[bi]
                for j in range(4):
                    J = Qi * 4 + j
                    src = V(self.otok, 4096, 0, 128, J * 256 + f * 128, [(1, 128)])
                    self.mm(bank[:, j * 128:(j + 1) * 128], src, self.ident[:], True, True,
                            [("otok", J), "ident"], [("bank", bi)])
                eng = "act" if self.rot("pev", 2) == 0 else "dve"
                dst = V(self.oT, 4096, 0, 128, f * 2048 + Qi * 512, [(1, 512)])
                self.cp(dst, bank[:], [("bank", bi)], [("oT", f, Qi)], eng=eng)
        for dc in range(KC):
            for tt in range(4):
                bi = 6 + self.rot("nbank", 2)
                bank = self.bank[bi]
                for f in range(2):
                    self.mm(bank[:], self.wout_s[:, f * 1024 + dc * 128: f * 1024 + (dc + 1) * 128],
                            V(self.oT, 4096, 0, 128, f * 2048 + tt * 512, [(1, 512)]),
                            f == 0, f == 1, ["wout", ("oT", f, tt)], [("bank", bi)])
                tsl = slice(tt * 512, (tt + 1) * 512)
                self.tt(self.xT[:, dc, tsl], bank[:], self.xT[:, dc, tsl], ALU.add,
                        [("bank", bi), ("xT", dc, tt)], [("xT", dc, tt)])

    def rank_mask(self, score_ap_fn, n, nj, K, mult_tab, mb_out_fn, stok, mtok):
        for j in range(nj):
            a = score_ap_fn(j, [(0, n), (1, n)])
            b = score_ap_fn(j, [(1, n), (0, n)])
            cmpv = V(self.cmpbuf, 1024, 0, 128, 0, [(n, n), (1, n)])
            self.tt(cmpv, a, b, ALU.is_gt, [stok], ["cmpbuf"])
            rk = V(self.small, 256, 0, 128, 192, [(1, n)])
            self.red(rk, cmpv, ALU.add, ["cmpbuf"], ["rank"])
            if callable(mult_tab):
                self.stt(mb_out_fn(j), rk, K - 0.5, mult_tab(j), ALU.is_ge, ALU.mult,
                         ["rank", "tabs"], [mtok])
            else:
                self.ts(mb_out_fn(j), rk, K - 0.5, mult_tab, ALU.is_ge, ALU.mult, ["rank"], [mtok])

    def pass_moba(self, l):
        sc = self.sc
        pj = self.pjM
        for t in range(2):
            self.proj_F(l, t, [(0, 128, pj[t], ("pjM", t))])
        for h in range(4):
            self.memset(pj[2 + h][:], 0.0, [("pjM", 2 + h, tt) for tt in range(4)], eng="pool")
        for t in range(2):
            self.proj_Fs(l, 2 + t, [(0, 64, pj[2 + 2 * t], ("pjM", 2 + 2 * t)),
                                    (64, 128, pj[3 + 2 * t], ("pjM", 3 + 2 * t))])
        self.memset(self.VM[:], 1.0, ["VM"])

        def evac(kt, bi):
            dst = V(self.VM, 4160, 0, 128, kt * 260, [(65, 4), (1, 64)])
            src = V(self.bank[bi], 512, 0, 128, 0, [(64, 4), (1, 64)])
            self.cp(dst, src, [("bank", bi)], ["VM"], eng="act" if kt % 2 else "dve")
        self.proj_T(l, self.winTm_d[l, :, :], 256, evac)
        self.dmaq("pool", self.bim[:], self.cbim_d[:, :], [], ["bim"], "bim")
        for i in range(8):
            self.memset(self.maskM[i][:], 0.0, [("maskM", i)], eng="pool")
        self.dmaq("sp", self.mA[:], self.cmA_d[:, :], [], ["tabs"], "mA")
        self.dmaq("sp", self.mB[:], self.cmB_d[:, :], [], ["tabs"], "mB")
        self.dmaq("sp", self.mN[:], self.cmN_d[:, :], [], ["tabs"], "mN")
        sc.barrier()
        for h in range(4):
            src = V(pj[2 + h], 2048, 0, 128, 0, [(256, 8), (1, 256)])
            dstf = V(self.kmF, 32, 0, 128, h * 8, [(1, 8)])
            self.red(dstf, src, ALU.add, [("pjM", 2 + h, tt) for tt in range(4)], [("kmF", h)])
            dst = V(self.kmT, 32, 0, 128, h * 8, [(1, 8)])
            self.ts(dst, dstf, 1.0 / 256, None, ALU.mult, None, [("kmF", h)], [("kmT", h)])
        mask_of = {}
        gb = 7
        for h in range(4):
            qtile = pj[h // 2]
            qtoks = [("pjM", h // 2, tt) for tt in range(4)]
            for Q in (2, 3):
                mi = h * 2 + (Q - 2)
                mask_of[(h, Q)] = mi
                for j in range(4):
                    J = Q * 4 + j
                    self.mm(self.bank[gb][:, mi * 32 + j * 8: mi * 32 + (j + 1) * 8],
                            V(qtile, 2048, 0, 128, J * 128, [(1, 128)]),
                            V(self.kmT, 32, 0, 128, h * 8, [(1, 8)]), True, True,
                            qtoks + [("kmT", h)], [("bank", gb)])
        for h in range(4):
            for Q in (2, 3):
                mi = mask_of[(h, Q)]
                scv = V(self.small, 256, 0, 128, 0, [(1, 32)])
                Av = V(self.mA, 128, 0, 128, Q * 32, [(1, 32)])
                Bv = V(self.mB, 128, 0, 128, Q * 32, [(1, 32)])
                self.tt(scv, self.bank[gb][:, mi * 32:(mi + 1) * 32], Av, ALU.mult,
                        [("bank", gb), "tabs"], ["score"])
                self.tt(scv, scv, Bv, ALU.add, ["score", "tabs"], ["score"])
                self.rank_mask(lambda j, dims: V(self.small, 256, 0, 128, j * 8, dims), 8, 4, 3,
                               lambda j, Q=Q: V(self.mN, 128, 0, 128, Q * 32 + j * 8, [(1, 8)]),
                               lambda j, mi=mi: V(self.mbM, 256, 0, 128, mi * 32 + j * 8, [(1, 8)]),
                               "score", ("mbM", mi))

        def units(Qs):
            for h in range(4):
                qtile = pj[h // 2]
                ktile = pj[2 + h]
                qtoks = [("pjM", h // 2, tt) for tt in range(4)]
                ktoks = [("pjM", 2 + h, tt) for tt in range(4)]
                for Q in Qs:
                    mi = mask_of.get((h, Q))
                    ob = 3 + self.rot("obank", 3)

                    def extras(kt, j0, j1, Q=Q, h=h, mi=mi):
                        ex = self.toep_extras(h, Q, kt, j0, j1)
                        if mi is not None and (kt // 2) < (4 * Q + 3) // 2:
                            ex.append((j0 * 128, j1 * 128, self.bim[:, kt * 128:(kt + 1) * 128],
                                       self.maskM[mi][:, j0 * 128:j1 * 128], ["bim", ("maskM", mi)]))
                        return ex

                    def post(Q=Q, h=h, ob=ob):
                        rbase = 64 + self.rot("rs", 4) * 16
                        rs = V(self.small, 256, 0, 128, rbase, [(1, 4)])
                        self.recip(rs, self.osum(ob), [("bank", ob)], ["rs"])
                        rsb = V(self.small, 256, 0, 128, rbase, [(1, 4), (0, 64)])
                        dst = V(self.otok, 4096, 0, 128, Q * 4 * 256 + h * 64, [(256, 4), (1, 64)])
                        self.tt(dst, self.obv(ob), rsb, ALU.mult, [("bank", ob), "rs"],
                                [("otok", Q * 4 + j) for j in range(4)])
                    self.attn(Q, self.causal_kts(Q),
                              lambda kt, ktile=ktile: V(ktile, 2048, 0, 128, kt * 128, [(1, 128)]),
                              lambda c0, c1, qtile=qtile: V(qtile, 2048, 0, 128, c0, [(1, c1 - c0)]),
                              lambda kt, h=h: V(self.VM, 4160, 0, 128, kt * 260 + h * 65, [(1, 65)]),
                              65, SC8, extras, ob, ["VM"], extra_reads=qtoks + ktoks, post=post)
        units((0, 1))
        self.flush()
        for mi in range(8):
            for j in range(4):
                self.mm(self.bank[gb][0:8, j * 128:(j + 1) * 128],
                        V(self.mbM, 256, 0, 128, mi * 32 + j * 8, [(1, 8)]), self.ident[:], True, True,
                        [("mbM", mi), "ident"], [("bank", gb)])
            self.cp(self.maskM[mi][0:8, :], self.bank[gb][0:8, :], [("bank", gb)], [("maskM", mi)])
        units((2, 3))
        self.outproj(l, 0)

    def mixer(self, l):
        sc = self.sc
        sc.barrier()
        self.norm(3 * l + 1, mixer=True)
        if "moba" in self.branches:
            sc.barrier()
            self.prefetch_ok = True
            self.pass_moba(l)
        if "diff" in self.branches:
            sc.barrier()
            self.prefetch_ok = True
            self.pass_diff(l)
        if "nsa" in self.branches:
            sc.barrier()
            self.prefetch_ok = True
            self.pass_nsa(l)
        sc.barrier()

    def build(self, consts):
        self.declare()
        self.setup(consts)
        outtoks = []
        for s in range(self.NS):
            self.load_x(s)
            for l in range(self.L):
                if "ffn1" in self.parts:
                    self.norm(3 * l + 0)
                    self.ffn(l, 0)
                if "mix" in self.parts:
                    self.mixer(l)
                if "ffn2" in self.parts:
                    self.norm(3 * l + 2)
                    self.ffn(l, 1)
            if self.do_final:
                self.norm(3 * self.L, final=True)
            outtoks += self.store_out(s)
        self.sc.final_wait("sp", outtoks)
        cnt = self.sc.emit(self.nc, self.stack)
        self.stack.close()
        return self.nc, cnt

    def causal_kts(self, Q):
        return [(kt, max(0, kt - 4 * Q), 4) for kt in range(4 * Q + 4)]

    def toep_extras(self, hb, Q, kt, j0, j1):
        ex = []
        for j in range(j0, j1):
            dlt = 4 * Q + j - kt
            if dlt in (0, 1):
                ex.append((j * 128, (j + 1) * 128, self.ident[:], self.Tbv(hb, dlt), ["ident", "Tb"]))
        return ex

    def obv(self, ob):
        return V(self.bank[ob], 512, 0, 128, 0, [(128, 4), (1, 64)])

    def osum(self, ob):
        return V(self.bank[ob], 512, 0, 128, 64, [(128, 4)])

    def pass_diff(self, l):
        sc = self.sc
        lam_init = 0.8 - 0.6 * math.exp(-0.3 * (l + self.layer0))
        pj = self.pjD
        for t in range(3):
            self.proj_F(l, 4 + t, [(0, 128, pj[t], ("pjD", t))])
        for gi in range(8):
            self.memset(pj[3 + gi][:], 0.0, [("pjD", 3 + gi, tt) for tt in range(4)], eng="pool")
        for t in range(3):
            ev = []
            for slot in range(3):
                gi = 3 * t + slot
                if gi < 8:
                    ev.append((32 * slot, 32 * slot + 32, pj[3 + gi], ("pjD", 3 + gi)))
            self.proj_Fs(l, 7 + t, ev)
        self.memset(self.VD[:], 1.0, ["VD"])

        def evac(kt, bi):
            dst = V(self.VD, 4160, 0, 128, kt * 260, [(65, 4), (1, 64)])
            src = V(self.bank[bi], 512, 0, 128, 0, [(64, 4), (1, 64)])
            self.cp(dst, src, [("bank", bi)], ["VD"], eng="act" if kt % 2 else "dve")
        self.proj_T(l, self.winTd_d[l, :, :], 256, evac)
        self.dmaq("sp", self.dl_s[:], self.dl_d[l, :, :], [], ["dl_s"], "dl_s")
        self.dmaq("sp", self.sub_s[:], self.sub_d[l, :, :], [], ["sub_s"], "sub_s")
        sc.barrier()
        t0 = V(self.tmp[0], 256, 0, 128, 0, [(1, 32)])
        self.tt(t0, self.dl_s[:, 0:32], self.dl_s[:, 32:64], ALU.mult, ["dl_s"], ["tmp0"])
        self.red(self.lam[:, 0:1], t0, ALU.add, ["tmp0"], ["lam"])
        self.tt(t0, self.dl_s[:, 64:96], self.dl_s[:, 96:128], ALU.mult, ["dl_s", "lam"], ["tmp0"])
        self.red(self.lam[:, 1:2], t0, ALU.add, ["tmp0"], ["lam"])
        self.actf(self.lam[:, 2:4], self.lam[:, 0:2], AF.Exp, ["lam"], ["lam"])
        self.tt(self.lam[:, 4:5], self.lam[:, 2:3], self.lam[:, 3:4], ALU.subtract, ["lam"], ["lam"])
        self.ts(self.lam[:, 5:6], self.lam[:, 4:5], lam_init, -1.0, ALU.add, ALU.mult, ["lam"], ["lam"])
        sub_b = V(self.sub_s, 64, 0, 128, 0, [(0, 4), (1, 64)])
        tv = [V(self.tmp[i], 256, 0, 128, 0, [(64, 4), (1, 64)]) for i in range(4)]
        bc = lambda base: V(self.small, 256, 0, 128, base, [(1, 4), (0, 64)])
        for h in range(4):
            for Q in range(4):
                obs = [3 + self.rot("obankD", 4), 3 + self.rot("obankD", 4)]

                def post(h=h, Q=Q, obs=obs):
                    rb = 64 + self.rot("rs", 4) * 16
                    rs0 = V(self.small, 256, 0, 128, rb, [(1, 4)])
                    rs1 = V(self.small, 256, 0, 128, rb + 4, [(1, 4)])
                    nl = V(self.small, 256, 0, 128, rb + 8, [(1, 4)])
                    ssv = V(self.small, 256, 0, 128, rb + 12, [(1, 4)])
                    self.recip(rs0, self.osum(obs[0]), [("bank", obs[0])], ["rs"])
                    self.recip(rs1, self.osum(obs[1]), [("bank", obs[1])], ["rs"])
                    self.ts(nl, rs1, self.lam[:, 5:6], None, ALU.mult, None, ["rs", "lam"], ["rs"])
                    self.tt(tv[0], self.obv(obs[0]), bc(rb), ALU.mult, [("bank", obs[0]), "rs"], ["tmp0"])
                    self.tt(tv[1], self.obv(obs[1]), bc(rb + 8), ALU.mult, [("bank", obs[1]), "rs"], ["tmp1"])
                    self.tt(tv[0], tv[0], tv[1], ALU.add, ["tmp0", "tmp1"], ["tmp0"])
                    self.tt(tv[2], tv[0], tv[0], ALU.mult, ["tmp0"], ["tmp2"])
                    self.red(ssv, tv[2], ALU.add, ["tmp2"], ["rs"])
                    self.actf(ssv, ssv, AF.Sqrt, ["rs"], ["rs"], scale=1.0 / 64, bias=RMS_EPS)
                    self.recip(ssv, ssv, ["rs"], ["rs"])
                    self.stt(tv[1], tv[0], 1.0 - lam_init, bc(rb + 12), ALU.mult, ALU.mult,
                             ["tmp0", "rs"], ["tmp1"])
                    dst = V(self.otok, 4096, 0, 128, Q * 4 * 256 + h * 64, [(256, 4), (1, 64)])
                    self.tt(dst, tv[1], sub_b, ALU.mult, ["tmp1", "sub_s"],
                            [("otok", Q * 4 + j) for j in range(4)])
                for m in range(2):
                    gi = 2 * h + m
                    r0 = 32 * (gi % 3)
                    qtile = pj[gi // 3]
                    ktile = pj[3 + gi]
                    self.attn(Q, self.causal_kts(Q),
                              lambda kt, ktile=ktile: V(ktile, 2048, 0, 128, kt * 128, [(1, 128)]),
                              lambda c0, c1, qtile=qtile: V(qtile, 2048, 0, 128, c0, [(1, c1 - c0)]),
                              lambda kt, h=h: V(self.VD, 4160, 0, 128, kt * 260 + h * 65, [(1, 65)]),
                              65, SCD, lambda kt, j0, j1, h=h, Q=Q: self.toep_extras(4 + h, Q, kt, j0, j1),
                              obs[m], ["VD"], post=(post if m == 1 else None))
        self.outproj(l, 1)

    def pass_nsa(self, l):
        sc = self.sc
        pj = self.pjN
        self.proj_F(l, 18, [(0, 64, self.kcT[0], ("kcT", 0)), (64, 128, self.kcT[1], ("kcT", 1))])
        self.proj_F(l, 19, [(0, 64, self.kcT[2], ("kcT", 2)), (64, 128, self.kcT[3], ("kcT", 3))])
        self.memset(self.VS[:], 1.0, ["VS"])
        self.memset(self.VW[:], 1.0, ["VW"])

        def evac(kt, bi):
            bank = self.bank[bi]
            dbg = os.environ.get("NSA_DBG", "")
            if "a" in dbg:
                self.cp(self.gate_s[:, kt * 24:(kt + 1) * 24], bank[:, 256:280], [("bank", bi)], ["gate"])
                return
            if "c" in dbg:
                self.cp(V(self.VS, 2080, 0, 128, kt * 130, [(65, 2), (1, 64)]),
                        V(bank, 512, 0, 128, 0, [(64, 2), (1, 64)]), [("bank", bi)], ["VS"], eng="dve")
                self.cp(V(self.VW, 2080, 0, 128, kt * 130, [(65, 2), (1, 64)]),
                        V(bank, 512, 0, 128, 128, [(64, 2), (1, 64)]), [("bank", bi)], ["VW"], eng="dve")
                return
            if "d" in dbg:
                self.actf(self.gate_s[:, kt * 24:(kt + 1) * 24], bank[:, 256:280], AF.Tanh, [("bank", bi)], ["gate"], scale=0.5)
                return
            if "b" in dbg:
                self.cp(V(self.VS, 2080, 0, 128, kt * 130, [(65, 2), (1, 64)]),
                        V(bank, 512, 0, 128, 0, [(64, 2), (1, 64)]), [("bank", bi)], ["VS"], eng="dve")
                return
            self.cp(V(self.VS, 2080, 0, 128, kt * 130, [(65, 2), (1, 64)]),
                    V(bank, 512, 0, 128, 0, [(64, 2), (1, 64)]), [("bank", bi)], ["VS"], eng="dve")
            self.cp(V(self.VW, 2080, 0, 128, kt * 130, [(65, 2), (1, 64)]),
                    V(bank, 512, 0, 128, 128, [(64, 2), (1, 64)]), [("bank", bi)], ["VW"], eng="dve")
            self.cp(self.gate_s[:, kt * 24:(kt + 1) * 24], bank[:, 256:280], [("bank", bi)], ["gate"])
        stage = int(os.environ.get("NSA_STAGE", "99"))
        if stage < 1:
            return
        self.proj_T(l, self.winTn_d[l, :, :], 320, evac)
        self.actf(self.gate_s[:], self.gate_s[:], AF.Tanh, ["gate"], ["gate"], scale=0.5)
        self.ts(self.gate_s[:], self.gate_s[:], 0.5, 0.5, ALU.mult, ALU.add, ["gate"], ["gate"])
        if stage < 2:
            return
        self.dmaq("pool", self.mcmp[:], self.cmcmp_d[:, :], [], ["mcmp"], "mcmp")
        self.dmaq("pool", self.bin_[:], self.cbin_d[:, :], [], ["bin"], "bin")
        for i in range(2):
            self.memset(self.maskT[i][:], 0.0, [("maskT", i)], eng="pool")
        self.dmaq("pool", self.wedge[:], self.cwedge_d[:, :], [], ["wedge"], "wedge")
        self.dmaq("sp", self.nA[:], self.cnA_d[:, :], [], ["tabs"], "nA")
        self.dmaq("sp", self.nB[:], self.cnB_d[:, :], [], ["tabs"], "nB")
        self.memset(self.vcaug[:], 1.0, ["vcaug"])
        for g in range(2):
            self.dmaq("pool", self.vcaug[:, g * 97 + 65: g * 97 + 97], self.cov_d[:, :], ["vcaug"], ["vcaug"],
                      "vcaug")
        sc.barrier()
        if stage < 3:
            return
        self.dmaq("pool", self.w2k_s[:], self.w2k_d[l, :, :], [], ["w2k"], "w2k")
        self.dmaq("pool", self.w2v_s[:], self.w2v_d[l, :, :], [], ["w2v"], "w2v")
        self.dmaq("pool", self.peT_s[:], self.peT_d[l, :, :], [], ["peT"], "peT")
        b1first = True
        for i in range(2):
            ab = i
            afirst = True
            for lg in range(8):
                wi = self.rot("w1buf", 2)
                wb = self.w1buf[wi]
                self.dmaq("pool", wb[:], self.w1_d[l, i, lg, :, :], [], [("w1buf", wi)], "w1buf%d" % wi)
                for ll in range(4):
                    li = lg * 4 + ll
                    for hc in range(2):
                        lhsT = wb[0:64, ll * 256 + hc * 128: ll * 256 + (hc + 1) * 128]
                        for g in range(2):
                            self.mm(self.bank[ab][:, (g * 2 + hc) * 128:(g * 2 + hc) * 128 + 127], lhsT,
                                    V(self.kcT[i * 2 + g], 2048, 0, 64, li, [(16, 127)]),
                                    afirst, li == 31, [("w1buf", wi)], [("bank", ab)])
                            afirst = False
                        self.mm(self.bank[2][:, i * 2 + hc: i * 2 + hc + 1], lhsT,
                                self.peT_s[0:64, i * 32 + li: i * 32 + li + 1],
                                b1first, li == 31, [("w1buf", wi), "peT"], [("bank", 2)])
                        b1first = False
            self.cp(self.b1_s[:, 0:2], self.bank[2][:, i * 2: i * 2 + 2], [("bank", 2)], ["b1"])
            for g in range(2):
                for hc in range(2):
                    acc = self.bank[ab][:, (g * 2 + hc) * 128:(g * 2 + hc) * 128 + 127]
                    xs, x2, sgm = self.gx[0][:, 0:127], self.gx[1][:, 0:127], self.gx[2][:, 0:127]
                    self.actf(xs, acc, AF.Identity, [("bank", ab), "b1"], ["gx0"], bias=self.b1_s[:, hc:hc + 1])
                    self.tt(x2, xs, xs, ALU.mult, ["gx0"], ["gx1"])
                    self.ts(x2, x2, 0.044715, 1.0, ALU.mult, ALU.add, ["gx1"], ["gx1"])
                    self.tt(x2, x2, xs, ALU.mult, ["gx1", "gx0"], ["gx1"])
                    self.actf(sgm, x2, AF.Tanh, ["gx1"], ["gx2"], scale=GELU_C * 0.5)
                    self.ts(sgm, sgm, 0.5, 0.5, ALU.mult, ALU.add, ["gx2"], ["gx2"])
                    hdst = V(self.HT, 512, 0, 128, (g * 2 + hc) * 128, [(1, 127)])
                    self.tt(hdst, xs, sgm, ALU.mult, ["gx0", "gx2"], [("HT", g, hc)])
            for g in range(2):
                if i == 0:
                    for hc in range(2):
                        self.mm(self.bank[3][:, g * 128: g * 128 + 127], self.w2k_s[:, hc * 128:(hc + 1) * 128],
                                V(self.HT, 512, 0, 128, (g * 2 + hc) * 128, [(1, 127)]),
                                g == 0 and hc == 0, hc == 1, ["w2k", ("HT", g, hc)], [("bank", 3)])
                else:
                    for hc in range(2):
                        self.mm(self.bank[4][0:127, g * 64:(g + 1) * 64],
                                V(self.HT, 512, 0, 128, (g * 2 + hc) * 128, [(1, 127)]),
                                self.w2v_s[:, hc * 64:(hc + 1) * 64],
                                g == 0 and hc == 0, hc == 1, ["w2v", ("HT", g, hc)], [("bank", 4)])
            if i == 0:
                for g in range(2):
                    self.cp(self.kcmpT[:, g * 128: g * 128 + 127], self.bank[3][:, g * 128: g * 128 + 127],
                            [("bank", 3)], ["kcmpT"])
            else:
                for g in range(2):
                    self.cp(self.vcaug[0:127, g * 97: g * 97 + 64], self.bank[4][0:127, g * 64:(g + 1) * 64],
                            [("bank", 4)], ["vcaug"])
        sc.barrier()
        if stage < 4:
            return
        bc64 = lambda base: V(self.small, 256, 0, 128, base, [(1, 4), (0, 64)])
        bc32 = lambda base: V(self.small, 256, 0, 128, base, [(1, 4), (0, 32)])
        tv = [V(self.tmp[i], 256, 0, 128, 0, [(64, 4), (1, 64)]) for i in range(4)]
        for g in range(2):
            self.pbank_all = False
            for t in range(2, 6):
                self.memset(pj[t][:], 0.0, [("pjN", t, tt) for tt in range(4)], eng="pool")
            for t in range(2):
                self.proj_Fs(l, 10 + 2 * g + t, [(0, 128, pj[t], ("pjN", t))])
            self.proj_Fs(l, 14 + g, [(0, 64, pj[2], ("pjN", 2)), (64, 128, pj[3], ("pjN", 3))])
            self.proj_Fs(l, 16 + g, [(0, 64, pj[4], ("pjN", 4)), (64, 128, pj[5], ("pjN", 5))])
            self.pbank_all = True
            alltoks = [("pjN", t, tt) for t in range(6) for tt in range(4)]
            for Q in range(4):
                for r in range(4):
                    h = 4 * g + r
                    r0 = 64 * (h % 2)
                    qtile = pj[r // 2]
                    ob = 3 + self.rot("obank", 3)

                    def post(r=r, h=h, ob=ob, Q=Q):
                        rb = 64 + self.rot("rs", 4) * 16
                        rs = V(self.small, 256, 0, 128, rb, [(1, 4)])
                        fv = V(self.small, 256, 0, 128, rb + 4, [(1, 4)])
                        self.ts(rs, self.osum(ob), 1e-30, None, ALU.add, None, [("bank", ob)], ["rs"])
                        self.recip(rs, rs, ["rs"], ["rs"])
                        impv = V(self.bank[ob], 512, 0, 128, 65, [(128, 4), (1, 32)])
                        iacc = V(self.impacc, 128, 0, 128, 0, [(32, 4), (1, 32)])
                        if r == 0:
                            self.tt(iacc, impv, bc32(rb), ALU.mult, [("bank", ob), "rs"], ["impacc"])
                        else:
                            itmp = V(self.tmp[3], 256, 0, 128, 0, [(32, 4), (1, 32)])
                            self.tt(itmp, impv, bc32(rb), ALU.mult, [("bank", ob), "rs"], ["tmp3"])
                            self.tt(iacc, iacc, itmp, ALU.add, ["impacc", "tmp3"], ["impacc"])
                        gv = V(self.gate_s, 384, 0, 128, Q * 96 + h * 3 + 0, [(24, 4)])
                        self.tt(fv, rs, gv, ALU.mult, ["rs", "gate"], ["rs"])
                        oc = V(self.ocomb, 1024, 0, 128, r * 256, [(64, 4), (1, 64)])
                        self.tt(oc, self.obv(ob), bc64(rb + 4), ALU.mult, [("bank", ob), "rs"], [("ocomb", r)])
                    self.attn(Q, [(0, 0, 4)],
                              lambda kt, r0=r0, g=g: V(self.kcmpT, 256, r0, 64, g * 128, [(1, 127)]),
                              lambda c0, c1, qtile=qtile, r0=r0: V(qtile, 2048, r0, 64, c0, [(1, c1 - c0)]),
                              lambda kt, g=g: V(self.vcaug, 194, 0, 127, g * 97, [(1, 97)]),
                              97, SC8,
                              lambda kt, j0, j1, Q=Q: [(0, 512, self.ident[0:127, 0:127],
                                                        self.mcmp[0:127, Q * 512:(Q + 1) * 512],
                                                        ["ident", "mcmp"])],
                              ob, ["vcaug"], kparts=127, extra_reads=["kcmpT"] + alltoks, post=post)
                if stage < 5:
                    continue
                mi = None
                if Q >= 2:
                    self.flush()
                    scv = V(self.impacc, 128, 0, 128, 0, [(1, 128)])
                    self.tt(scv, scv, self.nA[:, Q * 128:(Q + 1) * 128], ALU.mult, ["impacc", "tabs"], ["impacc"])
                    self.tt(scv, scv, self.nB[:, Q * 128:(Q + 1) * 128], ALU.add, ["impacc", "tabs"], ["impacc"])
                    self.rank_mask(lambda j, dims: V(self.impacc, 128, 0, 128, j * 32, dims), 32, 4, 16,
                                   NEG, lambda j: V(self.mb, 128, 0, 128, j * 32, [(1, 32)]), "impacc", "mb")
                    mi = self.rot("maskT", 2)
                if stage < 6:
                    continue
                for r in range(4):
                    h = 4 * g + r
                    r0 = 64 * (h % 2)
                    qtile = pj[r // 2]
                    ktile = pj[4 + (r % 2)]
                    ob = 3 + self.rot("obank", 3)
                    kts = []
                    for kt in range(max(0, 4 * Q - 4), 4 * Q + 4):
                        j0 = max(0, kt - 4 * Q)
                        j1 = min(4, kt - 4 * Q + 5)
                        kts.append((kt, j0, j1))

                    def extras(kt, j0, j1, h=h, Q=Q):
                        ex = self.toep_extras(8 + h, Q, kt, j0, j1)
                        for j in range(j0, j1):
                            if 4 * Q + j - kt == 4:
                                ex.append((j * 128, (j + 1) * 128, self.ident[:], self.wedge[:],
                                           ["ident", "wedge"]))
                        return ex

                    def post(r=r, h=h, ob=ob, Q=Q):
                        rb = 64 + self.rot("rs", 4) * 16
                        rs = V(self.small, 256, 0, 128, rb, [(1, 4)])
                        fv = V(self.small, 256, 0, 128, rb + 4, [(1, 4)])
                        self.recip(rs, self.osum(ob), [("bank", ob)], ["rs"])
                        gv = V(self.gate_s, 384, 0, 128, Q * 96 + h * 3 + 2, [(24, 4)])
                        self.tt(fv, rs, gv, ALU.mult, ["rs", "gate"], ["rs"])
                        oc = V(self.ocomb, 1024, 0, 128, r * 256, [(64, 4), (1, 64)])
                        self.tt(tv[1], self.obv(ob), bc64(rb + 4), ALU.mult, [("bank", ob), "rs"], ["tmp1"])
                        self.tt(oc, oc, tv[1], ALU.add, [("ocomb", r), "tmp1"], [("ocomb", r)])
                    self.attn(Q, kts,
                              lambda kt, ktile=ktile: V(ktile, 2048, 0, 128, kt * 128, [(1, 128)]),
                              lambda c0, c1, qtile=qtile: V(qtile, 2048, 0, 128, c0, [(1, c1 - c0)]),
                              lambda kt, g=g: V(self.VW, 2080, 0, 128, kt * 130 + g * 65, [(1, 65)]),
                              65, SC8, extras, ob, ["VW"], extra_reads=alltoks, post=post)
                if stage < 7:
                    continue
                if mi is not None:
                    self.flush()
                    for j in range(4):
                        self.mm(self.bank[7][0:32, j * 128:(j + 1) * 128],
                                V(self.mb, 128, 0, 128, j * 32, [(1, 32)]), self.ident[:], True, True,
                                ["mb", "ident"], [("bank", 7)])
                    self.cp(self.maskT[mi][0:32, :], self.bank[7][0:32, :], [("bank", 7)], [("maskT", mi)])
                for r in range(4):
                    h = 4 * g + r
                    r0 = 64 * (h % 2)
                    qtile = pj[r // 2]
                    ktile = pj[2 + (r % 2)]
                    ob = 3 + self.rot("obank", 3)

                    def extras(kt, j0, j1, h=h, mi=mi, Q=Q):
                        ex = self.toep_extras(8 + h, Q, kt, j0, j1)
                        if mi is not None:
                            ex.append((j0 * 128, j1 * 128, self.bin_[:, kt * 128:(kt + 1) * 128],
                                       self.maskT[mi][:, j0 * 128:j1 * 128], ["bin", ("maskT", mi)]))
                        return ex

                    def post(r=r, h=h, ob=ob, Q=Q):
                        rb = 64 + self.rot("rs", 4) * 16
                        rs = V(self.small, 256, 0, 128, rb, [(1, 4)])
                        fv = V(self.small, 256, 0, 128, rb + 4, [(1, 4)])
                        self.recip(rs, self.osum(ob), [("bank", ob)], ["rs"])
                        gv = V(self.gate_s, 384, 0, 128, Q * 96 + h * 3 + 1, [(24, 4)])
                        self.tt(fv, rs, gv, ALU.mult, ["rs", "gate"], ["rs"])
                        oc = V(self.ocomb, 1024, 0, 128, r * 256, [(64, 4), (1, 64)])
                        self.tt(tv[0], self.obv(ob), bc64(rb + 4), ALU.mult, [("bank", ob), "rs"], ["tmp0"])
                        dst = V(self.otok, 4096, 0, 128, Q * 4 * 256 + r * 64, [(256, 4), (1, 64)])
                        self.tt(dst, oc, tv[0], ALU.add, [("ocomb", r), "tmp0"],
                                [("otok", Q * 4 + j) for j in range(4)])
                    self.attn(Q, self.causal_kts(Q),
                              lambda kt, ktile=ktile: V(ktile, 2048, 0, 128, kt * 128, [(1, 128)]),
                              lambda c0, c1, qtile=qtile: V(qtile, 2048, 0, 128, c0, [(1, c1 - c0)]),
                              lambda kt, g=g: V(self.VS, 2080, 0, 128, kt * 130 + g * 65, [(1, 65)]),
                              65, SC8, extras, ob, ["VS"], extra_reads=alltoks, post=post)
            self.flush()
            if stage >= 8:
                self.outproj(l, 2 + g)


def _rel_bucket(dist):
    n = np.maximum(dist, 0)
    max_exact = 16
    n_f = np.maximum(n, max_exact).astype(np.float32)
    large = max_exact + (np.log(n_f / np.float32(max_exact)) / np.float32(np.log(128 / 16))
                         * np.float32(16)).astype(np.int32)
    return np.where(n < max_exact, n, np.minimum(large, 31))


def make_consts():
    c = {}
    k = np.arange(128)[:, None]
    q = np.arange(128)[None, :]
    E = np.zeros((128, 2, 32, 128), np.float32)
    for di in range(2):
        dist = di * 128 + q - k
        bk = _rel_bucket(dist)
        for b in range(32):
            E[:, di, b, :] = ((bk == b) & (dist >= 0)).astype(np.float32)
    c["E_nonzero"] = [[bool(E[:, di, b, :].any()) for b in range(32)] for di in range(2)]
    c["c_E"] = E.reshape(128, -1)
    c["c_ident"] = np.eye(128, dtype=np.float32)
    c["c_caus"] = np.where(q >= k, 0.0, NEG).astype(np.float32)
    c["c_wedge"] = np.where(q < k, 0.0, NEG).astype(np.float32)
    n = np.arange(128)[:, None]
    qq = np.arange(S)[None, :]
    mc = np.where(16 * n + 31 <= qq, 0.0, NEG).astype(np.float32)
    mc[127, :] = 0.0
    c["c_mcmp"] = mc
    kk = np.arange(S)[None, :]
    c["c_bim"] = (kk // 256 == np.arange(128)[:, None]).astype(np.float32)
    c["c_bin"] = (kk // 64 == np.arange(128)[:, None]).astype(np.float32)
    p = np.arange(128)[:, None, None]
    J = np.arange(16)[None, :, None]
    nn = np.arange(8)[None, None, :]
    own = (J * 128 + p) // 256
    A = (nn < own).astype(np.float32)
    B = np.where(nn < own, 0.0, -1e9 - 1e6 * nn).astype(np.float32)
    c["c_mA"] = A.reshape(128, 128)
    c["c_mB"] = np.broadcast_to(B, (128, 16, 8)).reshape(128, 128).copy()
    c["c_mN"] = (NEG * A).reshape(128, 128).astype(np.float32)
    jj = np.arange(32)[None, None, :]
    cur = (J * 128 + p) // 64
    valid = jj <= cur
    forced = (jj == 0) | (jj > cur - 2)
    A = (valid & ~forced).astype(np.float32)
    B = np.where(valid & forced, 1e4 + jj, np.where(~valid, -1e9 - 1e6 * jj, 0.0)).astype(np.float32)
    c["c_nA"] = A.reshape(128, 512)
    c["c_nB"] = np.broadcast_to(B, (128, 16, 32)).reshape(128, 512).copy()
    n_cmp = 127
    cs = np.arange(n_cmp) * 16
    ss = np.arange(32) * 64
    ov = np.clip(np.minimum(cs[:, None] + 32, ss[None, :] + 64) - np.maximum(cs[:, None], ss[None, :]),
                 0, None) / 32
    ovp = np.zeros((128, 32), np.float32)
    ovp[:127] = ov
    c["c_ov"] = ovp
    return c


_CONSTS = None


def _consts():
    global _CONSTS
    if _CONSTS is None:
        _CONSTS = make_consts()
    return _CONSTS


def _lay_gu(g, u):
    L = g.shape[0]
    a = np.stack([g, u], axis=1)
    a = a.reshape(L, 2, KC, 128, NFC, 128)
    a = a.transpose(0, 4, 3, 1, 2, 5)
    return np.ascontiguousarray(a).reshape(L, NFC, 128, 2 * KC * 128)


def _lay_wd(w):
    L = w.shape[0]
    a = w.reshape(L, NFC, 128, KC, 128)
    a = a.transpose(0, 3, 2, 1, 4)
    return np.ascontiguousarray(a).reshape(L, KC, 128, NFC * 128)


def _lay_gains(vecs):
    cols = [v.reshape(KC, 128).T for v in vecs]
    return np.ascontiguousarray(np.concatenate(cols, axis=1)).astype(np.float32)


def _win_tiles():
    tiles = []
    for t in range(2):
        tiles.append(list(range(0 + 128 * t, 128 * (t + 1))))
    for t in range(2):
        tiles.append(list(range(256 + 128 * t, 256 + 128 * (t + 1))))
    for base in (768, 1024):
        for t in range(3):
            cols = []
            for slot in range(4):
                gi = t * 3 + slot
                if slot < 3 and gi < 8:
                    cols += list(range(base + gi * 32, base + gi * 32 + 32))
                else:
                    cols += [-1] * 32
            tiles.append(cols)
    for t in range(4):
        tiles.append(list(range(1536 + 128 * t, 1536 + 128 * (t + 1))))
    for base in (2304, 2560):
        for g in range(2):
            cc = list(range(base + 64 * g, base + 64 * g + 64))
            tiles.append(cc + cc)
    tiles.append(list(range(2048, 2176)))
    tiles.append(list(range(2176, 2304)))
    return tiles


def _lay_cols(w, cols):
    L = w.shape[0]
    idx = np.array(cols)
    sel = w[:, :, np.where(idx < 0, 0, idx)].copy()
    if (idx < 0).any():
        sel[:, :, idx < 0] = 0.0
    a = sel.reshape(L, KC, 128, len(cols)).transpose(0, 2, 1, 3)
    return np.ascontiguousarray(a)


def prep_inputs(p, L, layer0=0):
    sl = slice(layer0, layer0 + L)
    c = _consts()
    m = {}
    gl = []
    for l in range(layer0, layer0 + L):
        gl += [p["norm_ffn1"][l], p["norm_mix"][l], p["norm_ffn2"][l]]
    gl.append(p["final_norm"])
    m["gains"] = _lay_gains(gl)
    m["gu1"] = _lay_gu(p["ffn1_gate"][sl], p["ffn1_up"][sl])
    m["gu2"] = _lay_gu(p["ffn2_gate"][sl], p["ffn2_up"][sl])
    m["wd1"] = _lay_wd(p["ffn1_down"][sl])
    m["wd2"] = _lay_wd(p["ffn2_down"][sl])
    w_in = p["w_in"][sl]
    tiles = _win_tiles()
    m["winF"] = np.ascontiguousarray(
        np.stack([_lay_cols(w_in, t).reshape(L, 128, KC * 128) for t in tiles], axis=1))
    m["winTm"] = _lay_cols(w_in, list(range(512, 768))).reshape(L, 128, KC * 256)
    m["winTd"] = _lay_cols(w_in, list(range(1280, 1536))).reshape(L, 128, KC * 256)
    ncols = list(range(2432, 2560)) + list(range(2688, 2816)) + list(range(2816, 2840)) + [-1] * 40
    m["winTn"] = _lay_cols(w_in, ncols).reshape(L, 128, KC * 320)
    wo = p["w_out"][sl].reshape(L, 4, 2, 128, 1024).transpose(0, 1, 3, 2, 4)
    m["woutp"] = np.ascontiguousarray(wo).reshape(L, 4, 128, 2048)
    w1 = p["nsa_cmp_w1"][sl].reshape(L, 2, 8, 4, 64, 256).transpose(0, 1, 2, 4, 3, 5)
    m["w1p"] = np.ascontiguousarray(w1).reshape(L, 2, 8, 64, 1024)
    w2 = p["nsa_cmp_w2"][sl]
    w2k = w2[:, 0].reshape(L, 2, 128, 64).transpose(0, 2, 1, 3)
    w2k = np.concatenate([w2k, w2k], axis=3)
    m["w2k"] = np.ascontiguousarray(w2k).reshape(L, 128, 256)
    w2v = w2[:, 1].reshape(L, 2, 128, 64).transpose(0, 2, 1, 3)
    m["w2v"] = np.ascontiguousarray(w2v).reshape(L, 128, 128)
    pe = p["nsa_cmp_pe"][sl]
    m["peT"] = np.ascontiguousarray(pe.transpose(0, 3, 1, 2)).reshape(L, 64, 64)
    dl = p["diff_lambda"][sl].reshape(L, 1, 128)
    m["dlrep"] = np.ascontiguousarray(np.broadcast_to(dl, (L, 128, 128)))
    sub = p["diff_subln"][sl].reshape(L, 1, 64)
    m["subrep"] = np.ascontiguousarray(np.broadcast_to(sub, (L, 128, 64)))
    m["tblrep"] = np.ascontiguousarray(np.broadcast_to(p["rel_bias"].reshape(1, 512), (128, 512)))
    for k_, v_ in c.items():
        if k_.startswith("c_"):
            m[k_] = v_
    return {k_: np.ascontiguousarray(v_, dtype=np.float32) for k_, v_ in m.items()}


_PROG_CACHE = {}


def _get_prog(n_layers, n_seq, do_final, parts, branches, layer0):
    key = (n_layers, n_seq, do_final, tuple(parts), tuple(branches), layer0)
    if key not in _PROG_CACHE:
        b = Builder(n_layers, n_seq, do_final, parts, branches, layer0)
        _PROG_CACHE[key] = b.build(_consts())
    return _PROG_CACHE[key]


def run_layers(x, p, n_layers, n_seq_per_core, do_final=True, parts=("ffn1", "mix", "ffn2"),
               branches=("moba", "diff", "nsa"), layer0=0, core_ids=None):
    L = n_layers
    nc, cnt = _get_prog(L, n_seq_per_core, do_final, parts, branches, layer0)
    B = x.shape[0]
    ncores = B // n_seq_per_core
    xT = np.ascontiguousarray(np.transpose(x, (0, 2, 1)))
    shared = prep_inputs(p, L, layer0)
    in_maps = []
    for c in range(ncores):
        m = dict(shared)
        m["xT"] = xT[c * n_seq_per_core:(c + 1) * n_seq_per_core]
        in_maps.append(m)
    res = run_bass_kernel_spmd(nc, in_maps, core_ids=list(range(ncores)) if core_ids is None else core_ids)
    o = np.concatenate([r["outT"] for r in res.results], axis=0)
    return np.ascontiguousarray(np.transpose(o, (0, 2, 1)))


def kernel(**inputs):
    x = np.asarray(inputs["x"], dtype=np.float32)
    p = {k: np.asarray(v, dtype=np.float32) for k, v in inputs.items() if k != "x"}
    return run_layers(x, p, DEPTH, x.shape[0] // NCORES)
```

```python
import math
import os
import numpy as np
import concourse.bass as bass
import concourse.mybir as mybir
from concourse.bass_utils import run_bass_kernel_spmd
from contextlib import ExitStack

F32 = mybir.dt.float32
BF16 = mybir.dt.bfloat16
AF = mybir.ActivationFunctionType
ALU = mybir.AluOpType
AX = mybir.AxisListType

S = 2048
D = 1024
KC = 8
FF = 2816
NFC = 22
DEPTH = 4
NCORES = 8
RMS_EPS = 1e-6
NEG = -30000.0


SAME_ENG_NOSYNC = ("pe",)


class _Op:
    __slots__ = ("eng", "fn", "reads", "writes", "isdma", "key", "n", "deps",
                 "signal", "val", "waits")


class Sched:
    ENGS = ("pe", "act", "dve", "pool", "sp")

    def __init__(self):
        self.ops = []
        self.per_eng = {e: [] for e in self.ENGS}
        self.last_w = {}
        self.readers = {}
        self.dma_count = {}
        self.pending_barrier = {e: None for e in self.ENGS}
        self.pending_old = {e: None for e in self.ENGS}
        self.last_op = {e: None for e in self.ENGS}
        self.live_dma = []

    def _mk(self, eng, fn, reads, writes, isdma, key, nobarrier=False):
        op = _Op()
        op.eng = eng
        op.fn = fn
        op.reads = tuple(reads)
        op.writes = tuple(writes)
        op.isdma = isdma
        op.key = key
        op.deps = []
        op.signal = False
        op.val = 0
        op.waits = []
        deps = set()
        for t in op.reads:
            w = self.last_w.get(t)
            if w is not None:
                deps.add((w, "raw"))
        for t in op.writes:
            w = self.last_w.get(t)
            if w is not None:
                deps.add((w, "waw"))
            for r in self.readers.get(t, ()):
                deps.add((r, "war"))
        if nobarrier:
            for o in (self.pending_old[eng] or ()):
                deps.add((o, "raw"))
        else:
            pb = self.pending_barrier[eng]
            if pb is not None:
                for o in pb:
                    deps.add((o, "raw"))
                self.pending_barrier[eng] = None
                self.pending_old[eng] = None
        for (p, kind) in deps:
            if p is op:
                continue
            if (not p.isdma) and (not isdma) and p.eng == eng:
                if kind != "raw" or eng in SAME_ENG_NOSYNC:
                    continue
            op.deps.append(p)
        for t in op.writes:
            self.last_w[t] = op
            self.readers[t] = []
        for t in op.reads:
            self.readers.setdefault(t, []).append(op)
        if isdma:
            c = self.dma_count.get(key, 0) + 1
            self.dma_count[key] = c
            op.val = 16 * c
            op.signal = True
            self.live_dma.append(op)
        self.ops.append(op)
        self.per_eng[eng].append(op)
        if not isdma:
            self.last_op[eng] = op
        return op

    def add(self, eng, fn, reads=(), writes=()):
        return self._mk(eng, fn, reads, writes, False, None)

    def dma(self, queue, fn, reads=(), writes=(), key=None, nobarrier=False):
        assert key is not None
        return self._mk(queue, fn, reads, writes, True, key, nobarrier)

    def barrier(self):
        lst = [o for o in self.last_op.values() if o is not None]
        lst += self.live_dma
        self.live_dma = []
        for e in self.ENGS:
            prev = self.pending_barrier[e]
            self.pending_old[e] = list(prev) if prev else None
            self.pending_barrier[e] = list(lst) + (prev or [])

    def final_wait(self, eng, tokens):
        return self._mk(eng, None, tokens, (), False, None)

    def analyse(self):
        seqno = {}
        cnt = {e: 0 for e in self.ENGS}
        for op in self.ops:
            if not op.isdma:
                cnt[op.eng] += 1
                seqno[id(op)] = cnt[op.eng]
        seen = {e: {} for e in self.ENGS}
        need = []
        for op in self.ops:
            sd = seen[op.eng]
            best = {}
            for p in op.deps:
                if p.isdma:
                    k = ("d", p.key)
                    v = p.val
                else:
                    k = ("e", p.eng)
                    v = seqno[id(p)]
                if sd.get(k, 0) >= v:
                    continue
                if k not in best or best[k][0] < v:
                    best[k] = (v, p)
            lst = []
            for k, (v, p) in best.items():
                sd[k] = v
                p.signal = True
                lst.append(p)
            need.append(lst)
        cnt = {e: 0 for e in self.ENGS}
        for op in self.ops:
            if (not op.isdma) and op.signal:
                cnt[op.eng] += 1
                op.val = cnt[op.eng]
        for op, lst in zip(self.ops, need):
            op.waits = [(("d", p.key) if p.isdma else ("e", p.eng), p.val) for p in lst]
        return cnt

    def emit(self, nc, stack):
        cnt = self.analyse()
        sems = {}
        for e in self.ENGS:
            sems[("e", e)] = stack.enter_context(nc.semaphore("s_" + e))
        for k in self.dma_count:
            sems[("d", k)] = stack.enter_context(nc.semaphore("d_" + str(k)))
        block = stack.enter_context(nc.Block())

        def run(engname):
            def body(eng):
                for op in self.per_eng[engname]:
                    for (k, v) in op.waits:
                        eng.wait_ge(sems[k], v)
                    if op.fn is None:
                        continue
                    inst = op.fn(eng)
                    if op.isdma:
                        inst.then_inc(sems[("d", op.key)], 16)
                    elif op.signal:
                        inst.then_inc(sems[("e", engname)], 1)
            return body

        block.tensor(run("pe"))
        block.scalar(run("act"))
        block.vector(run("dve"))
        block.gpsimd(run("pool"))
        block.sync(run("sp"))
        return cnt


SB_BASE = 16512
SB_END = 229376

SC8 = 0.125
SCD = 32.0 ** -0.5
GELU_C = 1.5957691216057308


def _nbytes(dt):
    return 4 if dt == F32 else 2


def V(t, ftot, p0, npart, f0, dims):
    return bass.AP(t, p0 * ftot + f0, [[ftot, npart]] + [[s, c] for (s, c) in dims])


class Builder:
    def __init__(self, n_layers, n_seq, do_final=True, parts=("ffn1", "mix", "ffn2"),
                 branches=("moba", "diff", "nsa"), layer0=0):
        self.L = n_layers
        self.NS = n_seq
        self.do_final = do_final
        self.parts = parts
        self.branches = branches
        self.layer0 = layer0
        self.nc = bass.Bass("TRN2", target_bir_lowering=False)
        self.sc = Sched()
        self.stack = ExitStack()
        self.off = SB_BASE
        self.nalloc = 0
        self.rr = {}
        self.pq = []
        self.pbank_all = True
        self.prefetch_ok = False
        self.deferred = None

    def din(self, name, shape, dt=F32):
        return self.nc.dram_tensor(name, list(shape), dt, kind="ExternalInput").ap()

    def dout(self, name, shape, dt=F32):
        return self.nc.dram_tensor(name, list(shape), dt, kind="ExternalOutput").ap()

    def sb(self, name, shape, dt):
        n = 1
        for s_ in shape[1:]:
            n *= s_
        nb = (n * _nbytes(dt) + 63) // 64 * 64
        off = self.off
        self.off += nb
        assert self.off <= SB_END, ("SBUF overflow", name, self.off)
        self.nalloc += 1
        return self.nc.alloc_sbuf_tensor_at("%s_%d" % (name, self.nalloc), list(shape), dt, offset=off)

    def ps(self, name, shape, dt=F32):
        return self.stack.enter_context(self.nc.psum_tensor(name, list(shape), dt))

    def rot(self, name, n):
        i = self.rr.get(name, 0)
        self.rr[name] = i + 1
        return i % n

    def mm(self, out, lhsT, rhs, start, stop, reads, writes):
        self.sc.add("pe", lambda e: e.matmul(out, lhsT, rhs, start=start, stop=stop,
                                             skip_group_check=True), reads, writes)

    def actf(self, out, in_, func, reads, writes, scale=1.0, bias=0.0):
        self.sc.add("act", lambda e: e.activation(out=out, in_=in_, func=func, bias=bias, scale=scale),
                    reads, writes)

    def tt(self, out, in0, in1, op, reads, writes, eng="dve"):
        self.sc.add(eng, lambda e: e.tensor_tensor(out=out, in0=in0, in1=in1, op=op), reads, writes)

    def ts(self, out, in0, s1, s2, op0, op1, reads, writes, eng="dve"):
        if op1 is None:
            self.sc.add(eng, lambda e: e.tensor_scalar(out=out, in0=in0, scalar1=s1, scalar2=None, op0=op0),
                        reads, writes)
        else:
            self.sc.add(eng, lambda e: e.tensor_scalar(out=out, in0=in0, scalar1=s1, scalar2=s2,
                                                        op0=op0, op1=op1), reads, writes)

    def stt(self, out, in0, scalar, in1, op0, op1, reads, writes, eng="dve"):
        self.sc.add(eng, lambda e: e.scalar_tensor_tensor(out=out, in0=in0, scalar=scalar, in1=in1,
                                                           op0=op0, op1=op1), reads, writes)

    def cp(self, out, in_, reads, writes, eng="dve"):
        if eng == "act":
            self.actf(out, in_, AF.Copy, reads, writes)
        else:
            self.sc.add(eng, lambda e: e.tensor_copy(out=out, in_=in_), reads, writes)

    def recip(self, out, in_, reads, writes):
        self.sc.add("dve", lambda e: e.reciprocal(out=out, in_=in_), reads, writes)

    def red(self, out, in_, op, reads, writes):
        self.sc.add("dve", lambda e: e.tensor_reduce(out=out, in_=in_, axis=AX.X, op=op), reads, writes)

    def memset(self, ap, val, writes, eng="dve"):
        self.sc.add(eng, lambda e: e.memset(ap, val), (), writes)

    def dmaq(self, queue, out, in_, reads, writes, key, nobarrier=False):
        self.sc.dma(queue, lambda e: e.dma_start(out=out, in_=in_), reads, writes, key, nobarrier)

    def declare(self):
        L, NS = self.L, self.NS
        d = self.din
        self.xT_d = d("xT", [NS, D, S])
        self.out_d = self.dout("outT", [NS, D, S])
        self.gains_d = d("gains", [128, (3 * L + 1) * KC])
        self.gu_d = [d("gu%d" % i, [L, NFC, 128, 2 * KC * 128]) for i in (1, 2)]
        self.wd_d = [d("wd%d" % i, [L, KC, 128, NFC * 128]) for i in (1, 2)]
        self.winF_d = d("winF", [L, 20, 128, KC * 128])
        self.winTm_d = d("winTm", [L, 128, KC * 256])
        self.winTd_d = d("winTd", [L, 128, KC * 256])
        self.winTn_d = d("winTn", [L, 128, KC * 320])
        self.wout_d = d("woutp", [L, 4, 128, 2 * 1024])
        self.w1_d = d("w1p", [L, 2, 8, 64, 4 * 256])
        self.w2k_d = d("w2k", [L, 128, 2 * 128])
        self.w2v_d = d("w2v", [L, 128, 2 * 64])
        self.peT_d = d("peT", [L, 64, 2 * 32])
        self.dl_d = d("dlrep", [L, 128, 128])
        self.sub_d = d("subrep", [L, 128, 64])
        self.tbl_d = d("tblrep", [128, 512])
        self.cE_d = d("c_E", [128, 2 * 32 * 128])
        self.cid_d = d("c_ident", [128, 128])
        self.ccaus_d = d("c_caus", [128, 128])
        self.cwedge_d = d("c_wedge", [128, 128])
        self.cmcmp_d = d("c_mcmp", [128, 2048])
        self.cbim_d = d("c_bim", [128, 2048])
        self.cbin_d = d("c_bin", [128, 2048])
        self.cmA_d = d("c_mA", [128, 128])
        self.cmB_d = d("c_mB", [128, 128])
        self.cmN_d = d("c_mN", [128, 128])
        self.cnA_d = d("c_nA", [128, 512])
        self.cnB_d = d("c_nB", [128, 512])
        self.cov_d = d("c_ov", [128, 32])

        sb = self.sb
        self.xT = sb("xT_s", [128, KC, S], F32)
        self.hT = sb("hT_s", [128, KC, S], BF16)
        self.Tb = sb("Tb", [128, 16 * 2 * 128], BF16)
        self.ident = sb("ident", [128, 128], BF16)
        self.ones_bf = sb("ones_bf", [128, 128], BF16)
        self.gains = sb("gains_s", [128, (3 * L + 1) * KC], F32)
        self.lam = sb("lam", [128, 8], F32)
        self.arena0 = self.off

        self.off = self.arena0
        self.E_s = sb("E_s", [128, 2 * 32 * 128], BF16)
        self.tbl_s = sb("tbl_s", [128, 512], F32)
        self.Tacc = sb("Tacc", [128, 2 * 16 * 128], F32)
        self.Ttmp = sb("Ttmp", [128, 16 * 128], F32)
        self.caus_s = sb("caus_s", [128, 128], F32)

        self.off = self.arena0
        self.act_s = sb("act_s", [128, NFC, 1024], BF16)
        self.wgu = [sb("wgu_s%d" % i, [128, 2 * KC * 128], BF16) for i in range(3)]
        self.wd = [sb("wd_s%d" % i, [128, NFC * 128], BF16) for i in range(2)]
        self.sg = [sb("sg%d" % i, [128, 512], F32) for i in range(2)]
        self.sq = [sb("sq%d" % i, [128, 512], BF16) for i in range(2)]
        self.rstd = [sb("rstd%d" % i, [128, 512], F32) for i in range(2)]
        self.ffn_end = self.off

        self.off = self.arena0
        self.PT = [sb("PT%d" % i, [128, 512], BF16) for i in range(4)]
        self.small = sb("small", [128, 256], F32)
        self.maskT = [sb("maskT%d" % i, [128, 512], BF16) for i in range(2)]
        self.mb = sb("mb", [128, 128], BF16)
        self.cmpbuf = sb("cmpbuf", [128, 1024], BF16)
        self.winbuf = [sb("winbuf%d" % i, [128, KC * 128], BF16) for i in range(2)]
        u0 = self.off
        self.wT = sb("wT", [128, KC * 512], BF16)
        e1 = self.off
        self.off = u0
        self.w1buf = [sb("w1buf%d" % i, [64, 4 * 256], BF16) for i in range(2)]
        self.HT = sb("HT", [128, 2 * 2 * 128], BF16)
        self.w2k_s = sb("w2k_s", [128, 2 * 128], BF16)
        self.w2v_s = sb("w2v_s", [128, 2 * 64], BF16)
        self.peT_s = sb("peT_s", [64, 64], BF16)
        self.b1_s = sb("b1_s", [128, 2], F32)
        self.gx = [sb("gx%d" % i, [128, 128], F32) for i in range(3)]
        assert self.off <= e1
        self.off = e1
        k0 = self.off
        self.kcT = [sb("kcT%d" % i, [64, S], BF16) for i in range(4)]
        e2 = self.off
        self.off = k0
        self.sq_m = [sb("sqm%d" % i, [128, 512], BF16) for i in range(2)]
        self.rstd_m = [sb("rstdm%d" % i, [128, 512], F32) for i in range(2)]
        assert self.off <= e2
        self.off = u0
        self.otok = sb("otok", [128, 16 * 256], BF16)
        self.oT = sb("oT", [128, 2 * S], BF16)
        self.wout_s = sb("wout_s", [128, 2 * 1024], BF16)
        self.tmp = [sb("tmp%d" % i, [128, 256], F32) for i in range(4)]
        self.ocomb = sb("ocomb", [128, 4 * 256], F32)
        e3 = self.off
        self.off = max(e2, e3)
        self.mix0 = self.off

        self.off = self.mix0
        self.pjM = [sb("pjM%d" % i, [128, S], BF16) for i in range(6)]
        self.VM = sb("VM", [128, 16 * 4 * 65], BF16)
        self.bim = sb("bim", [128, 2048], BF16)
        self.mA = sb("mA", [128, 128], F32)
        self.mB = sb("mB", [128, 128], F32)
        self.mN = sb("mN", [128, 128], F32)
        self.kmT = sb("kmT", [128, 4 * 8], BF16)
        self.kmF = sb("kmF", [128, 4 * 8], F32)
        self.maskM = [sb("maskM%d" % i, [128, 512], BF16) for i in range(8)]
        self.mbM = sb("mbM", [128, 256], BF16)
        self.moba_end = self.off

        self.off = self.mix0
        self.pjD = [sb("pjD%d" % i, [128, S], BF16) for i in range(11)]
        self.VD = sb("VD", [128, 16 * 4 * 65], BF16)
        self.dl_s = sb("dl_s", [128, 128], F32)
        self.sub_s = sb("sub_s", [128, 64], F32)
        self.diff_end = self.off

        self.off = self.mix0
        self.pjN = [sb("pjN%d" % i, [128, S], BF16) for i in range(6)]
        self.VS = sb("VS", [128, 16 * 2 * 65], BF16)
        self.VW = sb("VW", [128, 16 * 2 * 65], BF16)
        self.gate_s = sb("gate_s", [128, 16 * 24], F32)
        self.mcmp = sb("mcmp", [128, 2048], BF16)
        self.bin_ = sb("bin", [128, 2048], BF16)
        self.nA = sb("nA", [128, 512], F32)
        self.nB = sb("nB", [128, 512], F32)
        self.kcmpT = sb("kcmpT", [128, 2 * 128], BF16)
        self.vcaug = sb("vcaug", [128, 2 * 97], BF16)
        self.wedge = sb("wedge", [128, 128], BF16)
        self.impacc = sb("impacc", [128, 128], F32)
        self.nsa_end = self.off
        self.off = max(self.ffn_end, self.moba_end, self.diff_end, self.nsa_end)
        print("SBUF end", self.off, "of", SB_END, "ffn", self.ffn_end, "moba", self.moba_end,
              "diff", self.diff_end, "nsa", self.nsa_end)

        self.bank = [self.ps("bank%d" % i, [128, 512], F32) for i in range(8)]

    def bankv(self, b, p0, npart, f0, dims):
        return V(self.bank[b], 512, p0, npart, f0, dims)

    def setup(self, consts):
        sc = self.sc
        self.dmaq("sp", self.gains[:], self.gains_d[:, :], [], ["gains"], "gains")
        self.memset(self.ones_bf[:], 1.0, ["ones"])
        self.dmaq("pool", self.ident[:], self.cid_d[:, :], [], ["ident"], "ident")
        self.dmaq("pool", self.E_s[:], self.cE_d[:, :], [], ["E_s"], "E_s")
        self.dmaq("sp", self.tbl_s[:], self.tbl_d[:, :], [], ["tbl_s"], "tbl_s")
        self.dmaq("sp", self.caus_s[:], self.ccaus_d[:, :], [], ["caus_s"], "caus_s")
        tb3 = V(self.tbl_s, 512, 0, 128, 0, [(16, 32), (1, 16)])
        t31 = V(self.tbl_s, 512, 0, 128, 31 * 16, [(0, 32), (1, 16)])
        self.tt(tb3, tb3, t31, ALU.subtract, ["tbl_s"], ["tbl_s"])
        first = {0: True, 1: True}
        for di in range(2):
            for b in range(31):
                if not consts["E_nonzero"][di][b]:
                    continue
                Eb = V(self.E_s, 8192, 0, 128, (di * 32 + b) * 128, [(0, 16), (1, 128)])
                tv = V(self.tbl_s, 512, 0, 128, b * 16, [(1, 16), (0, 128)])
                acc = V(self.Tacc, 4096, 0, 128, di * 2048, [(128, 16), (1, 128)])
                if first[di]:
                    self.tt(acc, Eb, tv, ALU.mult, ["E_s", "tbl_s"], [("Tacc", di)])
                    first[di] = False
                else:
                    tmp = V(self.Ttmp, 2048, 0, 128, 0, [(128, 16), (1, 128)])
                    self.tt(tmp, Eb, tv, ALU.mult, ["E_s", "tbl_s"], ["Ttmp"])
                    self.tt(acc, acc, tmp, ALU.add, [("Tacc", di), "Ttmp"], [("Tacc", di)])
        for h in range(16):
            inv = (1.0 / SCD) if 4 <= h < 8 else (1.0 / SC8)
            for di in range(2):
                src = V(self.Tacc, 4096, 0, 128, di * 2048 + h * 128, [(1, 128)])
                dst = V(self.Tb, 4096, 0, 128, (h * 2 + di) * 128, [(1, 128)])
                if di == 0:
                    self.stt(dst, src, inv, self.caus_s[:], ALU.mult, ALU.add,
                             [("Tacc", di), "caus_s"], ["Tb"])
                else:
                    self.ts(dst, src, inv, None, ALU.mult, None, [("Tacc", di)], ["Tb"])
        sc.barrier()

    def Tbv(self, h, di):
        return V(self.Tb, 4096, 0, 128, (h * 2 + di) * 128, [(1, 128)])

    def load_x(self, s):
        for c in range(KC):
            self.dmaq("sp", self.xT[:, c, :], self.xT_d[s, c * 128:(c + 1) * 128, :],
                      [], [("xT", c, tt) for tt in range(4)], "xload%d" % c)

    def store_out(self, s):
        toks = []
        for c in range(KC):
            self.dmaq("sp", self.out_d[s, c * 128:(c + 1) * 128, :], self.xT[:, c, :],
                      [("xT", c, tt) for tt in range(4)], [("outd", s, c)], "xstore%d" % c)
            toks.append(("outd", s, c))
        return toks

    def norm(self, gidx, final=False, mixer=False):
        sqs = self.sq_m if mixer else self.sq
        rstds = self.rstd_m if mixer else self.rstd
        for tt in range(4):
            tsl = slice(tt * 512, (tt + 1) * 512)
            bi = 6 + self.rot("nbank", 2)
            bank = self.bank[bi]
            btok = ("bank", bi)
            for c in range(KC):
                qi = self.rot("sq", 2)
                sq = sqs[qi]
                self.actf(sq[:], self.xT[:, c, tsl], AF.Square, [("xT", c, tt)], [("sq", qi)])
                self.mm(bank[:], self.ones_bf[:], sq[:], c == 0, c == KC - 1,
                        [("sq", qi), "ones"], [btok])
            ri = self.rot("rstd", 2)
            rstd = rstds[ri]
            self.actf(rstd[:], bank[:], AF.Sqrt, [btok], [("rstd", ri)], scale=1.0 / D, bias=RMS_EPS)
            self.recip(rstd[:], rstd[:], [("rstd", ri)], [("rstd", ri)])
            for c in range(KC):
                gcol = self.gains[:, gidx * KC + c: gidx * KC + c + 1]
                if not final:
                    self.stt(self.hT[:, c, tsl], self.xT[:, c, tsl], gcol, rstd[:], ALU.mult, ALU.mult,
                             [("xT", c, tt), ("rstd", ri), "gains"], [("hT", c, tt)])
                else:
                    self.stt(self.xT[:, c, tsl], self.xT[:, c, tsl], gcol, rstd[:], ALU.mult, ALU.mult,
                             [("xT", c, tt), ("rstd", ri), "gains"], [("xT", c, tt)])

    def ffn(self, l, which):
        gu_d = self.gu_d[which]
        wd_d = self.wd_d[which]
        for half in range(2):
            for fc in range(NFC):
                wi = self.rot("wgu", 3)
                w = self.wgu[wi]
                self.dmaq("pool", w[:], gu_d[l, fc, :, :], [], [("wgu", wi)], "wgu%d" % wi)
                for sub in range(2):
                    tt = half * 2 + sub
                    tsl = slice(tt * 512, (tt + 1) * 512)
                    gb = self.rot("abank", 3) * 2
                    gbank, ubank = self.bank[gb], self.bank[gb + 1]
                    for c in range(KC):
                        self.mm(gbank[:], w[:, c * 128:(c + 1) * 128], self.hT[:, c, tsl],
                                c == 0, c == KC - 1, [("wgu", wi), ("hT", c, tt)], [("bank", gb)])
                    for c in range(KC):
                        self.mm(ubank[:], w[:, (KC + c) * 128:(KC + c + 1) * 128], self.hT[:, c, tsl],
                                c == 0, c == KC - 1, [("wgu", wi), ("hT", c, tt)], [("bank", gb + 1)])
                    si = self.rot("sg", 2)
                    sg = self.sg[si]
                    self.actf(sg[:], gbank[:], AF.Silu, [("bank", gb)], [("sg", si)])
                    self.tt(self.act_s[:, fc, sub * 512:(sub + 1) * 512], ubank[:], sg[:], ALU.mult,
                            [("bank", gb + 1), ("sg", si)], [("act", fc, sub)])
            for dc in range(KC):
                wi = self.rot("wd", 2)
                w = self.wd[wi]
                self.dmaq("pool", w[:], wd_d[l, dc, :, :], [], [("wd", wi)], "wd%d" % wi)
                for sub in range(2):
                    tt = half * 2 + sub
                    tsl = slice(tt * 512, (tt + 1) * 512)
                    yb = 6 + self.rot("nbank", 2)
                    ybank = self.bank[yb]
                    for fc in range(NFC):
                        self.mm(ybank[:], w[:, fc * 128:(fc + 1) * 128],
                                self.act_s[:, fc, sub * 512:(sub + 1) * 512],
                                fc == 0, fc == NFC - 1, [("wd", wi), ("act", fc, sub)], [("bank", yb)])
                    self.stt(self.xT[:, dc, tsl], ybank[:], 0.5, self.xT[:, dc, tsl], ALU.mult, ALU.add,
                             [("bank", yb), ("xT", dc, tt)], [("xT", dc, tt)])

    def proj_F(self, l, tile, subs):
        wi = self.rot("winbuf", 2)
        w = self.winbuf[wi]
        self.dmaq("pool", w[:], self.winF_d[l, tile, :, :], [], [("winbuf", wi)], "winbuf%d" % wi,
                  nobarrier=self.prefetch_ok)
        self.prefetch_ok = False
        for tt in range(4):
            tsl = slice(tt * 512, (tt + 1) * 512)
            for (c0, c1, dst, dtok) in subs:
                M = c1 - c0
                bi = self.rot("pbank", 6)
                bank = self.bank[bi]
                for c in range(KC):
                    self.mm(bank[0:M, :], w[:, c * 128 + c0: c * 128 + c1], self.hT[:, c, tsl],
                            c == 0, c == KC - 1, [("winbuf", wi), ("hT", c, tt)], [("bank", bi)])
                eng = "act" if self.rot("pev", 2) == 0 else "dve"
                self.cp(dst[0:M, tsl], bank[0:M, :], [("bank", bi)], [dtok + (tt,)], eng=eng)

    def proj_Fs(self, l, tile, evacs):
        wi = self.rot("winbuf", 2)
        w = self.winbuf[wi]
        self.dmaq("pool", w[:], self.winF_d[l, tile, :, :], [], [("winbuf", wi)], "winbuf%d" % wi,
                  nobarrier=self.prefetch_ok)
        self.prefetch_ok = False
        for tt in range(4):
            tsl = slice(tt * 512, (tt + 1) * 512)
            bi = self.rot("pbank", 6) if self.pbank_all else 6 + self.rot("nbank", 2)
            bank = self.bank[bi]
            for c in range(KC):
                self.mm(bank[:, :], w[:, c * 128:(c + 1) * 128], self.hT[:, c, tsl],
                        c == 0, c == KC - 1, [("winbuf", wi), ("hT", c, tt)], [("bank", bi)])
            eng = "act" if self.rot("pev", 2) == 0 else "dve"
            for (r0, r1, dst, dtok) in evacs:
                self.cp(dst[r0:r1, tsl], bank[r0:r1, :], [("bank", bi)], [dtok + (tt,)], eng=eng)

    def proj_T(self, l, wd_ap, ncols, evac):
        wT = self.wT
        self.dmaq("pool", wT[:, 0:KC * ncols], wd_ap, [], ["wT"], "wT")
        for kt in range(16):
            bi = self.rot("pbank", 6)
            bank = self.bank[bi]
            for c in range(KC):
                self.mm(bank[:, 0:ncols], self.hT[:, c, kt * 128:(kt + 1) * 128],
                        wT[:, c * ncols:(c + 1) * ncols], c == 0, c == KC - 1,
                        ["wT", ("hT", c, kt // 4)], [("bank", bi)])
            evac(kt, bi)

    LOOK = 2

    def flush(self):
        while self.pq:
            self.pq.pop(0)()

    def attn(self, Q, kt_list, kap, qap, vap, nv, scale, extras, ob, otoks,
             kparts=128, extra_reads=(), post=None):
        last = {}
        for (kt, j0, j1) in kt_list:
            for j in range(j0, j1):
                last[j] = kt
        obank = self.bank[ob]
        state = {"first": True}
        n = len(kt_list)
        for idx, (kt, j0, j1) in enumerate(kt_list):
            si = self.rot("sbank", 3)
            sbank = self.bank[si]
            c0, c1 = j0 * 128, j1 * 128
            ex = extras(kt, j0, j1)
            self.mm(sbank[0:kparts, c0:c1], kap(kt), qap(Q * 512 + c0, Q * 512 + c1),
                    True, len(ex) == 0, list(extra_reads), [("bank", si)])
            for i, (e0, e1, l_ap, r_ap, rd) in enumerate(ex):
                self.mm(sbank[0:kparts, e0:e1], l_ap, r_ap, False, i == len(ex) - 1,
                        list(rd), [("bank", si)])
            pi = self.rot("PT", 4)
            pt = self.PT[pi]
            self.actf(pt[0:kparts, c0:c1], sbank[0:kparts, c0:c1], AF.Exp, [("bank", si)], [("pt", pi)],
                      scale=scale)

            def stageB(kt=kt, j0=j0, j1=j1, pt=pt, pi=pi, islast=(idx == n - 1)):
                for j in range(j0, j1):
                    self.mm(obank[:, j * 128: j * 128 + nv], pt[0:kparts, j * 128:(j + 1) * 128], vap(kt),
                            state["first"], last[j] == kt, [("pt", pi)] + list(otoks), [("bank", ob)])
                    state["first"] = False
                if islast and post is not None:
                    post()
            self.pq.append(stageB)
            while len(self.pq) > self.LOOK:
                self.pq.pop(0)()

    def outproj(self, l, grp):
        self.flush()
        self.dmaq("pool", self.wout_s[:], self.wout_d[l, grp, :, :], [], ["wout"], "wout")
        for f in range(2):
            for Qi in range(4):
                bi = 6 + self.rot("nbank", 2)
                bank = self.bank[bi]
                for j in range(4):
                    J = Qi * 4 + j
                    src = V(self.otok, 4096, 0, 128, J * 256 + f * 128, [(1, 128)])
                    self.mm(bank[:, j * 128:(j + 1) * 128], src, self.ident[:], True, True,
                            [("otok", J), "ident"], [("bank", bi)])
                eng = "act" if self.rot("pev", 2) == 0 else "dve"
                dst = V(self.oT, 4096, 0, 128, f * 2048 + Qi * 512, [(1, 512)])
                self.cp(dst, bank[:], [("bank", bi)], [("oT", f, Qi)], eng=eng)
        for dc in range(KC):
            for tt in range(4):
                bi = 6 + self.rot("nbank", 2)
                bank = self.bank[bi]
                for f in range(2):
                    self.mm(bank[:], self.wout_s[:, f * 1024 + dc * 128: f * 1024 + (dc + 1) * 128],
                            V(self.oT, 4096, 0, 128, f * 2048 + tt * 512, [(1, 512)]),
                            f == 0, f == 1, ["wout", ("oT", f, tt)], [("bank", bi)])
                tsl = slice(tt * 512, (tt + 1) * 512)
                self.tt(self.xT[:, dc, tsl], bank[:], self.xT[:, dc, tsl], ALU.add,
                        [("bank", bi), ("xT", dc, tt)], [("xT", dc, tt)])

    def rank_mask(self, score_ap_fn, n, nj, K, mult_tab, mb_out_fn, stok, mtok):
        for j in range(nj):
            a = score_ap_fn(j, [(0, n), (1, n)])
            b = score_ap_fn(j, [(1, n), (0, n)])
            cmpv = V(self.cmpbuf, 1024, 0, 128, 0, [(n, n), (1, n)])
            self.tt(cmpv, a, b, ALU.is_gt, [stok], ["cmpbuf"])
            rk = V(self.small, 256, 0, 128, 192, [(1, n)])
            self.red(rk, cmpv, ALU.add, ["cmpbuf"], ["rank"])
            if callable(mult_tab):
                self.stt(mb_out_fn(j), rk, K - 0.5, mult_tab(j), ALU.is_ge, ALU.mult,
                         ["rank", "tabs"], [mtok])
            else:
                self.ts(mb_out_fn(j), rk, K - 0.5, mult_tab, ALU.is_ge, ALU.mult, ["rank"], [mtok])

    def pass_moba(self, l):
        sc = self.sc
        pj = self.pjM
        for t in range(2):
            self.proj_F(l, t, [(0, 128, pj[t], ("pjM", t))])
        for h in range(4):
            self.memset(pj[2 + h][:], 0.0, [("pjM", 2 + h, tt) for tt in range(4)], eng="pool")
        for t in range(2):
            self.proj_Fs(l, 2 + t, [(0, 64, pj[2 + 2 * t], ("pjM", 2 + 2 * t)),
                                    (64, 128, pj[3 + 2 * t], ("pjM", 3 + 2 * t))])
        self.memset(self.VM[:], 1.0, ["VM"])

        def evac(kt, bi):
            dst = V(self.VM, 4160, 0, 128, kt * 260, [(65, 4), (1, 64)])
            src = V(self.bank[bi], 512, 0, 128, 0, [(64, 4), (1, 64)])
            self.cp(dst, src, [("bank", bi)], ["VM"], eng="act" if kt % 2 else "dve")
        self.proj_T(l, self.winTm_d[l, :, :], 256, evac)
        self.dmaq("pool", self.bim[:], self.cbim_d[:, :], [], ["bim"], "bim")
        for i in range(8):
            self.memset(self.maskM[i][:], 0.0, [("maskM", i)], eng="pool")
        self.dmaq("sp", self.mA[:], self.cmA_d[:, :], [], ["tabs"], "mA")
        self.dmaq("sp", self.mB[:], self.cmB_d[:, :], [], ["tabs"], "mB")
        self.dmaq("sp", self.mN[:], self.cmN_d[:, :], [], ["tabs"], "mN")
        sc.barrier()
        for h in range(4):
            src = V(pj[2 + h], 2048, 0, 128, 0, [(256, 8), (1, 256)])
            dstf = V(self.kmF, 32, 0, 128, h * 8, [(1, 8)])
            self.red(dstf, src, ALU.add, [("pjM", 2 + h, tt) for tt in range(4)], [("kmF", h)])
            dst = V(self.kmT, 32, 0, 128, h * 8, [(1, 8)])
            self.ts(dst, dstf, 1.0 / 256, None, ALU.mult, None, [("kmF", h)], [("kmT", h)])
        mask_of = {}
        gb = 7
        for h in range(4):
            qtile = pj[h // 2]
            qtoks = [("pjM", h // 2, tt) for tt in range(4)]
            for Q in (2, 3):
                mi = h * 2 + (Q - 2)
                mask_of[(h, Q)] = mi
                for j in range(4):
                    J = Q * 4 + j
                    self.mm(self.bank[gb][:, mi * 32 + j * 8: mi * 32 + (j + 1) * 8],
                            V(qtile, 2048, 0, 128, J * 128, [(1, 128)]),
                            V(self.kmT, 32, 0, 128, h * 8, [(1, 8)]), True, True,
                            qtoks + [("kmT", h)], [("bank", gb)])
        for h in range(4):
            for Q in (2, 3):
                mi = mask_of[(h, Q)]
                scv = V(self.small, 256, 0, 128, 0, [(1, 32)])
                Av = V(self.mA, 128, 0, 128, Q * 32, [(1, 32)])
                Bv = V(self.mB, 128, 0, 128, Q * 32, [(1, 32)])
                self.tt(scv, self.bank[gb][:, mi * 32:(mi + 1) * 32], Av, ALU.mult,
                        [("bank", gb), "tabs"], ["score"])
                self.tt(scv, scv, Bv, ALU.add, ["score", "tabs"], ["score"])
                self.rank_mask(lambda j, dims: V(self.small, 256, 0, 128, j * 8, dims), 8, 4, 3,
                               lambda j, Q=Q: V(self.mN, 128, 0, 128, Q * 32 + j * 8, [(1, 8)]),
                               lambda j, mi=mi: V(self.mbM, 256, 0, 128, mi * 32 + j * 8, [(1, 8)]),
                               "score", ("mbM", mi))

        def units(Qs):
            for h in range(4):
                qtile = pj[h // 2]
                ktile = pj[2 + h]
                qtoks = [("pjM", h // 2, tt) for tt in range(4)]
                ktoks = [("pjM", 2 + h, tt) for tt in range(4)]
                for Q in Qs:
                    mi = mask_of.get((h, Q))
                    ob = 3 + self.rot("obank", 3)

                    def extras(kt, j0, j1, Q=Q, h=h, mi=mi):
                        ex = self.toep_extras(h, Q, kt, j0, j1)
                        if mi is not None and (kt // 2) < (4 * Q + 3) // 2:
                            ex.append((j0 * 128, j1 * 128, self.bim[:, kt * 128:(kt + 1) * 128],
                                       self.maskM[mi][:, j0 * 128:j1 * 128], ["bim", ("maskM", mi)]))
                        return ex

                    def post(Q=Q, h=h, ob=ob):
                        rbase = 64 + self.rot("rs", 4) * 16
                        rs = V(self.small, 256, 0, 128, rbase, [(1, 4)])
                        self.recip(rs, self.osum(ob), [("bank", ob)], ["rs"])
                        rsb = V(self.small, 256, 0, 128, rbase, [(1, 4), (0, 64)])
                        dst = V(self.otok, 4096, 0, 128, Q * 4 * 256 + h * 64, [(256, 4), (1, 64)])
                        self.tt(dst, self.obv(ob), rsb, ALU.mult, [("bank", ob), "rs"],
                                [("otok", Q * 4 + j) for j in range(4)])
                    self.attn(Q, self.causal_kts(Q),
                              lambda kt, ktile=ktile: V(ktile, 2048, 0, 128, kt * 128, [(1, 128)]),
                              lambda c0, c1, qtile=qtile: V(qtile, 2048, 0, 128, c0, [(1, c1 - c0)]),
                              lambda kt, h=h: V(self.VM, 4160, 0, 128, kt * 260 + h * 65, [(1, 65)]),
                              65, SC8, extras, ob, ["VM"], extra_reads=qtoks + ktoks, post=post)
        units((0, 1))
        self.flush()
        for mi in range(8):
            for j in range(4):
                self.mm(self.bank[gb][0:8, j * 128:(j + 1) * 128],
                        V(self.mbM, 256, 0, 128, mi * 32 + j * 8, [(1, 8)]), self.ident[:], True, True,
                        [("mbM", mi), "ident"], [("bank", gb)])
            self.cp(self.maskM[mi][0:8, :], self.bank[gb][0:8, :], [("bank", gb)], [("maskM", mi)])
        units((2, 3))
        self.outproj(l, 0)

    def mixer(self, l):
        sc = self.sc
        sc.barrier()
        self.norm(3 * l + 1, mixer=True)
        if "moba" in self.branches:
            sc.barrier()
            self.prefetch_ok = True
            self.pass_moba(l)
        if "diff" in self.branches:
            sc.barrier()
            self.prefetch_ok = True
            self.pass_diff(l)
        if "nsa" in self.branches:
            sc.barrier()
            self.prefetch_ok = True
            self.pass_nsa(l)
        sc.barrier()

    def build(self, consts):
        self.declare()
        self.setup(consts)
        outtoks = []
        for s in range(self.NS):
            self.load_x(s)
            for l in range(self.L):
                if "ffn1" in self.parts:
                    self.norm(3 * l + 0)
                    self.ffn(l, 0)
                if "mix" in self.parts:
                    self.mixer(l)
                if "ffn2" in self.parts:
                    self.norm(3 * l + 2)
                    self.ffn(l, 1)
            if self.do_final:
                self.norm(3 * self.L, final=True)
            outtoks += self.store_out(s)
        self.sc.final_wait("sp", outtoks)
        cnt = self.sc.emit(self.nc, self.stack)
        self.stack.close()
        return self.nc, cnt

    def causal_kts(self, Q):
        return [(kt, max(0, kt - 4 * Q), 4) for kt in range(4 * Q + 4)]

    def toep_extras(self, hb, Q, kt, j0, j1):
        ex = []
        jd = kt - 4 * Q
        if j0 <= jd < j1 and jd + 1 < j1:
            ex.append((jd * 128, (jd + 2) * 128, self.ident[:],
                       V(self.Tb, 4096, 0, 128, hb * 256, [(1, 256)]), ["ident", "Tb"]))
            return ex
        for j in range(j0, j1):
            dlt = 4 * Q + j - kt
            if dlt in (0, 1):
                ex.append((j * 128, (j + 1) * 128, self.ident[:], self.Tbv(hb, dlt), ["ident", "Tb"]))
        return ex

    def obv(self, ob):
        return V(self.bank[ob], 512, 0, 128, 0, [(128, 4), (1, 64)])

    def osum(self, ob):
        return V(self.bank[ob], 512, 0, 128, 64, [(128, 4)])

    def pass_diff(self, l):
        sc = self.sc
        lam_init = 0.8 - 0.6 * math.exp(-0.3 * (l + self.layer0))
        pj = self.pjD
        for t in range(3):
            self.proj_F(l, 4 + t, [(0, 128, pj[t], ("pjD", t))])
        for gi in range(8):
            self.memset(pj[3 + gi][:], 0.0, [("pjD", 3 + gi, tt) for tt in range(4)], eng="pool")
        for t in range(3):
            ev = []
            for slot in range(3):
                gi = 3 * t + slot
                if gi < 8:
                    ev.append((32 * slot, 32 * slot + 32, pj[3 + gi], ("pjD", 3 + gi)))
            self.proj_Fs(l, 7 + t, ev)
        self.memset(self.VD[:], 1.0, ["VD"])

        def evac(kt, bi):
            dst = V(self.VD, 4160, 0, 128, kt * 260, [(65, 4), (1, 64)])
            src = V(self.bank[bi], 512, 0, 128, 0, [(64, 4), (1, 64)])
            self.cp(dst, src, [("bank", bi)], ["VD"], eng="act" if kt % 2 else "dve")
        self.proj_T(l, self.winTd_d[l, :, :], 256, evac)
        self.dmaq("sp", self.dl_s[:], self.dl_d[l, :, :], [], ["dl_s"], "dl_s")
        self.dmaq("sp", self.sub_s[:], self.sub_d[l, :, :], [], ["sub_s"], "sub_s")
        sc.barrier()
        t0 = V(self.tmp[0], 256, 0, 128, 0, [(1, 32)])
        self.tt(t0, self.dl_s[:, 0:32], self.dl_s[:, 32:64], ALU.mult, ["dl_s"], ["tmp0"])
        self.red(self.lam[:, 0:1], t0, ALU.add, ["tmp0"], ["lam"])
        self.tt(t0, self.dl_s[:, 64:96], self.dl_s[:, 96:128], ALU.mult, ["dl_s", "lam"], ["tmp0"])
        self.red(self.lam[:, 1:2], t0, ALU.add, ["tmp0"], ["lam"])
        self.actf(self.lam[:, 2:4], self.lam[:, 0:2], AF.Exp, ["lam"], ["lam"])
        self.tt(self.lam[:, 4:5], self.lam[:, 2:3], self.lam[:, 3:4], ALU.subtract, ["lam"], ["lam"])
        self.ts(self.lam[:, 5:6], self.lam[:, 4:5], lam_init, -1.0, ALU.add, ALU.mult, ["lam"], ["lam"])
        sub_b = V(self.sub_s, 64, 0, 128, 0, [(0, 4), (1, 64)])
        tv = [V(self.tmp[i], 256, 0, 128, 0, [(64, 4), (1, 64)]) for i in range(4)]
        bc = lambda base: V(self.small, 256, 0, 128, base, [(1, 4), (0, 64)])
        for h in range(4):
            for Q in range(4):
                obs = [3 + self.rot("obankD", 4), 3 + self.rot("obankD", 4)]

                def post(h=h, Q=Q, obs=obs):
                    rb = 64 + self.rot("rs", 4) * 16
                    st = self.rot("dset", 2)
                    X, Y = tv[2 * st], tv[2 * st + 1]
                    tx, ty = "tmp%d" % (2 * st), "tmp%d" % (2 * st + 1)
                    rs0 = V(self.small, 256, 0, 128, rb, [(1, 4)])
                    rs1 = V(self.small, 256, 0, 128, rb + 4, [(1, 4)])
                    nl = V(self.small, 256, 0, 128, rb + 8, [(1, 4)])
                    ssv = V(self.small, 256, 0, 128, rb + 12, [(1, 4)])
                    self.recip(rs0, self.osum(obs[0]), [("bank", obs[0])], ["rs"])
                    self.recip(rs1, self.osum(obs[1]), [("bank", obs[1])], ["rs"])
                    self.ts(nl, rs1, self.lam[:, 5:6], None, ALU.mult, None, ["rs", "lam"], ["rs"])
                    self.tt(X, self.obv(obs[0]), bc(rb), ALU.mult, [("bank", obs[0]), "rs"], [tx])
                    self.tt(Y, self.obv(obs[1]), bc(rb + 8), ALU.mult, [("bank", obs[1]), "rs"], [ty])
                    self.tt(X, X, Y, ALU.add, [tx, ty], [tx])
                    self.tt(Y, X, X, ALU.mult, [tx], [ty])
                    self.red(ssv, Y, ALU.add, [ty], [("ss", rb)])

                    def part2(h=h, Q=Q, rb=rb, X=X, Y=Y, tx=tx, ty=ty, ssv=ssv):
                        self.actf(ssv, ssv, AF.Sqrt, [("ss", rb)], [("ss", rb)], scale=1.0 / 64, bias=RMS_EPS)
                        self.recip(ssv, ssv, [("ss", rb)], [("ss", rb)])
                        self.stt(Y, X, 1.0 - lam_init, bc(rb + 12), ALU.mult, ALU.mult,
                                 [tx, ("ss", rb)], [ty])
                        dst = V(self.otok, 4096, 0, 128, Q * 4 * 256 + h * 64, [(256, 4), (1, 64)])
                        self.tt(dst, Y, sub_b, ALU.mult, [ty, "sub_s"],
                                [("otok", Q * 4 + j) for j in range(4)])
                    prev = self.deferred
                    self.deferred = part2
                    if prev is not None:
                        prev()
                for m in range(2):
                    gi = 2 * h + m
                    r0 = 32 * (gi % 3)
                    qtile = pj[gi // 3]
                    ktile = pj[3 + gi]
                    self.attn(Q, self.causal_kts(Q),
                              lambda kt, ktile=ktile: V(ktile, 2048, 0, 128, kt * 128, [(1, 128)]),
                              lambda c0, c1, qtile=qtile: V(qtile, 2048, 0, 128, c0, [(1, c1 - c0)]),
                              lambda kt, h=h: V(self.VD, 4160, 0, 128, kt * 260 + h * 65, [(1, 65)]),
                              65, SCD, lambda kt, j0, j1, h=h, Q=Q: self.toep_extras(4 + h, Q, kt, j0, j1),
                              obs[m], ["VD"], post=(post if m == 1 else None))
        self.flush()
        if self.deferred is not None:
            self.deferred()
            self.deferred = None
        self.outproj(l, 1)

    def pass_nsa(self, l):
        sc = self.sc
        pj = self.pjN
        self.proj_F(l, 18, [(0, 64, self.kcT[0], ("kcT", 0)), (64, 128, self.kcT[1], ("kcT", 1))])
        self.proj_F(l, 19, [(0, 64, self.kcT[2], ("kcT", 2)), (64, 128, self.kcT[3], ("kcT", 3))])
        self.memset(self.VS[:], 1.0, ["VS"])
        self.memset(self.VW[:], 1.0, ["VW"])

        def evac(kt, bi):
            bank = self.bank[bi]
            dbg = os.environ.get("NSA_DBG", "")
            if "a" in dbg:
                self.cp(self.gate_s[:, kt * 24:(kt + 1) * 24], bank[:, 256:280], [("bank", bi)], ["gate"])
                return
            if "c" in dbg:
                self.cp(V(self.VS, 2080, 0, 128, kt * 130, [(65, 2), (1, 64)]),
                        V(bank, 512, 0, 128, 0, [(64, 2), (1, 64)]), [("bank", bi)], ["VS"], eng="dve")
                self.cp(V(self.VW, 2080, 0, 128, kt * 130, [(65, 2), (1, 64)]),
                        V(bank, 512, 0, 128, 128, [(64, 2), (1, 64)]), [("bank", bi)], ["VW"], eng="dve")
                return
            if "d" in dbg:
                self.actf(self.gate_s[:, kt * 24:(kt + 1) * 24], bank[:, 256:280], AF.Tanh, [("bank", bi)], ["gate"], scale=0.5)
                return
            if "b" in dbg:
                self.cp(V(self.VS, 2080, 0, 128, kt * 130, [(65, 2), (1, 64)]),
                        V(bank, 512, 0, 128, 0, [(64, 2), (1, 64)]), [("bank", bi)], ["VS"], eng="dve")
                return
            self.cp(V(self.VS, 2080, 0, 128, kt * 130, [(65, 2), (1, 64)]),
                    V(bank, 512, 0, 128, 0, [(64, 2), (1, 64)]), [("bank", bi)], ["VS"], eng="dve")
            self.cp(V(self.VW, 2080, 0, 128, kt * 130, [(65, 2), (1, 64)]),
                    V(bank, 512, 0, 128, 128, [(64, 2), (1, 64)]), [("bank", bi)], ["VW"], eng="dve")
            self.cp(self.gate_s[:, kt * 24:(kt + 1) * 24], bank[:, 256:280], [("bank", bi)], ["gate"])
        stage = int(os.environ.get("NSA_STAGE", "99"))
        if stage < 1:
            return
        self.proj_T(l, self.winTn_d[l, :, :], 320, evac)
        self.actf(self.gate_s[:], self.gate_s[:], AF.Tanh, ["gate"], ["gate"], scale=0.5)
        self.ts(self.gate_s[:], self.gate_s[:], 0.5, 0.5, ALU.mult, ALU.add, ["gate"], ["gate"])
        if stage < 2:
            return
        self.dmaq("pool", self.mcmp[:], self.cmcmp_d[:, :], [], ["mcmp"], "mcmp")
        self.dmaq("pool", self.bin_[:], self.cbin_d[:, :], [], ["bin"], "bin")
        for i in range(2):
            self.memset(self.maskT[i][:], 0.0, [("maskT", i)], eng="pool")
        self.dmaq("pool", self.wedge[:], self.cwedge_d[:, :], [], ["wedge"], "wedge")
        self.dmaq("sp", self.nA[:], self.cnA_d[:, :], [], ["tabs"], "nA")
        self.dmaq("sp", self.nB[:], self.cnB_d[:, :], [], ["tabs"], "nB")
        self.memset(self.vcaug[:], 1.0, ["vcaug"])
        for g in range(2):
            self.dmaq("pool", self.vcaug[:, g * 97 + 65: g * 97 + 97], self.cov_d[:, :], ["vcaug"], ["vcaug"],
                      "vcaug")
        sc.barrier()
        if stage < 3:
            return
        self.dmaq("pool", self.w2k_s[:], self.w2k_d[l, :, :], [], ["w2k"], "w2k")
        self.dmaq("pool", self.w2v_s[:], self.w2v_d[l, :, :], [], ["w2v"], "w2v")
        self.dmaq("pool", self.peT_s[:], self.peT_d[l, :, :], [], ["peT"], "peT")
        b1first = True
        for i in range(2):
            ab = i
            afirst = True
            for lg in range(8):
                wi = self.rot("w1buf", 2)
                wb = self.w1buf[wi]
                self.dmaq("pool", wb[:], self.w1_d[l, i, lg, :, :], [], [("w1buf", wi)], "w1buf%d" % wi)
                for ll in range(4):
                    li = lg * 4 + ll
                    for hc in range(2):
                        lhsT = wb[0:64, ll * 256 + hc * 128: ll * 256 + (hc + 1) * 128]
                        for g in range(2):
                            self.mm(self.bank[ab][:, (g * 2 + hc) * 128:(g * 2 + hc) * 128 + 127], lhsT,
                                    V(self.kcT[i * 2 + g], 2048, 0, 64, li, [(16, 127)]),
                                    afirst, li == 31, [("w1buf", wi)], [("bank", ab)])
                            afirst = False
                        self.mm(self.bank[2][:, i * 2 + hc: i * 2 + hc + 1], lhsT,
                                self.peT_s[0:64, i * 32 + li: i * 32 + li + 1],
                                b1first, li == 31, [("w1buf", wi), "peT"], [("bank", 2)])
                        b1first = False
        for i in range(2):
            ab = i
            self.cp(self.b1_s[:, 0:2], self.bank[2][:, i * 2: i * 2 + 2], [("bank", 2)], ["b1"])
            for g in range(2):
                for hc in range(2):
                    acc = self.bank[ab][:, (g * 2 + hc) * 128:(g * 2 + hc) * 128 + 127]
                    xs, x2, sgm = self.gx[0][:, 0:127], self.gx[1][:, 0:127], self.gx[2][:, 0:127]
                    self.actf(xs, acc, AF.Identity, [("bank", ab), "b1"], ["gx0"], bias=self.b1_s[:, hc:hc + 1])
                    self.tt(x2, xs, xs, ALU.mult, ["gx0"], ["gx1"])
                    self.ts(x2, x2, 0.044715, 1.0, ALU.mult, ALU.add, ["gx1"], ["gx1"])
                    self.tt(x2, x2, xs, ALU.mult, ["gx1", "gx0"], ["gx1"])
                    self.actf(sgm, x2, AF.Tanh, ["gx1"], ["gx2"], scale=GELU_C * 0.5)
                    self.ts(sgm, sgm, 0.5, 0.5, ALU.mult, ALU.add, ["gx2"], ["gx2"])
                    hdst = V(self.HT, 512, 0, 128, (g * 2 + hc) * 128, [(1, 127)])
                    self.tt(hdst, xs, sgm, ALU.mult, ["gx0", "gx2"], [("HT", g, hc)])
            for g in range(2):
                if i == 0:
                    for hc in range(2):
                        self.mm(self.bank[3][:, g * 128: g * 128 + 127], self.w2k_s[:, hc * 128:(hc + 1) * 128],
                                V(self.HT, 512, 0, 128, (g * 2 + hc) * 128, [(1, 127)]),
                                g == 0 and hc == 0, hc == 1, ["w2k", ("HT", g, hc)], [("bank", 3)])
                else:
                    for hc in range(2):
                        self.mm(self.bank[4][0:127, g * 64:(g + 1) * 64],
                                V(self.HT, 512, 0, 128, (g * 2 + hc) * 128, [(1, 127)]),
                                self.w2v_s[:, hc * 64:(hc + 1) * 64],
                                g == 0 and hc == 0, hc == 1, ["w2v", ("HT", g, hc)], [("bank", 4)])
            if i == 0:
                for g in range(2):
                    self.cp(self.kcmpT[:, g * 128: g * 128 + 127], self.bank[3][:, g * 128: g * 128 + 127],
                            [("bank", 3)], ["kcmpT"])
            else:
                for g in range(2):
                    self.cp(self.vcaug[0:127, g * 97: g * 97 + 64], self.bank[4][0:127, g * 64:(g + 1) * 64],
                            [("bank", 4)], ["vcaug"])
        sc.barrier()
        if stage < 4:
            return
        bc64 = lambda base: V(self.small, 256, 0, 128, base, [(1, 4), (0, 64)])
        bc32 = lambda base: V(self.small, 256, 0, 128, base, [(1, 4), (0, 32)])
        tv = [V(self.tmp[i], 256, 0, 128, 0, [(64, 4), (1, 64)]) for i in range(4)]
        for g in range(2):
            self.pbank_all = False
            for t in range(2, 6):
                self.memset(pj[t][:], 0.0, [("pjN", t, tt) for tt in range(4)], eng="pool")
            for t in range(2):
                self.proj_Fs(l, 10 + 2 * g + t, [(0, 128, pj[t], ("pjN", t))])
            self.proj_Fs(l, 14 + g, [(0, 64, pj[2], ("pjN", 2)), (64, 128, pj[3], ("pjN", 3))])
            self.proj_Fs(l, 16 + g, [(0, 64, pj[4], ("pjN", 4)), (64, 128, pj[5], ("pjN", 5))])
            self.pbank_all = True
            alltoks = [("pjN", t, tt) for t in range(6) for tt in range(4)]
            for Q in range(4):
                for r in range(4):
                    h = 4 * g + r
                    r0 = 64 * (h % 2)
                    qtile = pj[r // 2]
                    ob = 3 + self.rot("obank", 3)

                    def post(r=r, h=h, ob=ob, Q=Q):
                        rb = 64 + self.rot("rs", 4) * 16
                        rs = V(self.small, 256, 0, 128, rb, [(1, 4)])
                        fv = V(self.small, 256, 0, 128, rb + 4, [(1, 4)])
                        self.ts(rs, self.osum(ob), 1e-30, None, ALU.add, None, [("bank", ob)], ["rs"])
                        self.recip(rs, rs, ["rs"], ["rs"])
                        impv = V(self.bank[ob], 512, 0, 128, 65, [(128, 4), (1, 32)])
                        iacc = V(self.impacc, 128, 0, 128, 0, [(32, 4), (1, 32)])
                        if r == 0:
                            self.tt(iacc, impv, bc32(rb), ALU.mult, [("bank", ob), "rs"], ["impacc"])
                        else:
                            itmp = V(self.tmp[3], 256, 0, 128, 0, [(32, 4), (1, 32)])
                            self.tt(itmp, impv, bc32(rb), ALU.mult, [("bank", ob), "rs"], ["tmp3"])
                            self.tt(iacc, iacc, itmp, ALU.add, ["impacc", "tmp3"], ["impacc"])
                        gv = V(self.gate_s, 384, 0, 128, Q * 96 + h * 3 + 0, [(24, 4)])
                        self.tt(fv, rs, gv, ALU.mult, ["rs", "gate"], ["rs"])
                        oc = V(self.ocomb, 1024, 0, 128, r * 256, [(64, 4), (1, 64)])
                        self.tt(oc, self.obv(ob), bc64(rb + 4), ALU.mult, [("bank", ob), "rs"], [("ocomb", r)])
                    self.attn(Q, [(0, 0, 4)],
                              lambda kt, r0=r0, g=g: V(self.kcmpT, 256, r0, 64, g * 128, [(1, 127)]),
                              lambda c0, c1, qtile=qtile, r0=r0: V(qtile, 2048, r0, 64, c0, [(1, c1 - c0)]),
                              lambda kt, g=g: V(self.vcaug, 194, 0, 127, g * 97, [(1, 97)]),
                              97, SC8,
                              lambda kt, j0, j1, Q=Q: [(0, 512, self.ident[0:127, 0:127],
                                                        self.mcmp[0:127, Q * 512:(Q + 1) * 512],
                                                        ["ident", "mcmp"])],
                              ob, ["vcaug"], kparts=127, extra_reads=["kcmpT"] + alltoks, post=post)
                if stage < 5:
                    continue
                mi = None
                if Q >= 2:
                    self.flush()
                    scv = V(self.impacc, 128, 0, 128, 0, [(1, 128)])
                    self.tt(scv, scv, self.nA[:, Q * 128:(Q + 1) * 128], ALU.mult, ["impacc", "tabs"], ["impacc"])
                    self.tt(scv, scv, self.nB[:, Q * 128:(Q + 1) * 128], ALU.add, ["impacc", "tabs"], ["impacc"])
                    self.rank_mask(lambda j, dims: V(self.impacc, 128, 0, 128, j * 32, dims), 32, 4, 16,
                                   NEG, lambda j: V(self.mb, 128, 0, 128, j * 32, [(1, 32)]), "impacc", "mb")
                    mi = self.rot("maskT", 2)
                if stage < 6:
                    continue
                for r in range(4):
                    h = 4 * g + r
                    r0 = 64 * (h % 2)
                    qtile = pj[r // 2]
                    ktile = pj[4 + (r % 2)]
                    ob = 3 + self.rot("obank", 3)
                    kts = []
                    for kt in range(max(0, 4 * Q - 4), 4 * Q + 4):
                        j0 = max(0, kt - 4 * Q)
                        j1 = min(4, kt - 4 * Q + 5)
                        kts.append((kt, j0, j1))

                    def extras(kt, j0, j1, h=h, Q=Q):
                        ex = self.toep_extras(8 + h, Q, kt, j0, j1)
                        for j in range(j0, j1):
                            if 4 * Q + j - kt == 4:
                                ex.append((j * 128, (j + 1) * 128, self.ident[:], self.wedge[:],
                                           ["ident", "wedge"]))
                        return ex

                    def post(r=r, h=h, ob=ob, Q=Q):
                        rb = 64 + self.rot("rs", 4) * 16
                        rs = V(self.small, 256, 0, 128, rb, [(1, 4)])
                        fv = V(self.small, 256, 0, 128, rb + 4, [(1, 4)])
                        self.recip(rs, self.osum(ob), [("bank", ob)], ["rs"])
                        gv = V(self.gate_s, 384, 0, 128, Q * 96 + h * 3 + 2, [(24, 4)])
                        self.tt(fv, rs, gv, ALU.mult, ["rs", "gate"], ["rs"])
                        oc = V(self.ocomb, 1024, 0, 128, r * 256, [(64, 4), (1, 64)])
                        self.tt(tv[1], self.obv(ob), bc64(rb + 4), ALU.mult, [("bank", ob), "rs"], ["tmp1"])
                        self.tt(oc, oc, tv[1], ALU.add, [("ocomb", r), "tmp1"], [("ocomb", r)])
                    self.attn(Q, kts,
                              lambda kt, ktile=ktile: V(ktile, 2048, 0, 128, kt * 128, [(1, 128)]),
                              lambda c0, c1, qtile=qtile: V(qtile, 2048, 0, 128, c0, [(1, c1 - c0)]),
                              lambda kt, g=g: V(self.VW, 2080, 0, 128, kt * 130 + g * 65, [(1, 65)]),
                              65, SC8, extras, ob, ["VW"], extra_reads=alltoks, post=post)
                if stage < 7:
                    continue
                if mi is not None:
                    self.flush()
                    for j in range(4):
                        self.mm(self.bank[7][0:32, j * 128:(j + 1) * 128],
                                V(self.mb, 128, 0, 128, j * 32, [(1, 32)]), self.ident[:], True, True,
                                ["mb", "ident"], [("bank", 7)])
                    self.cp(self.maskT[mi][0:32, :], self.bank[7][0:32, :], [("bank", 7)], [("maskT", mi)])
                for r in range(4):
                    h = 4 * g + r
                    r0 = 64 * (h % 2)
                    qtile = pj[r // 2]
                    ktile = pj[2 + (r % 2)]
                    ob = 3 + self.rot("obank", 3)

                    def extras(kt, j0, j1, h=h, mi=mi, Q=Q):
                        ex = self.toep_extras(8 + h, Q, kt, j0, j1)
                        if mi is not None:
                            ex.append((j0 * 128, j1 * 128, self.bin_[:, kt * 128:(kt + 1) * 128],
                                       self.maskT[mi][:, j0 * 128:j1 * 128], ["bin", ("maskT", mi)]))
                        return ex

                    def post(r=r, h=h, ob=ob, Q=Q):
                        rb = 64 + self.rot("rs", 4) * 16
                        rs = V(self.small, 256, 0, 128, rb, [(1, 4)])
                        fv = V(self.small, 256, 0, 128, rb + 4, [(1, 4)])
                        self.recip(rs, self.osum(ob), [("bank", ob)], ["rs"])
                        gv = V(self.gate_s, 384, 0, 128, Q * 96 + h * 3 + 1, [(24, 4)])
                        self.tt(fv, rs, gv, ALU.mult, ["rs", "gate"], ["rs"])
                        oc = V(self.ocomb, 1024, 0, 128, r * 256, [(64, 4), (1, 64)])
                        self.tt(tv[0], self.obv(ob), bc64(rb + 4), ALU.mult, [("bank", ob), "rs"], ["tmp0"])
                        dst = V(self.otok, 4096, 0, 128, Q * 4 * 256 + r * 64, [(256, 4), (1, 64)])
                        self.tt(dst, oc, tv[0], ALU.add, [("ocomb", r), "tmp0"],
                                [("otok", Q * 4 + j) for j in range(4)])
                    self.attn(Q, self.causal_kts(Q),
                              lambda kt, ktile=ktile: V(ktile, 2048, 0, 128, kt * 128, [(1, 128)]),
                              lambda c0, c1, qtile=qtile: V(qtile, 2048, 0, 128, c0, [(1, c1 - c0)]),
                              lambda kt, g=g: V(self.VS, 2080, 0, 128, kt * 130 + g * 65, [(1, 65)]),
                              65, SC8, extras, ob, ["VS"], extra_reads=alltoks, post=post)
            self.flush()
            if stage >= 8:
                self.outproj(l, 2 + g)


def _rel_bucket(dist):
    n = np.maximum(dist, 0)
    max_exact = 16
    n_f = np.maximum(n, max_exact).astype(np.float32)
    large = max_exact + (np.log(n_f / np.float32(max_exact)) / np.float32(np.log(128 / 16))
                         * np.float32(16)).astype(np.int32)
    return np.where(n < max_exact, n, np.minimum(large, 31))


def make_consts():
    c = {}
    k = np.arange(128)[:, None]
    q = np.arange(128)[None, :]
    E = np.zeros((128, 2, 32, 128), np.float32)
    for di in range(2):
        dist = di * 128 + q - k
        bk = _rel_bucket(dist)
        for b in range(32):
            E[:, di, b, :] = ((bk == b) & (dist >= 0)).astype(np.float32)
    c["E_nonzero"] = [[bool(E[:, di, b, :].any()) for b in range(32)] for di in range(2)]
    c["c_E"] = E.reshape(128, -1)
    c["c_ident"] = np.eye(128, dtype=np.float32)
    c["c_caus"] = np.where(q >= k, 0.0, NEG).astype(np.float32)
    c["c_wedge"] = np.where(q < k, 0.0, NEG).astype(np.float32)
    n = np.arange(128)[:, None]
    qq = np.arange(S)[None, :]
    mc = np.where(16 * n + 31 <= qq, 0.0, NEG).astype(np.float32)
    mc[127, :] = 0.0
    c["c_mcmp"] = mc
    kk = np.arange(S)[None, :]
    c["c_bim"] = (kk // 256 == np.arange(128)[:, None]).astype(np.float32)
    c["c_bin"] = (kk // 64 == np.arange(128)[:, None]).astype(np.float32)
    p = np.arange(128)[:, None, None]
    J = np.arange(16)[None, :, None]
    nn = np.arange(8)[None, None, :]
    own = (J * 128 + p) // 256
    A = (nn < own).astype(np.float32)
    B = np.where(nn < own, 0.0, -1e9 - 1e6 * nn).astype(np.float32)
    c["c_mA"] = A.reshape(128, 128)
    c["c_mB"] = np.broadcast_to(B, (128, 16, 8)).reshape(128, 128).copy()
    c["c_mN"] = (NEG * A).reshape(128, 128).astype(np.float32)
    jj = np.arange(32)[None, None, :]
    cur = (J * 128 + p) // 64
    valid = jj <= cur
    forced = (jj == 0) | (jj > cur - 2)
    A = (valid & ~forced).astype(np.float32)
    B = np.where(valid & forced, 1e4 + jj, np.where(~valid, -1e9 - 1e6 * jj, 0.0)).astype(np.float32)
    c["c_nA"] = A.reshape(128, 512)
    c["c_nB"] = np.broadcast_to(B, (128, 16, 32)).reshape(128, 512).copy()
    n_cmp = 127
    cs = np.arange(n_cmp) * 16
    ss = np.arange(32) * 64
    ov = np.clip(np.minimum(cs[:, None] + 32, ss[None, :] + 64) - np.maximum(cs[:, None], ss[None, :]),
                 0, None) / 32
    ovp = np.zeros((128, 32), np.float32)
    ovp[:127] = ov
    c["c_ov"] = ovp
    return c


_CONSTS = None


def _consts():
    global _CONSTS
    if _CONSTS is None:
        _CONSTS = make_consts()
    return _CONSTS


def _lay_gu(g, u):
    L = g.shape[0]
    a = np.stack([g, u], axis=1)
    a = a.reshape(L, 2, KC, 128, NFC, 128)
    a = a.transpose(0, 4, 3, 1, 2, 5)
    return np.ascontiguousarray(a).reshape(L, NFC, 128, 2 * KC * 128)


def _lay_wd(w):
    L = w.shape[0]
    a = w.reshape(L, NFC, 128, KC, 128)
    a = a.transpose(0, 3, 2, 1, 4)
    return np.ascontiguousarray(a).reshape(L, KC, 128, NFC * 128)


def _lay_gains(vecs):
    cols = [v.reshape(KC, 128).T for v in vecs]
    return np.ascontiguousarray(np.concatenate(cols, axis=1)).astype(np.float32)


def _win_tiles():
    tiles = []
    for t in range(2):
        tiles.append(list(range(0 + 128 * t, 128 * (t + 1))))
    for t in range(2):
        tiles.append(list(range(256 + 128 * t, 256 + 128 * (t + 1))))
    for base in (768, 1024):
        for t in range(3):
            cols = []
            for slot in range(4):
                gi = t * 3 + slot
                if slot < 3 and gi < 8:
                    cols += list(range(base + gi * 32, base + gi * 32 + 32))
                else:
                    cols += [-1] * 32
            tiles.append(cols)
    for t in range(4):
        tiles.append(list(range(1536 + 128 * t, 1536 + 128 * (t + 1))))
    for base in (2304, 2560):
        for g in range(2):
            cc = list(range(base + 64 * g, base + 64 * g + 64))
            tiles.append(cc + cc)
    tiles.append(list(range(2048, 2176)))
    tiles.append(list(range(2176, 2304)))
    return tiles


def _lay_cols(w, cols):
    L = w.shape[0]
    idx = np.array(cols)
    sel = w[:, :, np.where(idx < 0, 0, idx)].copy()
    if (idx < 0).any():
        sel[:, :, idx < 0] = 0.0
    a = sel.reshape(L, KC, 128, len(cols)).transpose(0, 2, 1, 3)
    return np.ascontiguousarray(a)


def prep_inputs(p, L, layer0=0):
    sl = slice(layer0, layer0 + L)
    c = _consts()
    m = {}
    gl = []
    for l in range(layer0, layer0 + L):
        gl += [p["norm_ffn1"][l], p["norm_mix"][l], p["norm_ffn2"][l]]
    gl.append(p["final_norm"])
    m["gains"] = _lay_gains(gl)
    m["gu1"] = _lay_gu(p["ffn1_gate"][sl], p["ffn1_up"][sl])
    m["gu2"] = _lay_gu(p["ffn2_gate"][sl], p["ffn2_up"][sl])
    m["wd1"] = _lay_wd(p["ffn1_down"][sl])
    m["wd2"] = _lay_wd(p["ffn2_down"][sl])
    w_in = p["w_in"][sl]
    tiles = _win_tiles()
    m["winF"] = np.ascontiguousarray(
        np.stack([_lay_cols(w_in, t).reshape(L, 128, KC * 128) for t in tiles], axis=1))
    m["winTm"] = _lay_cols(w_in, list(range(512, 768))).reshape(L, 128, KC * 256)
    m["winTd"] = _lay_cols(w_in, list(range(1280, 1536))).reshape(L, 128, KC * 256)
    ncols = list(range(2432, 2560)) + list(range(2688, 2816)) + list(range(2816, 2840)) + [-1] * 40
    m["winTn"] = _lay_cols(w_in, ncols).reshape(L, 128, KC * 320)
    wo = p["w_out"][sl].reshape(L, 4, 2, 128, 1024).transpose(0, 1, 3, 2, 4)
    m["woutp"] = np.ascontiguousarray(wo).reshape(L, 4, 128, 2048)
    w1 = p["nsa_cmp_w1"][sl].reshape(L, 2, 8, 4, 64, 256).transpose(0, 1, 2, 4, 3, 5)
    m["w1p"] = np.ascontiguousarray(w1).reshape(L, 2, 8, 64, 1024)
    w2 = p["nsa_cmp_w2"][sl]
    w2k = w2[:, 0].reshape(L, 2, 128, 64).transpose(0, 2, 1, 3)
    w2k = np.concatenate([w2k, w2k], axis=3)
    m["w2k"] = np.ascontiguousarray(w2k).reshape(L, 128, 256)
    w2v = w2[:, 1].reshape(L, 2, 128, 64).transpose(0, 2, 1, 3)
    m["w2v"] = np.ascontiguousarray(w2v).reshape(L, 128, 128)
    pe = p["nsa_cmp_pe"][sl]
    m["peT"] = np.ascontiguousarray(pe.transpose(0, 3, 1, 2)).reshape(L, 64, 64)
    dl = p["diff_lambda"][sl].reshape(L, 1, 128)
    m["dlrep"] = np.ascontiguousarray(np.broadcast_to(dl, (L, 128, 128)))
    sub = p["diff_subln"][sl].reshape(L, 1, 64)
    m["subrep"] = np.ascontiguousarray(np.broadcast_to(sub, (L, 128, 64)))
    m["tblrep"] = np.ascontiguousarray(np.broadcast_to(p["rel_bias"].reshape(1, 512), (128, 512)))
    for k_, v_ in c.items():
        if k_.startswith("c_"):
            m[k_] = v_
    return {k_: np.ascontiguousarray(v_, dtype=np.float32) for k_, v_ in m.items()}


_PROG_CACHE = {}


def _get_prog(n_layers, n_seq, do_final, parts, branches, layer0):
    key = (n_layers, n_seq, do_final, tuple(parts), tuple(branches), layer0)
    if key not in _PROG_CACHE:
        b = Builder(n_layers, n_seq, do_final, parts, branches, layer0)
        _PROG_CACHE[key] = b.build(_consts())
    return _PROG_CACHE[key]


def run_layers(x, p, n_layers, n_seq_per_core, do_final=True, parts=("ffn1", "mix", "ffn2"),
               branches=("moba", "diff", "nsa"), layer0=0, core_ids=None):
    L = n_layers
    nc, cnt = _get_prog(L, n_seq_per_core, do_final, parts, branches, layer0)
    B = x.shape[0]
    ncores = B // n_seq_per_core
    xT = np.ascontiguousarray(np.transpose(x, (0, 2, 1)))
    shared = prep_inputs(p, L, layer0)
    in_maps = []
    for c in range(ncores):
        m = dict(shared)
        m["xT"] = xT[c * n_seq_per_core:(c + 1) * n_seq_per_core]
        in_maps.append(m)
    res = run_bass_kernel_spmd(nc, in_maps, core_ids=list(range(ncores)) if core_ids is None else core_ids)
    o = np.concatenate([r["outT"] for r in res.results], axis=0)
    return np.ascontiguousarray(np.transpose(o, (0, 2, 1)))


def kernel(**inputs):
    x = np.asarray(inputs["x"], dtype=np.float32)
    p = {k: np.asarray(v, dtype=np.float32) for k, v in inputs.items() if k != "x"}
    return run_layers(x, p, DEPTH, x.shape[0] // NCORES)
```
